# Optimizing a Trainium2 kernel written in Bass

```python
import jax, jax.numpy as jnp
from jax import lax
import numpy as np

D_MODEL = 2048
BATCH = 2
SEQ = 16384
DEPTH = 4
DEC_BATCH = 8
DEC_SEQ = 32
PAST_LEN = 1024

CHUNK = 64
D_MIX = D_MODEL // 2
HEAD_DIM = 128
D_SB = D_MIX // 2
D_FOX = D_MIX - D_SB
H_SB = D_SB // HEAD_DIM
H_FOX = D_FOX // HEAD_DIM
PLE_DIM = 256
Q_BLOCK = 128
K_BLOCK = 128
SUPER_BLOCK = 1024
EPS = 1e-6
NEG_BIG = -1e30
FORGET_BIAS = 3.0
D_IN = 4 * D_SB + 4 * D_FOX + H_FOX

kernel_name = "hymba_stickbreak_fox_streaming_step"


def rms_norm(x, g):
    xf = x.astype(jnp.float32)
    var = jnp.mean(xf * xf, axis=-1, keepdims=True)
    return (xf * lax.rsqrt(var + EPS)).astype(x.dtype) * g


def head_rms_norm(o, g, n_heads):
    B, T, _ = o.shape
    of = o.reshape(B, T, n_heads, HEAD_DIM).astype(jnp.float32)
    of = of * lax.rsqrt(jnp.mean(of * of, axis=-1, keepdims=True) + EPS)
    return of.reshape(B, T, n_heads * HEAD_DIM).astype(o.dtype) * g


def map_query_blocks(fn, *qs):
    T = qs[0].shape[1]
    if T <= Q_BLOCK:
        return fn(*qs)
    nb = T // Q_BLOCK

    def to_blocks(a):
        a = a.reshape(a.shape[0], nb, Q_BLOCK, *a.shape[2:])
        return jnp.moveaxis(a, 1, 0)

    out = lax.map(lambda blk: fn(*blk), tuple(to_blocks(a) for a in qs))
    out = jnp.moveaxis(out, 0, 1)
    return out.reshape(out.shape[0], T, *out.shape[3:])


def causal_sweep(block_fn, q_arrays, P, T):
    outs = []
    for start in range(0, T, SUPER_BLOCK):
        end = min(start + SUPER_BLOCK, T)
        n_keys = -(-(P + end) // K_BLOCK) * K_BLOCK
        qs = [a[:, start:end] for a in q_arrays]
        outs.append(map_query_blocks(lambda *b, n=n_keys: block_fn(n, *b), *qs))
    return outs[0] if len(outs) == 1 else jnp.concatenate(outs, axis=1)


def stick_breaking_attention(q, k, v, q_pos, k_pos, P, T):
    scale = HEAD_DIM ** -0.5
    after_in_block = jnp.tril(jnp.ones((K_BLOCK, K_BLOCK), jnp.float32), -1)

    def block(n_keys, qb, pb):
        kb, vb, kp = k[:, :n_keys], v[:, :n_keys], k_pos[:n_keys]
        nk = n_keys // K_BLOCK
        z = jnp.einsum("bqhd,bkhd->bhqk", qb, kb).astype(jnp.float32) * scale
        allowed = kp[None, None, None, :] < pb[:, None, :, None]
        log_1m_beta = jnp.where(allowed, jax.nn.log_sigmoid(-z), 0.0)
        Bq, Hq, Qq, _ = z.shape
        lb = log_1m_beta.reshape(Bq, Hq, Qq, nk, K_BLOCK)
        inner = jnp.einsum("bhqnp,pj->bhqnj", lb, after_in_block, precision=lax.Precision.HIGHEST)
        after_blocks = jnp.tril(jnp.ones((nk, nk), jnp.float32), -1)
        cross = jnp.einsum("bhqm,mn->bhqn", lb.sum(-1), after_blocks, precision=lax.Precision.HIGHEST)
        suffix = (inner + cross[..., None]).reshape(Bq, Hq, Qq, n_keys)
        w = jnp.where(allowed, jnp.exp(jax.nn.log_sigmoid(z) + suffix), 0.0)
        return jnp.einsum("bhqk,bkhd->bqhd", w.astype(vb.dtype), vb)

    return causal_sweep(block, [q, q_pos], P, T)


def forgetting_attention(q, k, v, q_cum, k_cum, q_pos, k_pos, P, T):
    scale = HEAD_DIM ** -0.5
    k_cum_h = jnp.transpose(k_cum, (0, 2, 1))

    def block(n_keys, qb, cb, pb):
        kb, vb, kp = k[:, :n_keys], v[:, :n_keys], k_pos[:n_keys]
        s = jnp.einsum("bqhd,bkhd->bhqk", qb, kb).astype(jnp.float32) * scale
        bias = jnp.transpose(cb, (0, 2, 1))[..., None] - k_cum_h[:, :, None, :n_keys]
        allowed = kp[None, None, None, :] <= pb[:, None, :, None]
        w = jax.nn.softmax(jnp.where(allowed, s + bias, NEG_BIG), axis=-1)
        return jnp.einsum("bhqk,bkhd->bqhd", w.astype(vb.dtype), vb)

    return causal_sweep(block, [q, q_cum, q_pos], P, T)


def trunk_layer(x, p_i, w_in, b_f, g_attn, g_osb, g_ofox, w_out, w_ple, g_ple, w_ple_gate, past):
    B, T, _ = x.shape
    h = rms_norm(x, g_attn)
    proj = h @ w_in
    sizes = [D_SB] * 4 + [D_FOX] * 4
    idx, acc = [], 0
    for s_ in sizes:
        acc += s_
        idx.append(acc)
    q_sb, k_sb, v_sb, z_sb, q_fx, k_fx, v_fx, z_fx, f_fx = jnp.split(proj, idx, axis=-1)
    q_sb = q_sb.reshape(B, T, H_SB, HEAD_DIM)
    k_sb = k_sb.reshape(B, T, H_SB, HEAD_DIM)
    v_sb = v_sb.reshape(B, T, H_SB, HEAD_DIM)
    q_fx = q_fx.reshape(B, T, H_FOX, HEAD_DIM)
    k_fx = k_fx.reshape(B, T, H_FOX, HEAD_DIM)
    v_fx = v_fx.reshape(B, T, H_FOX, HEAD_DIM)
    logf = jax.nn.log_sigmoid((f_fx + b_f).astype(jnp.float32))

    if past is None:
        P = 0
        ks_all, vs_all, kf_all, vf_all, logf_all = k_sb, v_sb, k_fx, v_fx, logf
    else:
        c_ksb, c_vsb, c_kfx, c_vfx, c_logf = past
        P = c_ksb.shape[1]
        ks_all = jnp.concatenate([c_ksb, k_sb], axis=1)
        vs_all = jnp.concatenate([c_vsb, v_sb], axis=1)
        kf_all = jnp.concatenate([c_kfx, k_fx], axis=1)
        vf_all = jnp.concatenate([c_vfx, v_fx], axis=1)
        logf_all = jnp.concatenate([c_logf.astype(jnp.float32), logf], axis=1)
    cum_all = lax.cumsum(logf_all, axis=1)
    q_cum = cum_all[:, P:]
    L = P + T
    n_pad = -(-L // K_BLOCK) * K_BLOCK - L

    def pad_rows(a):
        if n_pad == 0:
            return a
        return jnp.pad(a, ((0, 0), (0, n_pad)) + ((0, 0),) * (a.ndim - 2))

    k_pos = jnp.arange(L + n_pad, dtype=jnp.int32)
    q_pos = (P + jnp.arange(T, dtype=jnp.int32))[None, :]

    o_sb = stick_breaking_attention(q_sb, pad_rows(ks_all), pad_rows(vs_all), q_pos, k_pos, P, T)
    o_fx = forgetting_attention(q_fx, pad_rows(kf_all), pad_rows(vf_all), q_cum, pad_rows(cum_all),
                                q_pos, k_pos, P, T)
    o_sb = head_rms_norm(o_sb.reshape(B, T, D_SB), g_osb, H_SB) * jax.nn.silu(z_sb)
    o_fx = head_rms_norm(o_fx.reshape(B, T, D_FOX), g_ofox, H_FOX) * jax.nn.silu(z_fx)
    x = x + jnp.concatenate([o_sb, o_fx], axis=-1) @ w_out
    x = x + (p_i @ w_ple) * jax.nn.sigmoid(rms_norm(x, g_ple) @ w_ple_gate)
    return x, (k_sb, v_sb, k_fx, v_fx, logf.astype(x.dtype))


def setup_inputs(seed: int = 0) -> dict:
    key = jax.random.key(seed)
    ks = jax.random.split(key, 20)
    f32 = jnp.float32
    nrm = lambda k, shp, s=1.0: jax.random.normal(k, shp, f32) * s
    return {
        "x_prompt": nrm(ks[0], (BATCH, SEQ, D_MODEL)),
        "x_sample": nrm(ks[1], (DEC_BATCH, DEC_SEQ, D_MODEL)),
        "p_prompt": nrm(ks[2], (DEPTH, BATCH, SEQ, PLE_DIM)),
        "p_sample": nrm(ks[3], (DEPTH, DEC_BATCH, DEC_SEQ, PLE_DIM)),
        "cache_sb_k": nrm(ks[4], (DEPTH, DEC_BATCH, PAST_LEN, H_SB, HEAD_DIM)),
        "cache_sb_v": nrm(ks[5], (DEPTH, DEC_BATCH, PAST_LEN, H_SB, HEAD_DIM)),
        "cache_fox_k": nrm(ks[6], (DEPTH, DEC_BATCH, PAST_LEN, H_FOX, HEAD_DIM)),
        "cache_fox_v": nrm(ks[7], (DEPTH, DEC_BATCH, PAST_LEN, H_FOX, HEAD_DIM)),
        "cache_fox_logf": jax.nn.log_sigmoid(FORGET_BIAS + nrm(ks[8], (DEPTH, DEC_BATCH, PAST_LEN, H_FOX))),
        "w_in": nrm(ks[9], (DEPTH, D_MODEL, D_IN), D_MODEL ** -0.5),
        "b_forget": FORGET_BIAS + nrm(ks[10], (DEPTH, H_FOX), 0.5),
        "g_attn_norm": 1.0 + nrm(ks[11], (DEPTH, D_MODEL), 0.02),
        "g_out_sb": 1.0 + nrm(ks[12], (DEPTH, D_SB), 0.02),
        "g_out_fox": 1.0 + nrm(ks[13], (DEPTH, D_FOX), 0.02),
        "w_out": nrm(ks[14], (DEPTH, D_MIX, D_MODEL), D_MIX ** -0.5),
        "w_ple": nrm(ks[15], (DEPTH, PLE_DIM, D_MODEL), PLE_DIM ** -0.5),
        "g_ple_norm": 1.0 + nrm(ks[16], (DEPTH, D_MODEL), 0.02),
        "w_ple_gate": nrm(ks[17], (DEPTH, D_MODEL, D_MODEL), D_MODEL ** -0.5),
        "g_final": 1.0 + nrm(ks[18], (D_MODEL,), 0.02),
    }


def reference(x_prompt, x_sample, p_prompt, p_sample, cache_sb_k, cache_sb_v, cache_fox_k,
              cache_fox_v, cache_fox_logf, w_in, b_forget, g_attn_norm, g_out_sb, g_out_fox,
              w_out, w_ple, g_ple_norm, w_ple_gate, g_final):
    xp, xs = x_prompt, x_sample
    new_p = [[] for _ in range(5)]
    new_s = [[] for _ in range(5)]
    for i in range(DEPTH):
        lw = (w_in[i], b_forget[i], g_attn_norm[i], g_out_sb[i], g_out_fox[i], w_out[i],
              w_ple[i], g_ple_norm[i], w_ple_gate[i])
        xp, st_p = trunk_layer(xp, p_prompt[i], *lw, None)
        past = (cache_sb_k[i], cache_sb_v[i], cache_fox_k[i], cache_fox_v[i], cache_fox_logf[i])
        xs, st_s = trunk_layer(xs, p_sample[i], *lw, past)
        for j in range(5):
            new_p[j].append(st_p[j])
            new_s[j].append(st_s[j])
    y_prompt = rms_norm(xp, g_final)
    y_sample = rms_norm(xs, g_final)
    sb_k_prompt, sb_v_prompt, fox_k_prompt, fox_v_prompt, fox_logf_prompt = [jnp.stack(a, axis=0) for a in new_p]
    sb_k_sample, sb_v_sample, fox_k_sample, fox_v_sample, fox_logf_sample = [jnp.stack(a, axis=0) for a in new_s]
    return (y_prompt, y_sample, sb_k_prompt, sb_v_prompt, fox_k_prompt, fox_v_prompt, fox_logf_prompt,
            sb_k_sample, sb_v_sample, fox_k_sample, fox_v_sample, fox_logf_sample)
```

```python
import contextlib
import numpy as np
import ml_dtypes
import concourse.bass as bass
import concourse.mybir as mybir
from concourse.bass_utils import run_bass_kernel_spmd

F32 = mybir.dt.float32
BF16 = mybir.dt.bfloat16
AF = mybir.ActivationFunctionType
ALU = mybir.AluOpType

D = 2048
KC = 16
HD = 128
NH = 8
PLE = 256
EPS = 1e-6
DEC_T = 32
PAST = 1024
QSCALE = HD ** -0.5


class Cfg:
    def __init__(self, NT=8, DEPTH=4, sample=True):
        self.NT = NT
        self.DEPTH = DEPTH
        self.sample = sample
        self.NTOK = NT * 512
        self.NTOT = self.NTOK + DEC_T
        self.NB = NT * 16
        self.SEQ = NT * 4 * 512


class Buf:
    __slots__ = ("name", "w", "r", "excl")

    def __init__(self, name, excl=False):
        self.name = name
        self.w = None
        self.r = {}
        self.excl = excl


ENGS = ("pe", "act", "dve", "pool", "sync")


def _flat(x):
    out = []
    for b in x:
        if isinstance(b, (list, tuple)):
            out.extend(_flat(b))
        else:
            out.append(b)
    return out


class Sched:
    def __init__(self, nc, sems, dsems):
        self.nc = nc
        self.sem = sems
        self.q = {e: [] for e in ENGS}
        self.cnt = {e: 0 for e in ENGS}
        self.waited = {e: {} for e in ENGS}
        self.dsem = dsems
        self.duse = {k: [0] * len(v) for k, v in dsems.items()}
        self.dnext = {k: 0 for k in dsems}
        self.n_ops = 0
        self.n_waits = 0

    def _semobj(self, key):
        if isinstance(key, tuple):
            return self.dsem[key[0]][key[1]]
        return self.sem[key]

    def _collect(self, reads, writes):
        need = {}

        def add(tok):
            if tok is None:
                return
            k, v = tok
            if need.get(k, 0) < v:
                need[k] = v

        for b in reads:
            add(b.w)
        for b in writes:
            add(b.w)
            for k, v in b.r.items():
                add((k, v))
        return need

    def _emit_waits(self, eng, need):
        wl = []
        wd = self.waited[eng]
        for k, v in need.items():
            if eng == "pe" and k == "pe":
                continue
            if wd.get(k, 0) >= v:
                continue
            wd[k] = v
            wl.append((self._semobj(k), v))
        return wl

    def _commit(self, tok, reads, writes):
        k, v = tok
        for b in reads:
            if b.r.get(k, 0) < v:
                b.r[k] = v
        for b in writes:
            b.w = tok
            b.r = {}

    def op(self, eng, fn, reads=(), writes=()):
        reads, writes = _flat(reads), _flat(writes)
        ex = [b for b in reads if b.excl]
        if ex:
            writes = list(writes) + ex
        need = self._collect(reads, writes)
        wl = self._emit_waits(eng, need)
        self.cnt[eng] += 1
        tok = (eng, self.cnt[eng])
        sem = self.sem[eng]
        self.n_ops += 1
        self.n_waits += len(wl)

        def run(e, wl=wl, fn=fn, sem=sem):
            for s, v in wl:
                e.wait_ge(s, v)
            fn(e).then_inc(sem, 1)

        self.q[eng].append(run)
        self._commit(tok, reads, writes)

    def dma(self, queue, out, in_, reads=(), writes=(), **kw):
        eng = "pool" if queue == "cast" else queue
        reads, writes = _flat(reads), _flat(writes)
        pool = self.dsem[queue]
        i = self.dnext[queue]
        self.dnext[queue] = (i + 1) % len(pool)
        self.duse[queue][i] += 1
        use = self.duse[queue][i]
        key = (queue, i)
        need = self._collect(reads, writes)
        if use > 1:
            if need.get(key, 0) < 16 * (use - 1):
                need[key] = 16 * (use - 1)
        wl = self._emit_waits(eng, need)
        sem = pool[i]
        self.n_ops += 1
        self.n_waits += len(wl)

        def run(e, wl=wl, sem=sem, out=out, in_=in_, kw=kw):
            for s, v in wl:
                e.wait_ge(s, v)
            e.dma_start(out=out, in_=in_, **kw).then_inc(sem, 16)

        self.q[eng].append(run)
        self._commit((key, 16 * use), reads, writes)

    def custom(self, eng, fn, semobj_key, val, reads=(), writes=()):
        reads, writes = _flat(reads), _flat(writes)
        need = self._collect(reads, writes)
        wl = self._emit_waits(eng, need)

        def run(e, wl=wl, fn=fn):
            for s, v in wl:
                e.wait_ge(s, v)
            fn(e)

        self.q[eng].append(run)
        self._commit((semobj_key, val), reads, writes)

    def final_waits(self, eng):
        wl = []
        for qn, pool in self.dsem.items():
            for i, s in enumerate(pool):
                if self.duse[qn][i] > 0:
                    wl.append((s, 16 * self.duse[qn][i]))

        def run(e, wl=wl):
            for s, v in wl:
                e.wait_ge(s, v)

        self.q[eng].append(run)


def build_program(cfg):
    NT, L, NTOK, NTOT, NB = cfg.NT, cfg.DEPTH, cfg.NTOK, cfg.NTOT, cfg.NB
    nc = bass.Bass("TRN2", target_bir_lowering=False)

    def din(name, shape, dt=F32):
        return nc.dram_tensor(name, list(shape), dt, kind="ExternalInput")

    def dout(name, shape, dt=F32):
        return nc.dram_tensor(name, list(shape), dt, kind="ExternalOutput")

    def dint(name, shape, dt):
        return nc.dram_tensor(name, list(shape), dt)

    xT = din("xT", [D, NTOT])
    pT = din("pT", [L, PLE, NTOT])
    w_in_r = din("w_in_r", [L * 32 * 128 * 2, 1024])
    w_f_r = din("w_f_r", [L * 128, 64])
    w_out_r = din("w_out_r", [L * 16 * 128, 1024])
    w_g_r = din("w_g_r", [L * 16 * 128 * 2, 1024])
    w_ple_r = din("w_ple_r", [L * 16 * 128, 256])
    NPAR = L * 48 + 16 + 8
    params = din("params", [128, NPAR])
    cbf = din("cbf", [128, 384], BF16)
    cf32 = din("cf32", [128, 6 * 128])
    masks = din("masks", [128, 2 * 16 * 512], BF16)
    smask = din("smask", [128, 2 * 32], BF16)
    if cfg.sample:
        ckT = din("ckT", [L, NH, HD, PAST])
        cv = din("cv", [L, PAST, NH * HD])
        clf = din("clf", [L, 8, 4, 128])

    yT = dout("yT", [D, NTOT])
    kT_out = dout("kT_out", [L, NH, HD, NTOT])
    v_out = dout("v_out", [L, NTOT, NH * HD])
    lf_out = dout("lf_out", [L, 4, NTOT])

    wb_in = dint("wb_in", [L * 32 * 128 * 2, 1024], BF16)
    wb_f = dint("wb_f", [L * 128, 64], BF16)
    wb_out = dint("wb_out", [L * 16 * 128, 1024], BF16)
    wb_g = dint("wb_g", [L * 16 * 128 * 2, 1024], BF16)
    wb_ple = dint("wb_ple", [L * 16 * 128, 256], BF16)
    xS = dint("xS", [D, NTOT], F32)
    qS = dint("qS", [NH, HD, NTOT], BF16)
    gS = dint("gS", [NH, HD, NTOT], BF16)
    ogS = dout("ogS", [NH, HD, NTOT], BF16) if getattr(cfg, "debug", False) else dint("ogS", [NH, HD, NTOT], BF16)
    skK = [[dint(f"skK{l}_{t}", [1024, 512], BF16) for t in range(NT)] for l in range(L)]
    skV = [[dint(f"skV{l}_{t}", [512, 1024], BF16) for t in range(NT)] for l in range(L)]
    gK = [[dint(f"gK{l}_{t}", [4 * 1024, 512], BF16) for t in range(NT)] for l in range(L)]
    gV = [[dint(f"gV{l}_{t}", [4 * 512, 1024], BF16) for t in range(NT)] for l in range(L)]
    slf = [dint(f"slf{l}", [4, NTOK], F32) for l in range(L)]
    glf = [dint(f"glf{l}", [16, NTOK], F32) for l in range(L)]
    sks = dint("sks", [NH, HD, DEC_T], BF16)
    svs = dint("svs", [DEC_T, NH * HD], BF16)
    slfs = dint("slfs", [4, DEC_T], F32)

    es = contextlib.ExitStack()
    with es:
        def sb(name, shape, dt):
            return es.enter_context(nc.sbuf_tensor(name, list(shape), dt))

        def ps(name):
            return es.enter_context(nc.psum_tensor(name, [128, 512], F32))

        sems = {e: es.enter_context(nc.semaphore("s_" + e)) for e in ("pe", "act", "dve", "pool")}
        dsems = {
            "sync": [es.enter_context(nc.semaphore(f"ds{i}")) for i in range(24)],
            "pool": [es.enter_context(nc.semaphore(f"dp{i}")) for i in range(12)],
            "cast": [es.enter_context(nc.semaphore(f"dc{i}")) for i in range(3)],
        }
        cc_sem = es.enter_context(nc.semaphore("cc"))
        sems["cc"] = cc_sem
        S = Sched(nc, sems, dsems)

        BIG = sb("BIG", [128, 32768], BF16)
        KT = BIG[:, 0:16384]
        VV = BIG[:, 16384:32768]
        xt = BIG[:, 0:16384].bitcast(F32).rearrange("p (k t) -> p k t", t=512)
        hT = BIG[:, 16384:24576].rearrange("p (k t) -> p k t", t=512)
        B_KT = [Buf(f"KT{g}") for g in range(8)]
        B_VVg = [Buf(f"VV{g}") for g in range(8)]
        B_VV0, B_VV1 = B_VVg[0:4], B_VVg[4:8]
        NWS = 6
        wring = [sb(f"w{i}", [128, 2048], BF16) for i in range(NWS)]
        b_wring = [Buf(f"w{i}") for i in range(NWS)]
        ogt = sb("ogt", [128, 8, 512], BF16)
        b_ogt = Buf("ogt")
        ptb = sb("ptb", [128, 2, 512], BF16)
        b_ptb = Buf("ptb")
        mask_t = sb("mask_t", [128, 2 * 16 * 512], BF16)
        smask_t = sb("smask_t", [128, 64], BF16)
        cb = sb("cb", [128, 384], BF16)
        cf = sb("cf", [128, 768], F32)
        par = sb("par", [128, NPAR], F32)
        b_const = Buf("const")
        ones_b, tri_b, omt_b = cb[:, 0:128], cb[:, 128:256], cb[:, 256:384]
        ident_f, ones_f, triinc_f, su_f, sel127_f, sel31_f = (cf[:, i * 128:(i + 1) * 128] for i in range(6))
        NST = 3
        stf = [sb(f"stf{i}", [128, 512], F32) for i in range(NST)]
        b_stf = [Buf(f"stf{i}") for i in range(NST)]
        stb = [sb(f"stb{i}", [128, 512], BF16) for i in range(NST)]
        b_stb = [Buf(f"stb{i}") for i in range(NST)]
        sqb = [sb(f"sqb{i}", [128, 512], BF16) for i in range(2)]
        b_sqb = [Buf(f"sqb{i}") for i in range(2)]
        rstd = sb("rstd", [128, 512], F32)
        b_rstd = Buf("rstd")
        tmpf = [sb(f"tmpf{i}", [128, 512], F32) for i in range(2)]
        b_tmpf = [Buf(f"tmpf{i}") for i in range(2)]
        qh = [sb(f"qh{i}", [128, 512], BF16) for i in range(2)]
        b_qh = [Buf(f"qh{i}") for i in range(2)]
        gh = [sb(f"gh{i}", [128, 512], BF16) for i in range(2)]
        b_gh = [Buf(f"gh{i}") for i in range(2)]
        qg_ctr = [0]

        def load_qg(hh, ti, c0, QW):
            i = qg_ctr[0] % 2
            qg_ctr[0] += 1
            S.dma("sync", qh[i][:, :QW], qS[hh, :, c0:c0 + QW], reads=[b_qS[hh][ti]], writes=[b_qh[i]])
            S.dma("sync", gh[i][:, :QW], gS[hh, :, c0:c0 + QW], reads=[b_gS[hh][ti]], writes=[b_gh[i]])
            return i
        NE = 3
        e_t = [sb(f"e{i}", [128, 512], F32) for i in range(NE)]
        b_e = [Buf(f"e{i}") for i in range(NE)]
        sp_t = [sb(f"sp{i}", [128, 512], BF16) for i in range(NE)]
        b_sp = [Buf(f"sp{i}") for i in range(NE)]
        t_t = [sb(f"t{i}", [128, 512], F32) for i in range(NE)]
        b_t = [Buf(f"t{i}") for i in range(NE)]
        NW = 3
        w_t = [sb(f"wt{i}", [128, 512], BF16) for i in range(NW)]
        b_w = [Buf(f"wt{i}") for i in range(NW)]
        ncb = sb("ncb", [128, 4, 128], F32)
        b_ncb = Buf("ncb")
        lnp = sb("lnp", [128, 4, 128], F32)
        b_lnp = Buf("lnp")
        ltb = sb("ltb", [128, 128], F32)
        b_ltb = Buf("ltb")
        tott = sb("tott", [128, 128], F32)
        b_tott = Buf("tott")
        xn = sb("xn", [128, 128], F32)
        b_xn = Buf("xn")
        ncref = sb("ncref", [128, 128], F32)
        b_ncref = Buf("ncref")
        bt = [sb(f"bt{i}", [128, 4, 128], F32) for i in range(2)]
        b_bt = [Buf(f"bt{i}") for i in range(2)]
        lfst = sb("lfst", [4, 512], F32)
        b_lfst = Buf("lfst")
        kts = sb("kts", [128, PAST + DEC_T], BF16)
        b_kts = Buf("kts")
        vs = sb("vs", [128, 9, 128], BF16)
        b_vs = Buf("vs")
        lnps = sb("lnps", [128, 4, 128], F32)
        b_lnps = Buf("lnps")
        ncbs = sb("ncbs", [128, 4, 16], F32)
        b_ncbs = Buf("ncbs")
        xs_ref = sb("xs_ref", [128, 16], F32)
        b_xsref = Buf("xs_ref")

        pbank = [ps(f"pb{i}") for i in range(8)]
        b_pb = [Buf(f"pb{i}", excl=True) for i in range(8)]

        b_xS = [Buf(f"xS{t}") for t in range(NT + 1)]
        b_qS = [[Buf(f"qS{h}_{t}") for t in range(NT + 1)] for h in range(NH)]
        b_gS = [[Buf(f"gS{h}_{t}") for t in range(NT + 1)] for h in range(NH)]
        b_ogS = [[Buf(f"ogS{h}_{t}") for t in range(NT + 1)] for h in range(NH)]
        b_skK = [[[] for _ in range(NT)] for _ in range(L)]
        b_skV = [[[] for _ in range(NT)] for _ in range(L)]
        b_gK = [[Buf(f"gK{l}_{t}") for t in range(NT)] for l in range(L)]
        b_gV = [[Buf(f"gV{l}_{t}") for t in range(NT)] for l in range(L)]
        b_slf = [[] for _ in range(L)]
        b_glf = [Buf(f"glf{l}") for l in range(L)]
        b_wb = {}
        b_sks, b_svs, b_slfs = [Buf(f"sks{h}") for h in range(NH)], [Buf(f"svs{h}") for h in range(NH)], Buf("slfs")

        def pcol(l, what, k=None):
            base = l * 48
            if what == "g_attn":
                return par[:, base + k: base + k + 1]
            if what == "g_ple":
                return par[:, base + 16 + k: base + 16 + k + 1]
            if what == "g_o":
                return par[:, base + 32 + k: base + 32 + k + 1]
            if what == "nbf":
                return par[0:4, base + 40: base + 41]
            raise KeyError(what)

        def gfin(k):
            return par[:, L * 48 + k: L * 48 + k + 1]

        onehot = par[:, L * 48 + 16: L * 48 + 20]

        S.dma("sync", cb[:], cbf[:, :], writes=[b_const])
        S.dma("sync", cf[:], cf32[:, :], writes=[b_const])
        S.dma("sync", par[:], params[:, :], writes=[b_const])
        S.dma("sync", mask_t[:], masks[:, :], writes=[b_const])
        S.dma("sync", smask_t[:], smask[:, :], writes=[b_const])

        def cast_rows(dst, src, r0, r1, key, step=2048):
            bl = []
            for a in range(r0, r1, step):
                b = min(a + step, r1)
                bb = Buf(f"wb_{key}_{a}")
                S.dma("cast", dst[a:b, :], src[a:b, :], writes=[bb])
                bl.append(bb)
            return bl

        def cast_layer(l, part):
            if part == 0:
                b_wb[("in", l)] = cast_rows(wb_in, w_in_r, l * 8192, l * 8192 + 4096, f"in{l}")
                b_wb[("f", l)] = cast_rows(wb_f, w_f_r, l * 128, (l + 1) * 128, f"f{l}")
            elif part == 1:
                b_wb[("in", l)] += cast_rows(wb_in, w_in_r, l * 8192 + 4096, (l + 1) * 8192, f"in{l}b")
            elif part == 2:
                b_wb[("out", l)] = cast_rows(wb_out, w_out_r, l * 2048, (l + 1) * 2048, f"out{l}")
                b_wb[("ple", l)] = cast_rows(wb_ple, w_ple_r, l * 2048, (l + 1) * 2048, f"ple{l}")
            elif part == 3:
                b_wb[("g", l)] = cast_rows(wb_g, w_g_r, l * 4096, (l + 1) * 4096, f"g{l}")

        for part in range(4):
            cast_layer(0, part)

        wctr = [0]

        def load_w(kind, l, c):
            i = wctr[0] % NWS
            wctr[0] += 1
            slot, bs = wring[i], b_wring[i]
            if kind == "in":
                r = (l * 32 + c) * 256
                S.dma("sync", slot[:, :], wb_in[r:r + 256, :].rearrange("(p t) x -> p (t x)", t=2),
                      reads=b_wb[("in", l)], writes=[bs])
                return slot[:, :].rearrange("p (k n) -> p k n", n=128), bs
            if kind == "g":
                r = (l * 16 + c) * 256
                S.dma("sync", slot[:, :], wb_g[r:r + 256, :].rearrange("(p t) x -> p (t x)", t=2),
                      reads=b_wb[("g", l)], writes=[bs])
                return slot[:, :].rearrange("p (k n) -> p k n", n=128), bs
            if kind == "out":
                r = (l * 16 + c) * 128
                S.dma("sync", slot[:, 0:1024], wb_out[r:r + 128, :], reads=b_wb[("out", l)], writes=[bs])
                return slot[:, 0:1024].rearrange("p (k n) -> p k n", n=128), bs
            if kind == "ple":
                r = (l * 16 + c) * 128
                S.dma("sync", slot[:, 0:256], wb_ple[r:r + 128, :], reads=b_wb[("ple", l)], writes=[bs])
                return slot[:, 0:256].rearrange("p (k n) -> p k n", n=128), bs
            if kind == "f":
                S.dma("sync", slot[:, 0:64], wb_f[l * 128:(l + 1) * 128, :], reads=b_wb[("f", l)], writes=[bs])
                return slot[:, 0:64].rearrange("p (k n) -> p k n", n=4), bs
            raise KeyError(kind)

        pa_ctr = [0]

        def pa_bank():
            i = pa_ctr[0] % 3
            pa_ctr[0] += 1
            return pbank[i], b_pb[i]

        st_ctr = [0]

        def stage():
            i = st_ctr[0] % NST
            st_ctr[0] += 1
            return stf[i], b_stf[i], stb[i], b_stb[i]

        def rms_stats(TW):
            acc, bacc = pbank[5], b_pb[5]
            for k in range(KC):
                i = k % 2
                S.op("act", lambda e, k=k, i=i: e.activation(out=sqb[i][:, :TW], in_=xt[:, k, :TW], func=AF.Square),
                     reads=[B_KT], writes=[b_sqb[i]])
                S.op("pe", lambda e, k=k, i=i: e.matmul(acc[:, :TW], lhsT=ones_b, rhs=sqb[i][:, :TW],
                                                        start=(k == 0), stop=(k == KC - 1)),
                     reads=[b_sqb[i], b_const], writes=[bacc])
            S.op("act", lambda e: e.activation(out=rstd[:, :TW], in_=acc[:, :TW], func=AF.Sqrt, bias=EPS_t[:, 0:1],
                                               scale=1.0 / D),
                 reads=[bacc, b_const], writes=[b_rstd])
            S.op("dve", lambda e: e.reciprocal(out=rstd[:, :TW], in_=rstd[:, :TW]), reads=[b_rstd], writes=[b_rstd])

        def make_h(TW, gname, l):
            for k in range(KC):
                g = pcol(l, gname, k) if gname != "fin" else gfin(k)
                S.op("dve", lambda e, k=k, g=g: e.scalar_tensor_tensor(
                    out=hT[:, k, :TW], in0=xt[:, k, :TW], scalar=g, in1=rstd[:, :TW], op0=ALU.mult, op1=ALU.mult),
                     reads=[B_KT, b_rstd, b_const], writes=[B_VV0])

        S.op("dve", lambda e: e.memset(xn[:, :], 0.0), writes=[b_xn])
        EPS_t = sb("eps_t", [128, 2], F32)
        S.op("dve", lambda e: e.memset(EPS_t[:, 0:1], EPS), writes=[b_const])
        S.op("dve", lambda e: e.memset(EPS_t[:, 1:2], 1.0), writes=[b_const])

        def phase_c(l, ti, TW, c0):
            S.dma("sync", ogt[:, :, :TW], ogS[:, :, c0:c0 + TW].rearrange("h p t -> p h t"),
                  reads=[b_ogS[h][ti] for h in range(NH)], writes=[b_ogt])
            S.dma("pool", ptb[:, :, :TW], pT[l, :, c0:c0 + TW].rearrange("(k p) t -> p k t", p=128),
                  writes=[b_ptb])
            for c in range(KC):
                wv, bw = load_w("out", l, c)
                pb, bpb = pa_bank()
                for k in range(8):
                    S.op("pe", lambda e, k=k, wv=wv, pb=pb: e.matmul(pb[:, :TW], lhsT=wv[:, k, :], rhs=ogt[:, k, :TW],
                                                                      start=(k == 0), stop=(k == 7)),
                         reads=[bw, b_ogt], writes=[bpb])
                S.op("dve", lambda e, c=c, pb=pb: e.tensor_tensor(out=xt[:, c, :TW], in0=pb[:, :TW], in1=xt[:, c, :TW],
                                                                  op=ALU.add),
                     reads=[bpb, B_KT], writes=[B_KT])
            rms_stats(TW)
            make_h(TW, "g_ple", l)
            for c in range(KC):
                wv, bw = load_w("g", l, c)
                pb, bpb = pa_bank()
                for k in range(KC):
                    S.op("pe", lambda e, k=k, wv=wv, pb=pb: e.matmul(pb[:, :TW], lhsT=wv[:, k, :], rhs=hT[:, k, :TW],
                                                                      start=(k == 0), stop=(k == KC - 1)),
                         reads=[bw, B_VV0], writes=[bpb])
                i = c % 2
                S.op("act", lambda e, pb=pb, i=i: e.activation(out=tmpf[i][:, :TW], in_=pb[:, :TW], func=AF.Sigmoid),
                     reads=[bpb], writes=[b_tmpf[i]])
                wv2, bw2 = load_w("ple", l, c)
                pb2, bpb2 = pa_bank()
                for k in range(2):
                    S.op("pe", lambda e, k=k, wv2=wv2, pb2=pb2: e.matmul(pb2[:, :TW], lhsT=wv2[:, k, :],
                                                                          rhs=ptb[:, k, :TW], start=(k == 0), stop=(k == 1)),
                         reads=[bw2, b_ptb], writes=[bpb2])
                S.op("dve", lambda e, pb2=pb2, i=i: e.tensor_tensor(out=tmpf[i][:, :TW], in0=pb2[:, :TW],
                                                                    in1=tmpf[i][:, :TW], op=ALU.mult),
                     reads=[bpb2, b_tmpf[i]], writes=[b_tmpf[i]])
                S.op("dve", lambda e, c=c, i=i: e.tensor_tensor(out=xt[:, c, :TW], in0=xt[:, c, :TW],
                                                                in1=tmpf[i][:, :TW], op=ALU.add),
                     reads=[b_tmpf[i], B_KT], writes=[B_KT])

        def phase_a(l, ti, TW, c0):
            is_s = ti == NT
            AP_ = getattr(cfg, "aparts", 255)
            rms_stats(TW)
            make_h(TW, "g_attn", l)
            if not AP_ & 1:
                return
            nsub = max(TW // 128, 1)
            SW = min(TW, 128)
            pdma = S.dma if AP_ & 8 else (lambda *a, **k: None)
            kinds_ok = getattr(cfg, 'kinds', 'qkvz')
            for c in range(32):
                grp, hh4 = c // 4, c % 4
                kind = ("q", "k", "v", "z")[grp % 4]
                if kind not in kinds_ok:
                    continue
                hh = hh4 + (4 if grp >= 4 else 0)
                wv, bw = load_w("in", l, c)
                pb, bpb = pa_bank()
                sf, bsf, sbb, bsb = stage()
                if kind != "v":
                    for k in range(KC):
                        S.op("pe", lambda e, k=k, wv=wv, pb=pb: e.matmul(pb[:, :TW], lhsT=wv[:, k, :], rhs=hT[:, k, :TW],
                                                                          start=(k == 0), stop=(k == KC - 1)),
                             reads=[bw, B_VV0], writes=[bpb])
                else:
                    for s in range(nsub):
                        for k in range(KC):
                            S.op("pe", lambda e, k=k, s=s, wv=wv, pb=pb: e.matmul(
                                pb[:SW, s * 128:(s + 1) * 128], lhsT=hT[:, k, s * 128:s * 128 + SW], rhs=wv[:, k, :],
                                start=(k == 0), stop=(k == KC - 1)),
                                 reads=[bw, B_VV0], writes=[bpb])
                if kind == "q":
                    S.op("act", lambda e, pb=pb, sbb=sbb: e.activation(out=sbb[:, :TW], in_=pb[:, :TW], func=AF.Identity,
                                                                         scale=QSCALE),
                         reads=[bpb], writes=[bsb])
                    pdma("pool", qS[hh, :, c0:c0 + TW], sbb[:, :TW], reads=[bsb], writes=[b_qS[hh][ti]])
                elif kind == "z":
                    S.op("act", lambda e, pb=pb, sbb=sbb: e.activation(out=sbb[:, :TW], in_=pb[:, :TW], func=AF.Silu),
                         reads=[bpb], writes=[bsb])
                    pdma("pool", gS[hh, :, c0:c0 + TW], sbb[:, :TW], reads=[bsb], writes=[b_gS[hh][ti]])
                elif kind == "k":
                    S.op("act", lambda e, pb=pb, sf=sf: e.activation(out=sf[:, :TW], in_=pb[:, :TW], func=AF.Identity),
                         reads=[bpb], writes=[bsf])
                    S.op("dve", lambda e, pb=pb, sbb=sbb: e.tensor_copy(out=sbb[:, :TW], in_=pb[:, :TW]),
                         reads=[bpb], writes=[bsb])
                    pdma("pool", kT_out[l, hh, :, c0:c0 + TW], sf[:, :TW], reads=[bsf])
                    if not is_s:
                        bb = Buf("skvp")
                        b_skK[l][ti].append(bb)
                        pdma("pool", skK[l][ti][hh * 128:(hh + 1) * 128, :], sbb[:, :TW], reads=[bsb], writes=[bb])
                    else:
                        pdma("pool", sks[hh, :, :], sbb[:, :TW], reads=[bsb], writes=[b_sks[hh]])
                else:
                    W4 = nsub * 128
                    S.op("act", lambda e, pb=pb, sf=sf: e.activation(out=sf[:SW, :W4], in_=pb[:SW, :W4], func=AF.Identity),
                         reads=[bpb], writes=[bsf])
                    S.op("dve", lambda e, pb=pb, sbb=sbb: e.tensor_copy(out=sbb[:SW, :W4], in_=pb[:SW, :W4]),
                         reads=[bpb], writes=[bsb])
                    pdma("pool", v_out[l, c0:c0 + TW, hh * 128:(hh + 1) * 128].rearrange("(s t) d -> t s d", t=SW),
                          sf[:SW, :W4].rearrange("t (s d) -> t s d", d=128), reads=[bsf])
                    if not is_s:
                        bb = Buf("skvp")
                        b_skV[l][ti].append(bb)
                        pdma("pool", skV[l][ti][:, hh * 128:(hh + 1) * 128].rearrange("(s t) d -> t s d", t=SW),
                              sbb[:SW, :W4].rearrange("t (s d) -> t s d", d=128), reads=[bsb], writes=[bb])
                    else:
                        pdma("pool", svs[:, hh * 128:(hh + 1) * 128], sbb[:SW, :128], reads=[bsb], writes=[b_svs[hh]])
            if not AP_ & 2:
                return
            wv, bw = load_w("f", l, 0)
            pb, bpb = pa_bank()
            for k in range(KC):
                S.op("pe", lambda e, k=k, wv=wv, pb=pb: e.matmul(pb[0:4, :TW], lhsT=wv[:, k, :], rhs=hT[:, k, :TW],
                                                                  start=(k == 0), stop=(k == KC - 1)),
                     reads=[bw, B_VV0], writes=[bpb])
            S.op("act", lambda e, pb=pb: e.activation(out=lfst[:, :TW], in_=pb[0:4, :TW], func=AF.Exp,
                                                      bias=pcol(l, "nbf"), scale=-1.0),
                 reads=[bpb, b_const], writes=[b_lfst])
            S.op("act", lambda e: e.activation(out=lfst[:, :TW], in_=lfst[:, :TW], func=AF.Ln, bias=EPS_t[0:4, 1:2]),
                 reads=[b_lfst, b_const], writes=[b_lfst])
            S.op("dve", lambda e: e.tensor_scalar(out=lfst[:, :TW], in0=lfst[:, :TW], scalar1=-1.0, scalar2=None,
                                                  op0=ALU.mult),
                 reads=[b_lfst], writes=[b_lfst])
            S.dma("pool", lf_out[l, :, c0:c0 + TW], lfst[:, :TW], reads=[b_lfst])
            if not is_s:
                bb = Buf("slfp")
                b_slf[l].append(bb)
                S.dma("pool", slf[l][:, c0:c0 + TW], lfst[:, :TW], reads=[b_lfst], writes=[bb])
            else:
                S.dma("pool", slfs[:, :], lfst[:, :TW], reads=[b_lfst], writes=[b_slfs])
            if not AP_ & 4:
                return
            if not is_s:
                all_gather(skK[l][ti], gK[l][ti], b_skK[l][ti], b_gK[l][ti])
                all_gather(skV[l][ti], gV[l][ti], b_skV[l][ti], b_gV[l][ti])
                if ti == NT - 1:
                    all_gather(slf[l], glf[l], b_slf[l], b_glf[l])

        def final_norm(ti, TW, c0):
            rms_stats(TW)
            for k in range(KC):
                i = k % 2
                S.op("dve", lambda e, k=k, i=i: e.scalar_tensor_tensor(
                    out=tmpf[i][:, :TW], in0=xt[:, k, :TW], scalar=gfin(k), in1=rstd[:, :TW], op0=ALU.mult, op1=ALU.mult),
                     reads=[B_KT, b_rstd, b_const], writes=[b_tmpf[i]])
                S.dma("pool", yT[k * 128:(k + 1) * 128, c0:c0 + TW], tmpf[i][:, :TW], reads=[b_tmpf[i]])

        def tile_pass(l):
            tiles = list(range(NT)) + ([NT] if cfg.sample else [])
            for ti in tiles:
                TW = 512 if ti < NT else DEC_T
                c0 = ti * 512
                src = xT if l == 0 else xS
                S.dma("sync", xt[:, :, :TW], src[:, c0:c0 + TW].rearrange("(k p) t -> p k t", p=128),
                      reads=([b_xS[ti]] if l > 0 else []), writes=[B_KT])
                if l > 0:
                    phase_c(l - 1, ti, TW, c0)
                if l < L:
                    S.dma("pool", xS[:, c0:c0 + TW].rearrange("(k p) t -> p k t", p=128), xt[:, :, :TW],
                          reads=[B_KT], writes=[b_xS[ti]])
                    phase_a(l, ti, TW, c0)
                else:
                    final_norm(ti, TW, c0)

        def all_gather(src, dst, reads, bdst):
            S.cc_n += 1
            n = S.cc_n
            S.custom("pool", lambda e, src=src, dst=dst: e.collective_compute(
                "AllGather", ALU.bypass, replica_groups=[[0, 1, 2, 3], [4, 5, 6, 7]],
                ins=[src.ap().opt()], outs=[dst.ap().opt()]).then_inc(cc_sem, 1),
                     "cc", n, reads=reads, writes=[bdst])

        S.cc_n = 0

        def cumsum_tables(l):
            for ip in range(NT):
                for jp in range(4):
                    S.dma("sync", lnp[16 * ip + 4 * jp:16 * ip + 4 * jp + 4, :, :],
                          glf[l][jp * 4:(jp + 1) * 4, 512 * ip:512 * ip + 512].rearrange("h (s p) -> s h p", p=128),
                          reads=[b_glf[l]], writes=[b_lnp])
            cum_core(lnp, b_lnp, NB, ncb, b_ncb, NB)
            for h in range(4):
                for jp in range(4):
                    src = ncb[:, h, :NB].rearrange("p (i m) -> p i m", m=16)[:, :, 4 * jp:4 * jp + 4]
                    dst = xn[:, h * 32:h * 32 + NT * 4].rearrange("p (i s) -> p i s", s=4)
                    if jp == 0:
                        S.op("dve", lambda e, src=src, dst=dst: e.tensor_scalar(
                            out=dst, in0=src, scalar1=onehot[:, 0:1], scalar2=None, op0=ALU.mult),
                             reads=[b_ncb, b_const], writes=[b_xn])
                    else:
                        S.op("dve", lambda e, src=src, dst=dst, jp=jp: e.scalar_tensor_tensor(
                            out=dst, in0=src, scalar=onehot[:, jp:jp + 1], in1=dst, op0=ALU.mult, op1=ALU.add),
                             reads=[b_ncb, b_const, b_xn], writes=[b_xn])
            pb, bpb = pbank[6], b_pb[6]
            S.op("pe", lambda e: e.matmul(pb[:, 0:128], lhsT=sel127_f, rhs=xn[:, :], start=True, stop=True),
                 reads=[b_xn, b_const], writes=[bpb])
            S.op("act", lambda e: e.activation(out=ncref[:, :], in_=pb[:, 0:128], func=AF.Identity), reads=[bpb],
                 writes=[b_ncref])

        def cum_core(Lsrc, bL, nb, dst, bdst, nbp):
            pb, bpb = pbank[6], b_pb[6]
            pb2, bpb2 = pbank[7], b_pb[7]
            for h in range(4):
                S.op("pe", lambda e, h=h: e.transpose(pb[:, 0:nb], Lsrc[:nb, h, :], ident_f[:nb, :nb]),
                     reads=[bL, b_const], writes=[bpb])
                S.op("act", lambda e: e.activation(out=ltb[:, :nb], in_=pb[:, 0:nb], func=AF.Identity), reads=[bpb],
                     writes=[b_ltb])
                S.op("pe", lambda e: e.matmul(pb2[:nb, 0:128], lhsT=ltb[:, :nb], rhs=ones_f, start=True, stop=True),
                     reads=[b_ltb, b_const], writes=[bpb2])
                S.op("act", lambda e: e.activation(out=tott[:nb, :], in_=pb2[:nb, 0:128], func=AF.Identity), reads=[bpb2],
                     writes=[b_tott])
                S.op("pe", lambda e: e.matmul(pb[:, 128:128 + nb], lhsT=triinc_f, rhs=ltb[:, :nb], start=True, stop=False),
                     reads=[b_ltb, b_const], writes=[bpb])
                S.op("pe", lambda e: e.matmul(pb[:, 128:128 + nb], lhsT=tott[:nb, :], rhs=su_f[:nb, :nb], start=False,
                                              stop=True),
                     reads=[b_tott, b_const], writes=[bpb])
                S.op("act", lambda e, h=h: e.activation(out=dst[:, h, :nb], in_=pb[:, 128:128 + nb], func=AF.Identity,
                                                        scale=-1.0),
                     reads=[bpb], writes=[bdst])

        ectr = [0]
        wctr2 = [0]

        acc_t = [sb(f"acc{i}", [128, 512], BF16) for i in range(2)]
        b_acc = [Buf(f"acc{i}") for i in range(2)]
        rc = {"z": 0, "e": 0, "t": 0, "w": 0, "p": 0, "a": 0}

        def attn_tile(hh, QW, q_ap, bq, steps, bias_of, kind, ob=0):
            Zb = (0, 1, 2)
            Pb = (6, 7)
            O, bO = pbank[3 + ob], b_pb[3 + ob]
            DEN, bDEN = pbank[6 + ob], b_pb[6 + ob]
            ns = len(steps)
            R = [dict() for _ in range(ns)]

            def qk(si):
                kt_ap, v_ap, KP, m_ap, rds = steps[si]
                zi = Zb[rc["z"] % 3]
                rc["z"] += 1
                R[si]["z"] = (pbank[zi], b_pb[zi])
                z = pbank[zi]
                S.op("pe", lambda e, z=z, kt_ap=kt_ap, KP=KP: e.matmul(z[:KP, :QW], lhsT=kt_ap, rhs=q_ap, start=True,
                                                                        stop=True),
                     reads=rds + [bq], writes=[b_pb[zi]])

            def pv(si):
                kt_ap, v_ap, KP, m_ap, rds = steps[si]
                wt, bw = R[si]["w"]
                first, last = si == 0, si == ns - 1
                S.op("pe", lambda e, wt=wt, v_ap=v_ap, KP=KP, first=first, last=last: e.matmul(
                    O[:, :QW], lhsT=v_ap, rhs=wt[:KP, :QW], start=first, stop=last),
                     reads=rds + [bw], writes=[bO])
                if kind == "fox":
                    S.op("pe", lambda e, wt=wt, KP=KP, first=first, last=last: e.matmul(
                        DEN[:, :QW], lhsT=ones_b[:KP, :], rhs=wt[:KP, :QW], start=first, stop=last),
                         reads=[bw, b_const], writes=[bDEN])

            def take_w(si):
                wi = rc["w"] % NW
                rc["w"] += 1
                R[si]["w"] = (w_t[wi], b_w[wi])
                return w_t[wi], b_w[wi]

            def fox_x(si):
                kt_ap, v_ap, KP, m_ap, rds = steps[si]
                z, bz = R[si]["z"]
                wt, bw = take_w(si)
                sw_ = min(QW, 128)
                for sq in range(max(QW // 128, 1)):
                    bcol, bbt = bias_of(si, sq)
                    S.op("act", lambda e, z=z, wt=wt, KP=KP, bcol=bcol, sq=sq, sw_=sw_: e.activation(
                        out=wt[:KP, sq * sw_:(sq + 1) * sw_], in_=z[:KP, sq * sw_:(sq + 1) * sw_], func=AF.Exp,
                        bias=bcol),
                         reads=[bz, bbt], writes=[bw])
                if m_ap is not None:
                    S.op("dve", lambda e, wt=wt, m_ap=m_ap, KP=KP: e.tensor_tensor(
                        out=wt[:KP, :QW], in0=wt[:KP, :QW], in1=m_ap, op=ALU.mult),
                         reads=[bw, b_const], writes=[bw])

            if kind == "fox":
                qk(0)
                for si in range(ns):
                    if si + 1 < ns:
                        qk(si + 1)
                    fox_x(si)
                    if si >= 1:
                        pv(si - 1)
                pv(ns - 1)
                return

            def sb_el(si):
                kt_ap, v_ap, KP, m_ap, rds = steps[si]
                z, bz = R[si]["z"]
                ei = rc["e"] % NE
                rc["e"] += 1
                et, be, spt, bsp = e_t[ei], b_e[ei], sp_t[ei], b_sp[ei]
                R[si]["e"] = (et, be)
                R[si]["sp"] = (spt, bsp)
                S.op("act", lambda e, z=z, et=et, KP=KP: e.activation(out=et[:KP, :QW], in_=z[:KP, :QW], func=AF.Exp),
                     reads=[bz], writes=[be])
                if m_ap is not None:
                    S.op("dve", lambda e, et=et, m_ap=m_ap, KP=KP: e.tensor_tensor(
                        out=et[:KP, :QW], in0=et[:KP, :QW], in1=m_ap, op=ALU.mult),
                         reads=[be, b_const], writes=[be])
                S.op("act", lambda e, et=et, spt=spt, KP=KP: e.activation(out=spt[:KP, :QW], in_=et[:KP, :QW],
                                                                          func=AF.Ln, bias=EPS_t[:KP, 1:2]),
                     reads=[be, b_const], writes=[bsp])

            def sb_p(si):
                kt_ap, v_ap, KP, m_ap, rds = steps[si]
                spt, bsp = R[si]["sp"]
                pi = Pb[rc["p"] % 2]
                rc["p"] += 1
                P, bP = pbank[pi], b_pb[pi]
                R[si]["p"] = (P, bP)
                S.op("pe", lambda e, spt=spt, KP=KP, P=P, si=si: e.matmul(
                    P[:KP, :QW], lhsT=tri_b[:KP, :KP], rhs=spt[:KP, :QW], start=True, stop=(si == 0)),
                     reads=[bsp, b_const], writes=[bP])
                if si > 0:
                    at, ba = R[si]["acc"]
                    S.op("pe", lambda e, at=at, KP=KP, P=P: e.matmul(
                        P[:KP, :QW], lhsT=ones_b[:, :KP], rhs=at[:, :QW], start=False, stop=True),
                         reads=[ba, b_const], writes=[bP])

            def sb_acc(si):
                kt_ap, v_ap, KP, m_ap, rds = steps[si]
                spt, bsp = R[si]["sp"]
                ai = rc["a"] % 2
                rc["a"] += 1
                an, ban = acc_t[ai], b_acc[ai]
                R[si + 1]["acc"] = (an, ban)
                if si == 0:
                    if KP < 128:
                        S.op("pool", lambda e, an=an: e.memset(an[:, :QW], 0.0), writes=[ban])
                    S.op("pool", lambda e, an=an, spt=spt, KP=KP: e.tensor_copy(out=an[:KP, :QW], in_=spt[:KP, :QW]),
                         reads=[bsp], writes=[ban])
                else:
                    ac, bac = R[si]["acc"]
                    if KP < 128:
                        raise NotImplementedError
                    S.op("pool", lambda e, an=an, ac=ac, spt=spt: e.tensor_tensor(
                        out=an[:, :QW], in0=ac[:, :QW], in1=spt[:, :QW], op=ALU.add),
                         reads=[bac, bsp], writes=[ban])

            def sb_tw(si):
                kt_ap, v_ap, KP, m_ap, rds = steps[si]
                et, be = R[si]["e"]
                P, bP = R[si]["p"]
                ti_ = rc["t"] % NE
                rc["t"] += 1
                tt, btt = t_t[ti_], b_t[ti_]
                wt, bw = take_w(si)
                S.op("act", lambda e, tt=tt, KP=KP, P=P: e.activation(out=tt[:KP, :QW], in_=P[:KP, :QW], func=AF.Exp,
                                                                     scale=-1.0),
                     reads=[bP], writes=[btt])
                S.op("dve", lambda e, wt=wt, et=et, tt=tt, KP=KP: e.tensor_tensor(
                    out=wt[:KP, :QW], in0=et[:KP, :QW], in1=tt[:KP, :QW], op=ALU.mult),
                     reads=[be, btt], writes=[bw])

            qk(0)
            sb_el(0)
            for si in range(ns):
                if si + 1 < ns:
                    qk(si + 1)
                    sb_el(si + 1)
                sb_p(si)
                if si + 1 < ns:
                    sb_acc(si)
                sb_tw(si)
                if si >= 1:
                    pv(si - 1)
            pv(ns - 1)

        def epilogue(l, hh, QW, c0, ti, gate_ap, bgate, kind, ob=0):
            O, bO = pbank[3 + ob], b_pb[3 + ob]
            DEN, bDEN = pbank[6 + ob], b_pb[6 + ob]
            SS, bSS = pbank[5], b_pb[5]
            i = ectr[0] % 2
            if kind == "fox":
                S.op("dve", lambda e: e.reciprocal(out=tmpf[0][:, :QW], in_=DEN[:, :QW]), reads=[bDEN], writes=[b_tmpf[0]])
                S.op("dve", lambda e: e.tensor_tensor(out=tmpf[1][:, :QW], in0=O[:, :QW], in1=tmpf[0][:, :QW], op=ALU.mult),
                     reads=[bO, b_tmpf[0]], writes=[b_tmpf[1]])
                u_ap, bu = tmpf[1], b_tmpf[1]
            else:
                u_ap, bu = O, bO
            S.op("act", lambda e: e.activation(out=sqb[i][:, :QW], in_=u_ap[:, :QW], func=AF.Square), reads=[bu],
                 writes=[b_sqb[i]])
            S.op("pe", lambda e: e.matmul(SS[:, :QW], lhsT=ones_b, rhs=sqb[i][:, :QW], start=True, stop=True),
                 reads=[b_sqb[i], b_const], writes=[bSS])
            S.op("act", lambda e: e.activation(out=tmpf[0][:, :QW], in_=SS[:, :QW], func=AF.Sqrt, bias=EPS_t[:, 0:1],
                                               scale=1.0 / 128),
                 reads=[bSS, b_const], writes=[b_tmpf[0]])
            S.op("dve", lambda e: e.reciprocal(out=tmpf[0][:, :QW], in_=tmpf[0][:, :QW]), reads=[b_tmpf[0]],
                 writes=[b_tmpf[0]])
            S.op("dve", lambda e: e.scalar_tensor_tensor(out=tmpf[1][:, :QW], in0=u_ap[:, :QW], scalar=pcol(l, "g_o", hh),
                                                         in1=tmpf[0][:, :QW], op0=ALU.mult, op1=ALU.mult),
                 reads=[bu, b_tmpf[0], b_const], writes=[b_tmpf[1]])
            sf, bsf, sbb, bsb = stage()
            S.op("dve", lambda e, sbb=sbb: e.tensor_tensor(out=sbb[:, :QW], in0=tmpf[1][:, :QW], in1=gate_ap, op=ALU.mult),
                 reads=[b_tmpf[1], bgate], writes=[bsb])
            S.dma("pool", ogS[hh, :, c0:c0 + QW], sbb[:, :QW], reads=[bsb], writes=[b_ogS[hh][ti]])

        ob_ctr = [0]

        def next_ob():
            i = ob_ctr[0] % 2
            ob_ctr[0] += 1
            return i

        def phase_b(l):
            cumsum_tables(l)
            if cfg.sample:
                sample_tables(l)
            if getattr(cfg, "stop", 99) < 4:
                return
            for hh in range(NH if getattr(cfg, "stop", 99) >= 5 else 1):
                kind = "sb" if hh < 4 else "fox"
                if l + 1 < L and hh < 4:
                    cast_layer(l + 1, hh)
                qi = hh % 2
                def load_kv(hx, ip):
                    for jp in range(4):
                        n0 = 16 * ip + 4 * jp
                        S.dma("sync", KT[:, n0 * 128:(n0 + 4) * 128],
                              gK[l][ip][jp * 1024 + hx * 128: jp * 1024 + hx * 128 + 128, :],
                              reads=[b_gK[l][ip]], writes=[B_KT[ip]])
                        S.dma("sync", VV[:, n0 * 128:(n0 + 4) * 128].rearrange("p (s d) -> p s d", d=128),
                              gV[l][ip][jp * 512:(jp + 1) * 512, hx * 128:(hx + 1) * 128].rearrange("(s p) d -> p s d", p=128),
                              reads=[b_gV[l][ip]], writes=[B_VVg[ip]])

                for ip in (range(NT) if hh == 0 else range(1)):
                    load_kv(hh, ip)
                for ti in range(NT - 1, -1, -1):
                    nkb = 16 * ti + 16
                    order = list(range(nkb - 1, -1, -1)) if kind == "sb" else list(range(nkb))
                    steps = []
                    for n in order:
                        m = n - 16 * ti
                        m_ap = None
                        if m >= 0:
                            off = ((0 if kind == "sb" else 1) * 16 + m) * 512
                            m_ap = mask_t[:, off:off + 512]
                        steps.append((KT[:, n * 128:(n + 1) * 128], VV[:, n * 128:(n + 1) * 128], 128, m_ap,
                                      [B_KT[n // 16], B_VVg[n // 16]]))
                    bias_of = None
                    if kind == "fox":
                        h = hh - 4
                        bi = ti % 2
                        for sq in range(4):
                            cix = h * 32 + ti * 4 + sq
                            S.op("dve", lambda e, h=h, bi=bi, sq=sq, cix=cix, nkb=nkb: e.tensor_scalar(
                                out=bt[bi][:, sq, :nkb], in0=ncb[:, h, :nkb], scalar1=ncref[:, cix:cix + 1],
                                scalar2=0.0, op0=ALU.subtract, op1=ALU.min),
                                 reads=[b_ncb, b_ncref], writes=[b_bt[bi]])
                        bias_of = lambda si, sq, bi=bi, order=order: (bt[bi][:, sq, order[si]:order[si] + 1], b_bt[bi])
                    qi = load_qg(hh, ti, ti * 512, 512)
                    ob = next_ob()
                    attn_tile(hh, 512, qh[qi][:, :], b_qh[qi], steps, bias_of, kind, ob)
                    epilogue(l, hh, 512, ti * 512, ti, gh[qi][:, :], b_gh[qi], kind, ob)
                    if hh + 1 < NH and ti + 1 <= NT - 1:
                        load_kv(hh + 1, ti + 1)
                if cfg.sample:
                    sample_attn(l, hh, kind, qi)

        def sample_tables(l):
            S.op("dve", lambda e: e.memset(lnps[:, :, :], 0.0), writes=[b_lnps])
            S.dma("sync", lnps[0:8, :, :], clf[l, :, :, :], reads=[], writes=[b_lnps])
            S.dma("sync", lnps[8:9, :, 0:DEC_T], slfs[:, :].rearrange("(o h) p -> o h p", o=1), reads=[b_slfs],
                  writes=[b_lnps])
            cum_core(lnps, b_lnps, 9, ncbs, b_ncbs, 9)

        def sample_attn(l, hh, kind, qi):
            c0 = NTOK
            QW = DEC_T
            S.dma("pool", kts[:, 0:PAST], ckT[l, hh, :, :], writes=[b_kts])
            S.dma("sync", kts[:, PAST:PAST + DEC_T], sks[hh, :, :], reads=[b_sks[hh]], writes=[b_kts])
            S.dma("pool", vs[:, 0:8, :], cv[l, :, hh * 128:(hh + 1) * 128].rearrange("(n p) d -> p n d", p=128),
                  writes=[b_vs])
            S.dma("sync", vs[0:DEC_T, 8, :], svs[:, hh * 128:(hh + 1) * 128], reads=[b_svs[hh]], writes=[b_vs])
            qi = load_qg(hh, NT, c0, QW)
            q_ap = qh[qi][:, :QW]
            mk = smask_t[0:DEC_T, (0 if kind == "sb" else 32):(0 if kind == "sb" else 32) + 32]
            blocks = [(kts[:, PAST:PAST + DEC_T], vs[0:DEC_T, 8, :], DEC_T, mk, [b_kts, b_vs])]
            for n in range(7, -1, -1):
                blocks.append((kts[:, n * 128:(n + 1) * 128], vs[:, n, :], 128, None, [b_kts, b_vs]))
            bias_of = None
            if kind == "fox":
                h = hh - 4
                pb, bpb = pbank[6], b_pb[6]
                S.op("pe", lambda e, h=h: e.matmul(pb[:, 0:16], lhsT=sel31_f, rhs=ncbs[:, h, :], start=True, stop=True),
                     reads=[b_ncbs, b_const], writes=[bpb])
                S.op("act", lambda e: e.activation(out=xs_ref[:, 0:16], in_=pb[:, 0:16], func=AF.Identity), reads=[bpb],
                     writes=[b_xsref])
                S.op("dve", lambda e, h=h: e.tensor_scalar(out=bt[0][:, 0, 0:9], in0=ncbs[:, h, 0:9], scalar1=xs_ref[:, 8:9],
                                                           scalar2=0.0, op0=ALU.subtract, op1=ALU.min),
                     reads=[b_ncbs, b_xsref], writes=[b_bt[0]])
                nidx = [8] + list(range(7, -1, -1))
                bias_of = lambda si, sq: (bt[0][:(DEC_T if si == 0 else 128), 0, nidx[si]:nidx[si] + 1], b_bt[0])
            ob = next_ob()
            attn_tile(hh, QW, q_ap, b_qh[qi], blocks, bias_of, kind, ob)
            epilogue(l, hh, QW, c0, NT, gh[qi][:, :QW], b_gh[qi], kind, ob)

        stop = getattr(cfg, "stop", 99)
        if stop >= 2:
            for l in range(L):
                tile_pass(l)
                if stop >= 3:
                    phase_b(l)
            if stop >= 6:
                tile_pass(L)
        S.final_waits("sync")

        with nc.Block() as block:
            @block.tensor
            def _(e):
                for f in S.q["pe"]:
                    f(e)

            @block.scalar
            def _(e):
                for f in S.q["act"]:
                    f(e)

            @block.vector
            def _(e):
                for f in S.q["dve"]:
                    f(e)

            @block.gpsimd
            def _(e):
                for f in S.q["pool"]:
                    f(e)

            @block.sync
            def _(e):
                for f in S.q["sync"]:
                    f(e)
    return nc, S


def _consts():
    k = np.arange(128)
    ones = np.ones((128, 128), np.float32)
    tri = (k[:, None] >= k[None, :]).astype(np.float32)
    omt = 1.0 - tri
    cbf = np.concatenate([ones, tri, omt], axis=1).astype(ml_dtypes.bfloat16)
    ident = np.eye(128, dtype=np.float32)
    triinc = (k[:, None] <= k[None, :]).astype(np.float32)
    su = (k[:, None] < k[None, :]).astype(np.float32)
    sel = np.zeros((128, 128), np.float32)
    sel[127, :] = 1.0
    sel31 = np.zeros((128, 128), np.float32)
    sel31[31, :] = 1.0
    cf32 = np.concatenate([ident, ones, triinc, su, sel, sel31], axis=1).astype(np.float32)
    return cbf, cf32


def _masks(j):
    k = np.arange(128)
    out = np.zeros((128, 2, 16, 512), np.float32)
    tri_sb = (k[:, None] < k[None, :]).astype(np.float32)
    tri_fx = (k[:, None] <= k[None, :]).astype(np.float32)
    for kind, tri in enumerate((tri_sb, tri_fx)):
        for m in range(16):
            d = m - 4 * j
            for s in range(4):
                if d < s:
                    out[:, kind, m, s * 128:(s + 1) * 128] = 1.0
                elif d == s:
                    out[:, kind, m, s * 128:(s + 1) * 128] = tri
    return out.reshape(128, 2 * 16 * 512).astype(ml_dtypes.bfloat16)


def _smask():
    k = np.arange(32)
    out = np.zeros((128, 64), np.float32)
    out[:32, 0:32] = (k[:, None] < k[None, :])
    out[:32, 32:64] = (k[:, None] <= k[None, :])
    return out.astype(ml_dtypes.bfloat16)


_PROG_CACHE = {}


def run(cfg, inputs):
    NT, L, NTOK, NTOT = cfg.NT, cfg.DEPTH, cfg.NTOK, cfg.NTOT
    f32 = np.float32
    g = {k: np.asarray(v) for k, v in inputs.items()}
    key = (NT, L, cfg.sample)
    if key not in _PROG_CACHE:
        _PROG_CACHE[key] = build_program(cfg)
    nc, S = _PROG_CACHE[key]

    w_in = g["w_in"].astype(f32, copy=False)
    w_in_r = w_in[:, :, :4096].reshape(L, KC, 128, 32, 128).transpose(0, 3, 2, 1, 4).reshape(L * 32 * 128 * 2, 1024)
    w_f_r = w_in[:, :, 4096:4100].reshape(L, KC, 128, 4).transpose(0, 2, 1, 3).reshape(L * 128, 64)
    w_out_r = g["w_out"].reshape(L, 8, 128, 16, 128).transpose(0, 3, 2, 1, 4).reshape(L * 16 * 128, 1024)
    w_g_r = g["w_ple_gate"].reshape(L, KC, 128, 16, 128).transpose(0, 3, 2, 1, 4).reshape(L * 16 * 128 * 2, 1024)
    w_ple_r = g["w_ple"].reshape(L, 2, 128, 16, 128).transpose(0, 3, 2, 1, 4).reshape(L * 16 * 128, 256)
    w_in_r, w_f_r, w_out_r, w_g_r, w_ple_r = (np.ascontiguousarray(a, dtype=f32) for a in
                                              (w_in_r, w_f_r, w_out_r, w_g_r, w_ple_r))
    cbf, cf32 = _consts()
    smask = _smask()
    NPAR = L * 48 + 16 + 8
    in_maps = []
    for c in range(8):
        b, j = c // 4, c % 4
        tiles = [4 * i + j for i in range(NT)]
        xp = g["x_prompt"][b].reshape(4 * NT, 512, D)[tiles].reshape(NTOK, D)
        xs = g["x_sample"][c]
        xTc = np.ascontiguousarray(np.concatenate([xp, xs], axis=0).T, dtype=f32)
        pp = g["p_prompt"][:, b].reshape(L, 4 * NT, 512, PLE)[:, tiles].reshape(L, NTOK, PLE)
        pTc = np.ascontiguousarray(np.concatenate([pp, g["p_sample"][:, c]], axis=1).transpose(0, 2, 1), dtype=f32)
        par = np.zeros((128, NPAR), f32)
        for l in range(L):
            par[:, l * 48: l * 48 + 16] = g["g_attn_norm"][l].reshape(KC, 128).T
            par[:, l * 48 + 16: l * 48 + 32] = g["g_ple_norm"][l].reshape(KC, 128).T
            par[:, l * 48 + 32: l * 48 + 36] = g["g_out_sb"][l].reshape(4, 128).T
            par[:, l * 48 + 36: l * 48 + 40] = g["g_out_fox"][l].reshape(4, 128).T
            par[0:4, l * 48 + 40] = -g["b_forget"][l]
        par[:, L * 48: L * 48 + 16] = g["g_final"].reshape(KC, 128).T
        par[:, L * 48 + 16 + j] = 1.0
        m = {"xT": xTc, "pT": pTc, "w_in_r": w_in_r, "w_f_r": w_f_r, "w_out_r": w_out_r, "w_g_r": w_g_r,
             "w_ple_r": w_ple_r, "params": par, "cbf": cbf, "cf32": cf32, "masks": _masks(j), "smask": smask}
        if cfg.sample:
            ck = np.concatenate([g["cache_sb_k"][:, c], g["cache_fox_k"][:, c]], axis=2)
            m["ckT"] = np.ascontiguousarray(ck.transpose(0, 2, 3, 1), dtype=f32)
            cvv = np.concatenate([g["cache_sb_v"][:, c], g["cache_fox_v"][:, c]], axis=2)
            m["cv"] = np.ascontiguousarray(cvv.reshape(L, PAST, NH * HD), dtype=f32)
            m["clf"] = np.ascontiguousarray(g["cache_fox_logf"][:, c].reshape(L, 8, 128, 4).transpose(0, 1, 3, 2), dtype=f32)
        in_maps.append(m)

    res = run_bass_kernel_spmd(nc, in_maps, core_ids=list(range(8)), trace=getattr(cfg, "trace", False))
    R = res.results
    global LAST_R, LAST_RES
    LAST_R = R
    LAST_RES = res
    SEQ = cfg.SEQ
    y_p = np.zeros((2, SEQ, D), f32)
    y_s = np.zeros((8, DEC_T, D), f32)
    kp = np.zeros((L, 2, SEQ, NH, HD), f32)
    vp = np.zeros((L, 2, SEQ, NH, HD), f32)
    lp = np.zeros((L, 2, SEQ, 4), f32)
    ks = np.zeros((L, 8, DEC_T, NH, HD), f32)
    vsa = np.zeros((L, 8, DEC_T, NH, HD), f32)
    ls = np.zeros((L, 8, DEC_T, 4), f32)
    for c in range(8):
        b, j = c // 4, c % 4
        r = R[c]
        yt = r["yT"].T
        kt = r["kT_out"].transpose(0, 3, 1, 2)
        vt = r["v_out"].reshape(L, NTOT, NH, HD)
        lt = r["lf_out"].transpose(0, 2, 1)
        for i in range(NT):
            gt = 4 * i + j
            y_p[b, gt * 512:(gt + 1) * 512] = yt[i * 512:(i + 1) * 512]
            kp[:, b, gt * 512:(gt + 1) * 512] = kt[:, i * 512:(i + 1) * 512]
            vp[:, b, gt * 512:(gt + 1) * 512] = vt[:, i * 512:(i + 1) * 512]
            lp[:, b, gt * 512:(gt + 1) * 512] = lt[:, i * 512:(i + 1) * 512]
        y_s[c] = yt[NTOK:]
        ks[:, c] = kt[:, NTOK:]
        vsa[:, c] = vt[:, NTOK:]
        ls[:, c] = lt[:, NTOK:]
    return (y_p, y_s, kp[..., 0:4, :], kp[..., 4:8, :], vp[..., 0:4, :], vp[..., 4:8, :], lp,
            ks[..., 0:4, :], ks[..., 4:8, :], vsa[..., 0:4, :], vsa[..., 4:8, :], ls)


def kernel(**inputs):
    cfg = Cfg(NT=8, DEPTH=4, sample=True)
    outs = run(cfg, inputs)
    y_p, y_s, skp, fkp, svp, fvp, lp, sks, fks, svs, fvs, ls = outs
    c = np.ascontiguousarray
    return (c(y_p), c(y_s), c(skp), c(svp), c(fkp), c(fvp), c(lp), c(sks), c(svs), c(fks), c(fvs), c(ls))
```

```python
import contextlib
import numpy as np
import ml_dtypes
import concourse.bass as bass
import concourse.mybir as mybir
from concourse.bass_utils import run_bass_kernel_spmd

F32 = mybir.dt.float32
BF16 = mybir.dt.bfloat16
AF = mybir.ActivationFunctionType
ALU = mybir.AluOpType

D = 2048
KC = 16
HD = 128
NH = 8
PLE = 256
EPS = 1e-6
DEC_T = 32
PAST = 1024
QSCALE = HD ** -0.5


class Cfg:
    def __init__(self, NT=8, DEPTH=4, sample=True):
        self.NT = NT
        self.DEPTH = DEPTH
        self.sample = sample
        self.NTOK = NT * 512
        self.NTOT = self.NTOK + DEC_T
        self.NB = NT * 16
        self.SEQ = NT * 4 * 512


class Buf:
    __slots__ = ("name", "w", "r", "excl")

    def __init__(self, name, excl=False):
        self.name = name
        self.w = None
        self.r = {}
        self.excl = excl


ENGS = ("pe", "act", "dve", "pool", "sync")


def _flat(x):
    out = []
    for b in x:
        if isinstance(b, (list, tuple)):
            out.extend(_flat(b))
        else:
            out.append(b)
    return out


class Sched:
    def __init__(self, nc, sems, dsems):
        self.nc = nc
        self.sem = sems
        self.q = {e: [] for e in ENGS}
        self.cnt = {e: 0 for e in ENGS}
        self.waited = {e: {} for e in ENGS}
        self.dsem = dsems
        self.duse = {k: [0] * len(v) for k, v in dsems.items()}
        self.dnext = {k: 0 for k in dsems}
        self.n_ops = 0
        self.n_waits = 0

    def _semobj(self, key):
        if isinstance(key, tuple):
            return self.dsem[key[0]][key[1]]
        return self.sem[key]

    def _collect(self, reads, writes):
        need = {}

        def add(tok):
            if tok is None:
                return
            k, v = tok
            if need.get(k, 0) < v:
                need[k] = v

        for b in reads:
            add(b.w)
        for b in writes:
            add(b.w)
            for k, v in b.r.items():
                add((k, v))
        return need

    def _emit_waits(self, eng, need):
        wl = []
        wd = self.waited[eng]
        for k, v in need.items():
            if eng == "pe" and k == "pe":
                continue
            if wd.get(k, 0) >= v:
                continue
            wd[k] = v
            wl.append((self._semobj(k), v))
        return wl

    def _commit(self, tok, reads, writes):
        k, v = tok
        for b in reads:
            if b.r.get(k, 0) < v:
                b.r[k] = v
        for b in writes:
            b.w = tok
            b.r = {}

    def op(self, eng, fn, reads=(), writes=()):
        reads, writes = _flat(reads), _flat(writes)
        ex = [b for b in reads if b.excl]
        if ex:
            writes = list(writes) + ex
        need = self._collect(reads, writes)
        wl = self._emit_waits(eng, need)
        self.cnt[eng] += 1
        tok = (eng, self.cnt[eng])
        sem = self.sem[eng]
        self.n_ops += 1
        self.n_waits += len(wl)

        def run(e, wl=wl, fn=fn, sem=sem):
            for s, v in wl:
                e.wait_ge(s, v)
            fn(e).then_inc(sem, 1)

        self.q[eng].append(run)
        self._commit(tok, reads, writes)

    def dma(self, queue, out, in_, reads=(), writes=(), **kw):
        eng = "pool" if queue == "cast" else queue
        reads, writes = _flat(reads), _flat(writes)
        pool = self.dsem[queue]
        i = self.dnext[queue]
        self.dnext[queue] = (i + 1) % len(pool)
        self.duse[queue][i] += 1
        use = self.duse[queue][i]
        key = (queue, i)
        need = self._collect(reads, writes)
        if use > 1:
            if need.get(key, 0) < 16 * (use - 1):
                need[key] = 16 * (use - 1)
        wl = self._emit_waits(eng, need)
        sem = pool[i]
        self.n_ops += 1
        self.n_waits += len(wl)

        def run(e, wl=wl, sem=sem, out=out, in_=in_, kw=kw):
            for s, v in wl:
                e.wait_ge(s, v)
            e.dma_start(out=out, in_=in_, **kw).then_inc(sem, 16)

        self.q[eng].append(run)
        self._commit((key, 16 * use), reads, writes)

    def custom(self, eng, fn, semobj_key, val, reads=(), writes=()):
        reads, writes = _flat(reads), _flat(writes)
        need = self._collect(reads, writes)
        wl = self._emit_waits(eng, need)

        def run(e, wl=wl, fn=fn):
            for s, v in wl:
                e.wait_ge(s, v)
            fn(e)

        self.q[eng].append(run)
        self._commit((semobj_key, val), reads, writes)

    def final_waits(self, eng):
        wl = []
        for qn, pool in self.dsem.items():
            for i, s in enumerate(pool):
                if self.duse[qn][i] > 0:
                    wl.append((s, 16 * self.duse[qn][i]))

        def run(e, wl=wl):
            for s, v in wl:
                e.wait_ge(s, v)

        self.q[eng].append(run)


def build_program(cfg):
    NT, L, NTOK, NTOT, NB = cfg.NT, cfg.DEPTH, cfg.NTOK, cfg.NTOT, cfg.NB
    nc = bass.Bass("TRN2", target_bir_lowering=False)

    def din(name, shape, dt=F32):
        return nc.dram_tensor(name, list(shape), dt, kind="ExternalInput")

    def dout(name, shape, dt=F32):
        return nc.dram_tensor(name, list(shape), dt, kind="ExternalOutput")

    def dint(name, shape, dt):
        return nc.dram_tensor(name, list(shape), dt)

    xT = din("xT", [D, NTOT])
    pT = din("pT", [L, PLE, NTOT])
    w_in_r = din("w_in_r", [L * 32 * 128 * 2, 1024])
    w_f_r = din("w_f_r", [L * 128, 64])
    w_out_r = din("w_out_r", [L * 16 * 128, 1024])
    w_g_r = din("w_g_r", [L * 16 * 128 * 2, 1024])
    w_ple_r = din("w_ple_r", [L * 16 * 128, 256])
    NPAR = L * 48 + 16 + 8
    params = din("params", [128, NPAR])
    cbf = din("cbf", [128, 384], BF16)
    cf32 = din("cf32", [128, 6 * 128])
    masks = din("masks", [128, 2 * 16 * 512], BF16)
    smask = din("smask", [128, 2 * 32], BF16)
    if cfg.sample:
        ckT = din("ckT", [L, NH, HD, PAST])
        cv = din("cv", [L, PAST, NH * HD])
        clf = din("clf", [L, 8, 4, 128])

    yT = dout("yT", [D, NTOT])
    kT_out = dout("kT_out", [L, NH, HD, NTOT])
    v_out = dout("v_out", [L, NTOT, NH * HD])
    lf_out = dout("lf_out", [L, 4, NTOT])

    wb_in = dint("wb_in", [L * 32 * 128 * 2, 1024], BF16)
    wb_f = dint("wb_f", [L * 128, 64], BF16)
    wb_out = dint("wb_out", [L * 16 * 128, 1024], BF16)
    wb_g = dint("wb_g", [L * 16 * 128 * 2, 1024], BF16)
    wb_ple = dint("wb_ple", [L * 16 * 128, 256], BF16)
    xS = dint("xS", [D, NTOT], F32)
    qS = dint("qS", [NH, HD, NTOT], BF16)
    gS = dint("gS", [NH, HD, NTOT], BF16)
    ogS = dout("ogS", [NH, HD, NTOT], BF16) if getattr(cfg, "debug", False) else dint("ogS", [NH, HD, NTOT], BF16)
    skK = [[dint(f"skK{l}_{t}", [1024, 512], BF16) for t in range(NT)] for l in range(L)]
    skV = [[dint(f"skV{l}_{t}", [512, 1024], BF16) for t in range(NT)] for l in range(L)]
    gK = [[dint(f"gK{l}_{t}", [4 * 1024, 512], BF16) for t in range(NT)] for l in range(L)]
    gV = [[dint(f"gV{l}_{t}", [4 * 512, 1024], BF16) for t in range(NT)] for l in range(L)]
    slf = [dint(f"slf{l}", [4, NTOK], F32) for l in range(L)]
    glf = [dint(f"glf{l}", [16, NTOK], F32) for l in range(L)]
    sks = dint("sks", [NH, HD, DEC_T], BF16)
    svs = dint("svs", [DEC_T, NH * HD], BF16)
    slfs = dint("slfs", [4, DEC_T], F32)

    es = contextlib.ExitStack()
    with es:
        def sb(name, shape, dt):
            return es.enter_context(nc.sbuf_tensor(name, list(shape), dt))

        def ps(name):
            return es.enter_context(nc.psum_tensor(name, [128, 512], F32))

        sems = {e: es.enter_context(nc.semaphore("s_" + e)) for e in ("pe", "act", "dve", "pool")}
        dsems = {
            "sync": [es.enter_context(nc.semaphore(f"ds{i}")) for i in range(24)],
            "pool": [es.enter_context(nc.semaphore(f"dp{i}")) for i in range(12)],
            "cast": [es.enter_context(nc.semaphore(f"dc{i}")) for i in range(3)],
        }
        cc_sem = es.enter_context(nc.semaphore("cc"))
        sems["cc"] = cc_sem
        S = Sched(nc, sems, dsems)

        BIG = sb("BIG", [128, 32768], BF16)
        KT = BIG[:, 0:16384]
        VV = BIG[:, 16384:32768]
        xt = BIG[:, 0:16384].bitcast(F32).rearrange("p (k t) -> p k t", t=512)
        hT = BIG[:, 16384:24576].rearrange("p (k t) -> p k t", t=512)
        B_KT = [Buf(f"KT{g}") for g in range(8)]
        B_VVg = [Buf(f"VV{g}") for g in range(8)]
        B_VV0, B_VV1 = B_VVg[0:4], B_VVg[4:8]
        NWS = 6
        wring = [sb(f"w{i}", [128, 2048], BF16) for i in range(NWS)]
        b_wring = [Buf(f"w{i}") for i in range(NWS)]
        ogt = sb("ogt", [128, 8, 512], BF16)
        b_ogt = Buf("ogt")
        ptb = sb("ptb", [128, 2, 512], BF16)
        b_ptb = Buf("ptb")
        mask_t = sb("mask_t", [128, 2 * 16 * 512], BF16)
        smask_t = sb("smask_t", [128, 64], BF16)
        cb = sb("cb", [128, 384], BF16)
        cf = sb("cf", [128, 768], F32)
        par = sb("par", [128, NPAR], F32)
        b_const = Buf("const")
        ones_b, tri_b, omt_b = cb[:, 0:128], cb[:, 128:256], cb[:, 256:384]
        ident_f, ones_f, triinc_f, su_f, sel127_f, sel31_f = (cf[:, i * 128:(i + 1) * 128] for i in range(6))
        NST = 3
        stf = [sb(f"stf{i}", [128, 512], F32) for i in range(NST)]
        b_stf = [Buf(f"stf{i}") for i in range(NST)]
        stb = [sb(f"stb{i}", [128, 512], BF16) for i in range(NST)]
        b_stb = [Buf(f"stb{i}") for i in range(NST)]
        sqb = [sb(f"sqb{i}", [128, 512], BF16) for i in range(2)]
        b_sqb = [Buf(f"sqb{i}") for i in range(2)]
        rstd = sb("rstd", [128, 512], F32)
        b_rstd = Buf("rstd")
        tmpf = [sb(f"tmpf{i}", [128, 512], F32) for i in range(2)]
        b_tmpf = [Buf(f"tmpf{i}") for i in range(2)]
        qh = [sb(f"qh{i}", [128, 512], BF16) for i in range(2)]
        b_qh = [Buf(f"qh{i}") for i in range(2)]
        gh = [sb(f"gh{i}", [128, 512], BF16) for i in range(2)]
        b_gh = [Buf(f"gh{i}") for i in range(2)]
        qg_ctr = [0]

        def load_qg(hh, ti, c0, QW):
            i = qg_ctr[0] % 2
            qg_ctr[0] += 1
            S.dma("sync", qh[i][:, :QW], qS[hh, :, c0:c0 + QW], reads=[b_qS[hh][ti]], writes=[b_qh[i]])
            S.dma("sync", gh[i][:, :QW], gS[hh, :, c0:c0 + QW], reads=[b_gS[hh][ti]], writes=[b_gh[i]])
            return i
        NE = 4
        e_t = [sb(f"e{i}", [128, 512], F32) for i in range(NE)]
        b_e = [Buf(f"e{i}") for i in range(NE)]
        sp_t = [sb(f"sp{i}", [128, 512], BF16) for i in range(NE)]
        b_sp = [Buf(f"sp{i}") for i in range(NE)]
        t_t = [sb(f"t{i}", [128, 512], F32) for i in range(NE)]
        b_t = [Buf(f"t{i}") for i in range(NE)]
        NW = 3
        w_t = [sb(f"wt{i}", [128, 512], BF16) for i in range(NW)]
        b_w = [Buf(f"wt{i}") for i in range(NW)]
        ncb = sb("ncb", [128, 4, 128], F32)
        b_ncb = Buf("ncb")
        lnp = sb("lnp", [128, 4, 128], F32)
        b_lnp = Buf("lnp")
        ltb = sb("ltb", [128, 128], F32)
        b_ltb = Buf("ltb")
        tott = sb("tott", [128, 128], F32)
        b_tott = Buf("tott")
        xn = sb("xn", [128, 128], F32)
        b_xn = Buf("xn")
        ncref = sb("ncref", [128, 128], F32)
        b_ncref = Buf("ncref")
        bt = [sb(f"bt{i}", [128, 4, 128], F32) for i in range(2)]
        b_bt = [Buf(f"bt{i}") for i in range(2)]
        lfst = sb("lfst", [4, 512], F32)
        b_lfst = Buf("lfst")
        kts = sb("kts", [128, PAST + DEC_T], BF16)
        b_kts = Buf("kts")
        vs = sb("vs", [128, 9, 128], BF16)
        b_vs = Buf("vs")
        lnps = sb("lnps", [128, 4, 128], F32)
        b_lnps = Buf("lnps")
        ncbs = sb("ncbs", [128, 4, 16], F32)
        b_ncbs = Buf("ncbs")
        xs_ref = sb("xs_ref", [128, 16], F32)
        b_xsref = Buf("xs_ref")

        pbank = [ps(f"pb{i}") for i in range(8)]
        b_pb = [Buf(f"pb{i}", excl=True) for i in range(8)]

        b_xS = [Buf(f"xS{t}") for t in range(NT + 1)]
        b_qS = [[Buf(f"qS{h}_{t}") for t in range(NT + 1)] for h in range(NH)]
        b_gS = [[Buf(f"gS{h}_{t}") for t in range(NT + 1)] for h in range(NH)]
        b_ogS = [[Buf(f"ogS{h}_{t}") for t in range(NT + 1)] for h in range(NH)]
        b_skK = [[[] for _ in range(NT)] for _ in range(L)]
        b_skV = [[[] for _ in range(NT)] for _ in range(L)]
        b_gK = [[Buf(f"gK{l}_{t}") for t in range(NT)] for l in range(L)]
        b_gV = [[Buf(f"gV{l}_{t}") for t in range(NT)] for l in range(L)]
        b_slf = [[] for _ in range(L)]
        b_glf = [Buf(f"glf{l}") for l in range(L)]
        b_wb = {}
        b_sks, b_svs, b_slfs = [Buf(f"sks{h}") for h in range(NH)], [Buf(f"svs{h}") for h in range(NH)], Buf("slfs")

        def pcol(l, what, k=None):
            base = l * 48
            if what == "g_attn":
                return par[:, base + k: base + k + 1]
            if what == "g_ple":
                return par[:, base + 16 + k: base + 16 + k + 1]
            if what == "g_o":
                return par[:, base + 32 + k: base + 32 + k + 1]
            if what == "nbf":
                return par[0:4, base + 40: base + 41]
            raise KeyError(what)

        def gfin(k):
            return par[:, L * 48 + k: L * 48 + k + 1]

        onehot = par[:, L * 48 + 16: L * 48 + 20]

        S.dma("sync", cb[:], cbf[:, :], writes=[b_const])
        S.dma("sync", cf[:], cf32[:, :], writes=[b_const])
        S.dma("sync", par[:], params[:, :], writes=[b_const])
        S.dma("sync", mask_t[:], masks[:, :], writes=[b_const])
        S.dma("sync", smask_t[:], smask[:, :], writes=[b_const])

        def cast_rows(dst, src, r0, r1, key, step=2048):
            bl = []
            for a in range(r0, r1, step):
                b = min(a + step, r1)
                bb = Buf(f"wb_{key}_{a}")
                S.dma("cast", dst[a:b, :], src[a:b, :], writes=[bb])
                bl.append(bb)
            return bl

        def cast_layer(l, part):
            if part == 0:
                b_wb[("in", l)] = cast_rows(wb_in, w_in_r, l * 8192, l * 8192 + 4096, f"in{l}")
                b_wb[("f", l)] = cast_rows(wb_f, w_f_r, l * 128, (l + 1) * 128, f"f{l}")
            elif part == 1:
                b_wb[("in", l)] += cast_rows(wb_in, w_in_r, l * 8192 + 4096, (l + 1) * 8192, f"in{l}b")
            elif part == 2:
                b_wb[("out", l)] = cast_rows(wb_out, w_out_r, l * 2048, (l + 1) * 2048, f"out{l}")
                b_wb[("ple", l)] = cast_rows(wb_ple, w_ple_r, l * 2048, (l + 1) * 2048, f"ple{l}")
            elif part == 3:
                b_wb[("g", l)] = cast_rows(wb_g, w_g_r, l * 4096, (l + 1) * 4096, f"g{l}")

        for part in range(4):
            cast_layer(0, part)

        wctr = [0]

        def load_w(kind, l, c):
            i = wctr[0] % NWS
            wctr[0] += 1
            slot, bs = wring[i], b_wring[i]
            if kind == "in":
                r = (l * 32 + c) * 256
                S.dma("sync", slot[:, :], wb_in[r:r + 256, :].rearrange("(p t) x -> p (t x)", t=2),
                      reads=b_wb[("in", l)], writes=[bs])
                return slot[:, :].rearrange("p (k n) -> p k n", n=128), bs
            if kind == "g":
                r = (l * 16 + c) * 256
                S.dma("sync", slot[:, :], wb_g[r:r + 256, :].rearrange("(p t) x -> p (t x)", t=2),
                      reads=b_wb[("g", l)], writes=[bs])
                return slot[:, :].rearrange("p (k n) -> p k n", n=128), bs
            if kind == "out":
                r = (l * 16 + c) * 128
                S.dma("sync", slot[:, 0:1024], wb_out[r:r + 128, :], reads=b_wb[("out", l)], writes=[bs])
                return slot[:, 0:1024].rearrange("p (k n) -> p k n", n=128), bs
            if kind == "ple":
                r = (l * 16 + c) * 128
                S.dma("sync", slot[:, 0:256], wb_ple[r:r + 128, :], reads=b_wb[("ple", l)], writes=[bs])
                return slot[:, 0:256].rearrange("p (k n) -> p k n", n=128), bs
            if kind == "f":
                S.dma("sync", slot[:, 0:64], wb_f[l * 128:(l + 1) * 128, :], reads=b_wb[("f", l)], writes=[bs])
                return slot[:, 0:64].rearrange("p (k n) -> p k n", n=4), bs
            raise KeyError(kind)

        pa_ctr = [0]

        def pa_bank():
            i = pa_ctr[0] % 3
            pa_ctr[0] += 1
            return pbank[i], b_pb[i]

        st_ctr = [0]

        def stage():
            i = st_ctr[0] % NST
            st_ctr[0] += 1
            return stf[i], b_stf[i], stb[i], b_stb[i]

        def rms_stats(TW):
            acc, bacc = pbank[5], b_pb[5]
            for k in range(KC):
                i = k % 2
                S.op("act", lambda e, k=k, i=i: e.activation(out=sqb[i][:, :TW], in_=xt[:, k, :TW], func=AF.Square),
                     reads=[B_KT], writes=[b_sqb[i]])
                S.op("pe", lambda e, k=k, i=i: e.matmul(acc[:, :TW], lhsT=ones_b, rhs=sqb[i][:, :TW],
                                                        start=(k == 0), stop=(k == KC - 1)),
                     reads=[b_sqb[i], b_const], writes=[bacc])
            S.op("act", lambda e: e.activation(out=rstd[:, :TW], in_=acc[:, :TW], func=AF.Sqrt, bias=EPS_t[:, 0:1],
                                               scale=1.0 / D),
                 reads=[bacc, b_const], writes=[b_rstd])
            S.op("dve", lambda e: e.reciprocal(out=rstd[:, :TW], in_=rstd[:, :TW]), reads=[b_rstd], writes=[b_rstd])

        def make_h(TW, gname, l):
            for k in range(KC):
                g = pcol(l, gname, k) if gname != "fin" else gfin(k)
                S.op("dve", lambda e, k=k, g=g: e.scalar_tensor_tensor(
                    out=hT[:, k, :TW], in0=xt[:, k, :TW], scalar=g, in1=rstd[:, :TW], op0=ALU.mult, op1=ALU.mult),
                     reads=[B_KT, b_rstd, b_const], writes=[B_VV0])

        S.op("dve", lambda e: e.memset(xn[:, :], 0.0), writes=[b_xn])
        EPS_t = sb("eps_t", [128, 2], F32)
        S.op("dve", lambda e: e.memset(EPS_t[:, 0:1], EPS), writes=[b_const])
        S.op("dve", lambda e: e.memset(EPS_t[:, 1:2], 1.0), writes=[b_const])

        def phase_c(l, ti, TW, c0):
            S.dma("sync", ogt[:, :, :TW], ogS[:, :, c0:c0 + TW].rearrange("h p t -> p h t"),
                  reads=[b_ogS[h][ti] for h in range(NH)], writes=[b_ogt])
            S.dma("pool", ptb[:, :, :TW], pT[l, :, c0:c0 + TW].rearrange("(k p) t -> p k t", p=128),
                  writes=[b_ptb])
            for c in range(KC):
                wv, bw = load_w("out", l, c)
                pb, bpb = pa_bank()
                for k in range(8):
                    S.op("pe", lambda e, k=k, wv=wv, pb=pb: e.matmul(pb[:, :TW], lhsT=wv[:, k, :], rhs=ogt[:, k, :TW],
                                                                      start=(k == 0), stop=(k == 7)),
                         reads=[bw, b_ogt], writes=[bpb])
                S.op("dve", lambda e, c=c, pb=pb: e.tensor_tensor(out=xt[:, c, :TW], in0=pb[:, :TW], in1=xt[:, c, :TW],
                                                                  op=ALU.add),
                     reads=[bpb, B_KT], writes=[B_KT])
            rms_stats(TW)
            make_h(TW, "g_ple", l)
            for c in range(KC):
                wv, bw = load_w("g", l, c)
                pb, bpb = pa_bank()
                for k in range(KC):
                    S.op("pe", lambda e, k=k, wv=wv, pb=pb: e.matmul(pb[:, :TW], lhsT=wv[:, k, :], rhs=hT[:, k, :TW],
                                                                      start=(k == 0), stop=(k == KC - 1)),
                         reads=[bw, B_VV0], writes=[bpb])
                i = c % 2
                S.op("act", lambda e, pb=pb, i=i: e.activation(out=tmpf[i][:, :TW], in_=pb[:, :TW], func=AF.Sigmoid),
                     reads=[bpb], writes=[b_tmpf[i]])
                wv2, bw2 = load_w("ple", l, c)
                pb2, bpb2 = pa_bank()
                for k in range(2):
                    S.op("pe", lambda e, k=k, wv2=wv2, pb2=pb2: e.matmul(pb2[:, :TW], lhsT=wv2[:, k, :],
                                                                          rhs=ptb[:, k, :TW], start=(k == 0), stop=(k == 1)),
                         reads=[bw2, b_ptb], writes=[bpb2])
                S.op("dve", lambda e, pb2=pb2, i=i: e.tensor_tensor(out=tmpf[i][:, :TW], in0=pb2[:, :TW],
                                                                    in1=tmpf[i][:, :TW], op=ALU.mult),
                     reads=[bpb2, b_tmpf[i]], writes=[b_tmpf[i]])
                S.op("dve", lambda e, c=c, i=i: e.tensor_tensor(out=xt[:, c, :TW], in0=xt[:, c, :TW],
                                                                in1=tmpf[i][:, :TW], op=ALU.add),
                     reads=[b_tmpf[i], B_KT], writes=[B_KT])

        def phase_a(l, ti, TW, c0):
            is_s = ti == NT
            AP_ = getattr(cfg, "aparts", 255)
            rms_stats(TW)
            make_h(TW, "g_attn", l)
            if not AP_ & 1:
                return
            nsub = max(TW // 128, 1)
            SW = min(TW, 128)
            pdma = S.dma if AP_ & 8 else (lambda *a, **k: None)
            kinds_ok = getattr(cfg, 'kinds', 'qkvz')
            for c in range(32):
                grp, hh4 = c // 4, c % 4
                kind = ("q", "k", "v", "z")[grp % 4]
                if kind not in kinds_ok:
                    continue
                hh = hh4 + (4 if grp >= 4 else 0)
                wv, bw = load_w("in", l, c)
                pb, bpb = pa_bank()
                sf, bsf, sbb, bsb = stage()
                if kind != "v":
                    for k in range(KC):
                        S.op("pe", lambda e, k=k, wv=wv, pb=pb: e.matmul(pb[:, :TW], lhsT=wv[:, k, :], rhs=hT[:, k, :TW],
                                                                          start=(k == 0), stop=(k == KC - 1)),
                             reads=[bw, B_VV0], writes=[bpb])
                else:
                    for s in range(nsub):
                        for k in range(KC):
                            S.op("pe", lambda e, k=k, s=s, wv=wv, pb=pb: e.matmul(
                                pb[:SW, s * 128:(s + 1) * 128], lhsT=hT[:, k, s * 128:s * 128 + SW], rhs=wv[:, k, :],
                                start=(k == 0), stop=(k == KC - 1)),
                                 reads=[bw, B_VV0], writes=[bpb])
                if kind == "q":
                    S.op("act", lambda e, pb=pb, sbb=sbb: e.activation(out=sbb[:, :TW], in_=pb[:, :TW], func=AF.Identity,
                                                                         scale=QSCALE),
                         reads=[bpb], writes=[bsb])
                    pdma("pool", qS[hh, :, c0:c0 + TW], sbb[:, :TW], reads=[bsb], writes=[b_qS[hh][ti]])
                elif kind == "z":
                    S.op("act", lambda e, pb=pb, sbb=sbb: e.activation(out=sbb[:, :TW], in_=pb[:, :TW], func=AF.Silu),
                         reads=[bpb], writes=[bsb])
                    pdma("pool", gS[hh, :, c0:c0 + TW], sbb[:, :TW], reads=[bsb], writes=[b_gS[hh][ti]])
                elif kind == "k":
                    S.op("act", lambda e, pb=pb, sf=sf: e.activation(out=sf[:, :TW], in_=pb[:, :TW], func=AF.Identity),
                         reads=[bpb], writes=[bsf])
                    S.op("dve", lambda e, pb=pb, sbb=sbb: e.tensor_copy(out=sbb[:, :TW], in_=pb[:, :TW]),
                         reads=[bpb], writes=[bsb])
                    pdma("pool", kT_out[l, hh, :, c0:c0 + TW], sf[:, :TW], reads=[bsf])
                    if not is_s:
                        bb = Buf("skvp")
                        b_skK[l][ti].append(bb)
                        pdma("pool", skK[l][ti][hh * 128:(hh + 1) * 128, :], sbb[:, :TW], reads=[bsb], writes=[bb])
                    else:
                        pdma("pool", sks[hh, :, :], sbb[:, :TW], reads=[bsb], writes=[b_sks[hh]])
                else:
                    W4 = nsub * 128
                    S.op("act", lambda e, pb=pb, sf=sf: e.activation(out=sf[:SW, :W4], in_=pb[:SW, :W4], func=AF.Identity),
                         reads=[bpb], writes=[bsf])
                    S.op("dve", lambda e, pb=pb, sbb=sbb: e.tensor_copy(out=sbb[:SW, :W4], in_=pb[:SW, :W4]),
                         reads=[bpb], writes=[bsb])
                    pdma("pool", v_out[l, c0:c0 + TW, hh * 128:(hh + 1) * 128].rearrange("(s t) d -> t s d", t=SW),
                          sf[:SW, :W4].rearrange("t (s d) -> t s d", d=128), reads=[bsf])
                    if not is_s:
                        bb = Buf("skvp")
                        b_skV[l][ti].append(bb)
                        pdma("pool", skV[l][ti][:, hh * 128:(hh + 1) * 128].rearrange("(s t) d -> t s d", t=SW),
                              sbb[:SW, :W4].rearrange("t (s d) -> t s d", d=128), reads=[bsb], writes=[bb])
                    else:
                        pdma("pool", svs[:, hh * 128:(hh + 1) * 128], sbb[:SW, :128], reads=[bsb], writes=[b_svs[hh]])
            if not AP_ & 2:
                return
            wv, bw = load_w("f", l, 0)
            pb, bpb = pa_bank()
            for k in range(KC):
                S.op("pe", lambda e, k=k, wv=wv, pb=pb: e.matmul(pb[0:4, :TW], lhsT=wv[:, k, :], rhs=hT[:, k, :TW],
                                                                  start=(k == 0), stop=(k == KC - 1)),
                     reads=[bw, B_VV0], writes=[bpb])
            S.op("act", lambda e, pb=pb: e.activation(out=lfst[:, :TW], in_=pb[0:4, :TW], func=AF.Exp,
                                                      bias=pcol(l, "nbf"), scale=-1.0),
                 reads=[bpb, b_const], writes=[b_lfst])
            S.op("act", lambda e: e.activation(out=lfst[:, :TW], in_=lfst[:, :TW], func=AF.Ln, bias=EPS_t[0:4, 1:2]),
                 reads=[b_lfst, b_const], writes=[b_lfst])
            S.op("dve", lambda e: e.tensor_scalar(out=lfst[:, :TW], in0=lfst[:, :TW], scalar1=-1.0, scalar2=None,
                                                  op0=ALU.mult),
                 reads=[b_lfst], writes=[b_lfst])
            S.dma("pool", lf_out[l, :, c0:c0 + TW], lfst[:, :TW], reads=[b_lfst])
            if not is_s:
                bb = Buf("slfp")
                b_slf[l].append(bb)
                S.dma("pool", slf[l][:, c0:c0 + TW], lfst[:, :TW], reads=[b_lfst], writes=[bb])
            else:
                S.dma("pool", slfs[:, :], lfst[:, :TW], reads=[b_lfst], writes=[b_slfs])
            if not AP_ & 4:
                return
            if not is_s:
                all_gather(skK[l][ti], gK[l][ti], b_skK[l][ti], b_gK[l][ti])
                all_gather(skV[l][ti], gV[l][ti], b_skV[l][ti], b_gV[l][ti])
                if ti == NT - 1:
                    all_gather(slf[l], glf[l], b_slf[l], b_glf[l])

        def final_norm(ti, TW, c0):
            rms_stats(TW)
            for k in range(KC):
                i = k % 2
                S.op("dve", lambda e, k=k, i=i: e.scalar_tensor_tensor(
                    out=tmpf[i][:, :TW], in0=xt[:, k, :TW], scalar=gfin(k), in1=rstd[:, :TW], op0=ALU.mult, op1=ALU.mult),
                     reads=[B_KT, b_rstd, b_const], writes=[b_tmpf[i]])
                S.dma("pool", yT[k * 128:(k + 1) * 128, c0:c0 + TW], tmpf[i][:, :TW], reads=[b_tmpf[i]])

        def tile_pass(l):
            tiles = list(range(NT)) + ([NT] if cfg.sample else [])
            for ti in tiles:
                TW = 512 if ti < NT else DEC_T
                c0 = ti * 512
                src = xT if l == 0 else xS
                S.dma("sync", xt[:, :, :TW], src[:, c0:c0 + TW].rearrange("(k p) t -> p k t", p=128),
                      reads=([b_xS[ti]] if l > 0 else []), writes=[B_KT])
                if l > 0:
                    phase_c(l - 1, ti, TW, c0)
                if l < L:
                    S.dma("pool", xS[:, c0:c0 + TW].rearrange("(k p) t -> p k t", p=128), xt[:, :, :TW],
                          reads=[B_KT], writes=[b_xS[ti]])
                    phase_a(l, ti, TW, c0)
                else:
                    final_norm(ti, TW, c0)

        def all_gather(src, dst, reads, bdst):
            S.cc_n += 1
            n = S.cc_n
            S.custom("pool", lambda e, src=src, dst=dst: e.collective_compute(
                "AllGather", ALU.bypass, replica_groups=[[0, 1, 2, 3], [4, 5, 6, 7]],
                ins=[src.ap().opt()], outs=[dst.ap().opt()]).then_inc(cc_sem, 1),
                     "cc", n, reads=reads, writes=[bdst])

        S.cc_n = 0

        def cumsum_tables(l):
            for ip in range(NT):
                for jp in range(4):
                    S.dma("sync", lnp[16 * ip + 4 * jp:16 * ip + 4 * jp + 4, :, :],
                          glf[l][jp * 4:(jp + 1) * 4, 512 * ip:512 * ip + 512].rearrange("h (s p) -> s h p", p=128),
                          reads=[b_glf[l]], writes=[b_lnp])
            cum_core(lnp, b_lnp, NB, ncb, b_ncb, NB)
            for h in range(4):
                for jp in range(4):
                    src = ncb[:, h, :NB].rearrange("p (i m) -> p i m", m=16)[:, :, 4 * jp:4 * jp + 4]
                    dst = xn[:, h * 32:h * 32 + NT * 4].rearrange("p (i s) -> p i s", s=4)
                    if jp == 0:
                        S.op("dve", lambda e, src=src, dst=dst: e.tensor_scalar(
                            out=dst, in0=src, scalar1=onehot[:, 0:1], scalar2=None, op0=ALU.mult),
                             reads=[b_ncb, b_const], writes=[b_xn])
                    else:
                        S.op("dve", lambda e, src=src, dst=dst, jp=jp: e.scalar_tensor_tensor(
                            out=dst, in0=src, scalar=onehot[:, jp:jp + 1], in1=dst, op0=ALU.mult, op1=ALU.add),
                             reads=[b_ncb, b_const, b_xn], writes=[b_xn])
            pb, bpb = pbank[6], b_pb[6]
            S.op("pe", lambda e: e.matmul(pb[:, 0:128], lhsT=sel127_f, rhs=xn[:, :], start=True, stop=True),
                 reads=[b_xn, b_const], writes=[bpb])
            S.op("act", lambda e: e.activation(out=ncref[:, :], in_=pb[:, 0:128], func=AF.Identity), reads=[bpb],
                 writes=[b_ncref])

        def cum_core(Lsrc, bL, nb, dst, bdst, nbp):
            pb, bpb = pbank[6], b_pb[6]
            pb2, bpb2 = pbank[7], b_pb[7]
            for h in range(4):
                S.op("pe", lambda e, h=h: e.transpose(pb[:, 0:nb], Lsrc[:nb, h, :], ident_f[:nb, :nb]),
                     reads=[bL, b_const], writes=[bpb])
                S.op("act", lambda e: e.activation(out=ltb[:, :nb], in_=pb[:, 0:nb], func=AF.Identity), reads=[bpb],
                     writes=[b_ltb])
                S.op("pe", lambda e: e.matmul(pb2[:nb, 0:128], lhsT=ltb[:, :nb], rhs=ones_f, start=True, stop=True),
                     reads=[b_ltb, b_const], writes=[bpb2])
                S.op("act", lambda e: e.activation(out=tott[:nb, :], in_=pb2[:nb, 0:128], func=AF.Identity), reads=[bpb2],
                     writes=[b_tott])
                S.op("pe", lambda e: e.matmul(pb[:, 128:128 + nb], lhsT=triinc_f, rhs=ltb[:, :nb], start=True, stop=False),
                     reads=[b_ltb, b_const], writes=[bpb])
                S.op("pe", lambda e: e.matmul(pb[:, 128:128 + nb], lhsT=tott[:nb, :], rhs=su_f[:nb, :nb], start=False,
                                              stop=True),
                     reads=[b_tott, b_const], writes=[bpb])
                S.op("act", lambda e, h=h: e.activation(out=dst[:, h, :nb], in_=pb[:, 128:128 + nb], func=AF.Identity,
                                                        scale=-1.0),
                     reads=[bpb], writes=[bdst])

        ectr = [0]
        wctr2 = [0]

        acc_t = [sb(f"acc{i}", [128, 512], BF16) for i in range(2)]
        b_acc = [Buf(f"acc{i}") for i in range(2)]
        rc = {"z": 0, "e": 0, "t": 0, "w": 0, "p": 0, "a": 0}

        def attn_tile(hh, QW, q_ap, bq, steps, bias_of, kind, ob=0, hook=None):
            Zb = (0, 1, 2)
            Pb = (6, 7)
            O, bO = pbank[3 + ob], b_pb[3 + ob]
            DEN, bDEN = pbank[6 + ob], b_pb[6 + ob]
            ns = len(steps)
            R = [dict() for _ in range(ns)]

            def qk(si):
                kt_ap, v_ap, KP, m_ap, rds = steps[si]
                zi = Zb[rc["z"] % 3]
                rc["z"] += 1
                R[si]["z"] = (pbank[zi], b_pb[zi])
                z = pbank[zi]
                S.op("pe", lambda e, z=z, kt_ap=kt_ap, KP=KP: e.matmul(z[:KP, :QW], lhsT=kt_ap, rhs=q_ap, start=True,
                                                                        stop=True),
                     reads=rds + [bq], writes=[b_pb[zi]])

            def pv(si):
                kt_ap, v_ap, KP, m_ap, rds = steps[si]
                wt, bw = R[si]["w"]
                first, last = si == 0, si == ns - 1
                S.op("pe", lambda e, wt=wt, v_ap=v_ap, KP=KP, first=first, last=last: e.matmul(
                    O[:, :QW], lhsT=v_ap, rhs=wt[:KP, :QW], start=first, stop=last),
                     reads=rds + [bw], writes=[bO])
                if kind == "fox":
                    S.op("pe", lambda e, wt=wt, KP=KP, first=first, last=last: e.matmul(
                        DEN[:, :QW], lhsT=ones_b[:KP, :], rhs=wt[:KP, :QW], start=first, stop=last),
                         reads=[bw, b_const], writes=[bDEN])

            def take_w(si):
                wi = rc["w"] % NW
                rc["w"] += 1
                R[si]["w"] = (w_t[wi], b_w[wi])
                return w_t[wi], b_w[wi]

            def fox_x(si):
                kt_ap, v_ap, KP, m_ap, rds = steps[si]
                z, bz = R[si]["z"]
                wt, bw = take_w(si)
                sw_ = min(QW, 128)
                for sq in range(max(QW // 128, 1)):
                    bcol, bbt = bias_of(si, sq)
                    S.op("act", lambda e, z=z, wt=wt, KP=KP, bcol=bcol, sq=sq, sw_=sw_: e.activation(
                        out=wt[:KP, sq * sw_:(sq + 1) * sw_], in_=z[:KP, sq * sw_:(sq + 1) * sw_], func=AF.Exp,
                        bias=bcol),
                         reads=[bz, bbt], writes=[bw])
                if m_ap is not None:
                    S.op("dve", lambda e, wt=wt, m_ap=m_ap, KP=KP: e.tensor_tensor(
                        out=wt[:KP, :QW], in0=wt[:KP, :QW], in1=m_ap, op=ALU.mult),
                         reads=[bw, b_const], writes=[bw])

            hook_at = min(3, ns - 1)
            if kind == "fox":
                qk(0)
                if ns > 1:
                    qk(1)
                for si in range(ns):
                    if si + 2 < ns:
                        qk(si + 2)
                    fox_x(si)
                    if si >= 1:
                        pv(si - 1)
                    if si == hook_at and hook is not None:
                        hook()
                pv(ns - 1)
                return

            def sb_el(si):
                kt_ap, v_ap, KP, m_ap, rds = steps[si]
                z, bz = R[si]["z"]
                ei = rc["e"] % NE
                rc["e"] += 1
                et, be, spt, bsp = e_t[ei], b_e[ei], sp_t[ei], b_sp[ei]
                R[si]["e"] = (et, be)
                R[si]["sp"] = (spt, bsp)
                S.op("act", lambda e, z=z, et=et, KP=KP: e.activation(out=et[:KP, :QW], in_=z[:KP, :QW], func=AF.Exp),
                     reads=[bz], writes=[be])
                if m_ap is not None:
                    S.op("dve", lambda e, et=et, m_ap=m_ap, KP=KP: e.tensor_tensor(
                        out=et[:KP, :QW], in0=et[:KP, :QW], in1=m_ap, op=ALU.mult),
                         reads=[be, b_const], writes=[be])
                S.op("act", lambda e, et=et, spt=spt, KP=KP: e.activation(out=spt[:KP, :QW], in_=et[:KP, :QW],
                                                                          func=AF.Ln, bias=EPS_t[:KP, 1:2]),
                     reads=[be, b_const], writes=[bsp])

            def sb_p(si):
                kt_ap, v_ap, KP, m_ap, rds = steps[si]
                spt, bsp = R[si]["sp"]
                pi = Pb[rc["p"] % 2]
                rc["p"] += 1
                P, bP = pbank[pi], b_pb[pi]
                R[si]["p"] = (P, bP)
                S.op("pe", lambda e, spt=spt, KP=KP, P=P, si=si: e.matmul(
                    P[:KP, :QW], lhsT=tri_b[:KP, :KP], rhs=spt[:KP, :QW], start=True, stop=(si == 0)),
                     reads=[bsp, b_const], writes=[bP])
                if si > 0:
                    at, ba = R[si]["acc"]
                    S.op("pe", lambda e, at=at, KP=KP, P=P: e.matmul(
                        P[:KP, :QW], lhsT=ones_b[:, :KP], rhs=at[:, :QW], start=False, stop=True),
                         reads=[ba, b_const], writes=[bP])

            def sb_acc(si):
                kt_ap, v_ap, KP, m_ap, rds = steps[si]
                spt, bsp = R[si]["sp"]
                ai = rc["a"] % 2
                rc["a"] += 1
                an, ban = acc_t[ai], b_acc[ai]
                R[si + 1]["acc"] = (an, ban)
                if si == 0:
                    if KP < 128:
                        S.op("pool", lambda e, an=an: e.memset(an[:, :QW], 0.0), writes=[ban])
                    S.op("pool", lambda e, an=an, spt=spt, KP=KP: e.tensor_copy(out=an[:KP, :QW], in_=spt[:KP, :QW]),
                         reads=[bsp], writes=[ban])
                else:
                    ac, bac = R[si]["acc"]
                    if KP < 128:
                        raise NotImplementedError
                    S.op("pool", lambda e, an=an, ac=ac, spt=spt: e.tensor_tensor(
                        out=an[:, :QW], in0=ac[:, :QW], in1=spt[:, :QW], op=ALU.add),
                         reads=[bac, bsp], writes=[ban])

            def sb_tw(si):
                kt_ap, v_ap, KP, m_ap, rds = steps[si]
                et, be = R[si]["e"]
                P, bP = R[si]["p"]
                ti_ = rc["t"] % NE
                rc["t"] += 1
                tt, btt = t_t[ti_], b_t[ti_]
                wt, bw = take_w(si)
                S.op("act", lambda e, tt=tt, KP=KP, P=P: e.activation(out=tt[:KP, :QW], in_=P[:KP, :QW], func=AF.Exp,
                                                                     scale=-1.0),
                     reads=[bP], writes=[btt])
                S.op("dve", lambda e, wt=wt, et=et, tt=tt, KP=KP: e.tensor_tensor(
                    out=wt[:KP, :QW], in0=et[:KP, :QW], in1=tt[:KP, :QW], op=ALU.mult),
                     reads=[be, btt], writes=[bw])

            qk(0)
            sb_el(0)
            if ns > 1:
                qk(1)
                sb_el(1)
            for si in range(ns):
                if si + 2 < ns:
                    qk(si + 2)
                    sb_el(si + 2)
                sb_p(si)
                if si + 1 < ns:
                    sb_acc(si)
                sb_tw(si)
                if si >= 1:
                    pv(si - 1)
                if si == hook_at and hook is not None:
                    hook()
            pv(ns - 1)

        def epilogue(l, hh, QW, c0, ti, gate_ap, bgate, kind, ob=0):
            O, bO = pbank[3 + ob], b_pb[3 + ob]
            DEN, bDEN = pbank[6 + ob], b_pb[6 + ob]
            SS, bSS = pbank[5], b_pb[5]
            i = ectr[0] % 2
            if kind == "fox":
                S.op("dve", lambda e: e.reciprocal(out=tmpf[0][:, :QW], in_=DEN[:, :QW]), reads=[bDEN], writes=[b_tmpf[0]])
                S.op("dve", lambda e: e.tensor_tensor(out=tmpf[1][:, :QW], in0=O[:, :QW], in1=tmpf[0][:, :QW], op=ALU.mult),
                     reads=[bO, b_tmpf[0]], writes=[b_tmpf[1]])
                u_ap, bu = tmpf[1], b_tmpf[1]
            else:
                u_ap, bu = O, bO
            S.op("act", lambda e: e.activation(out=sqb[i][:, :QW], in_=u_ap[:, :QW], func=AF.Square), reads=[bu],
                 writes=[b_sqb[i]])
            S.op("pe", lambda e: e.matmul(SS[:, :QW], lhsT=ones_b, rhs=sqb[i][:, :QW], start=True, stop=True),
                 reads=[b_sqb[i], b_const], writes=[bSS])
            S.op("act", lambda e: e.activation(out=tmpf[0][:, :QW], in_=SS[:, :QW], func=AF.Sqrt, bias=EPS_t[:, 0:1],
                                               scale=1.0 / 128),
                 reads=[bSS, b_const], writes=[b_tmpf[0]])
            S.op("dve", lambda e: e.reciprocal(out=tmpf[0][:, :QW], in_=tmpf[0][:, :QW]), reads=[b_tmpf[0]],
                 writes=[b_tmpf[0]])
            S.op("dve", lambda e: e.scalar_tensor_tensor(out=tmpf[1][:, :QW], in0=u_ap[:, :QW], scalar=pcol(l, "g_o", hh),
                                                         in1=tmpf[0][:, :QW], op0=ALU.mult, op1=ALU.mult),
                 reads=[bu, b_tmpf[0], b_const], writes=[b_tmpf[1]])
            sf, bsf, sbb, bsb = stage()
            S.op("dve", lambda e, sbb=sbb: e.tensor_tensor(out=sbb[:, :QW], in0=tmpf[1][:, :QW], in1=gate_ap, op=ALU.mult),
                 reads=[b_tmpf[1], bgate], writes=[bsb])
            S.dma("pool", ogS[hh, :, c0:c0 + QW], sbb[:, :QW], reads=[bsb], writes=[b_ogS[hh][ti]])

        ob_ctr = [0]
        pend = [None]

        def next_ob():
            i = ob_ctr[0] % 2
            ob_ctr[0] += 1
            return i

        def phase_b(l):
            cumsum_tables(l)
            if cfg.sample:
                sample_tables(l)
            if getattr(cfg, "stop", 99) < 4:
                return
            for hh in range(NH if getattr(cfg, "stop", 99) >= 5 else 1):
                kind = "sb" if hh < 4 else "fox"
                if l + 1 < L and hh < 4:
                    cast_layer(l + 1, hh)
                qi = hh % 2
                def load_kv(hx, ip):
                    for jp in range(4):
                        n0 = 16 * ip + 4 * jp
                        S.dma("sync", KT[:, n0 * 128:(n0 + 4) * 128],
                              gK[l][ip][jp * 1024 + hx * 128: jp * 1024 + hx * 128 + 128, :],
                              reads=[b_gK[l][ip]], writes=[B_KT[ip]])
                        S.dma("sync", VV[:, n0 * 128:(n0 + 4) * 128].rearrange("p (s d) -> p s d", d=128),
                              gV[l][ip][jp * 512:(jp + 1) * 512, hx * 128:(hx + 1) * 128].rearrange("(s p) d -> p s d", p=128),
                              reads=[b_gV[l][ip]], writes=[B_VVg[ip]])

                for ip in (range(NT) if hh == 0 else range(1)):
                    load_kv(hh, ip)
                def fox_bias(ti):
                    h = hh - 4
                    bi = ti % 2
                    nkb_ = 16 * ti + 16
                    for sq in range(4):
                        cix = h * 32 + ti * 4 + sq
                        S.op("dve", lambda e, h=h, bi=bi, sq=sq, cix=cix, nkb_=nkb_: e.tensor_scalar(
                            out=bt[bi][:, sq, :nkb_], in0=ncb[:, h, :nkb_], scalar1=ncref[:, cix:cix + 1],
                            scalar2=0.0, op0=ALU.subtract, op1=ALU.min),
                             reads=[b_ncb, b_ncref], writes=[b_bt[bi]])

                if kind == "fox":
                    fox_bias(NT - 1)
                for ti in range(NT - 1, -1, -1):
                    nkb = 16 * ti + 16
                    order = list(range(nkb - 1, -1, -1)) if kind == "sb" else list(range(nkb))
                    steps = []
                    for n in order:
                        m = n - 16 * ti
                        m_ap = None
                        if m >= 0:
                            off = ((0 if kind == "sb" else 1) * 16 + m) * 512
                            m_ap = mask_t[:, off:off + 512]
                        steps.append((KT[:, n * 128:(n + 1) * 128], VV[:, n * 128:(n + 1) * 128], 128, m_ap,
                                      [B_KT[n // 16], B_VVg[n // 16]]))
                    bias_of = None
                    if kind == "fox":
                        bi = ti % 2
                        bias_of = lambda si, sq, bi=bi, order=order: (bt[bi][:, sq, order[si]:order[si] + 1], b_bt[bi])
                    qi = load_qg(hh, ti, ti * 512, 512)
                    ob = next_ob()
                    hk = pend[0]
                    pend[0] = None
                    if kind == "fox" and ti >= 1:
                        hk0 = hk
                        hk = lambda hk0=hk0, ti=ti: ((hk0() if hk0 is not None else None), fox_bias(ti - 1))
                    attn_tile(hh, 512, qh[qi][:, :], b_qh[qi], steps, bias_of, kind, ob, hook=hk)
                    pend[0] = (lambda l=l, hh=hh, ti=ti, qi=qi, kind=kind, ob=ob: epilogue(
                        l, hh, 512, ti * 512, ti, gh[qi][:, :], b_gh[qi], kind, ob))
                    if hh + 1 < NH and ti + 1 <= NT - 1:
                        load_kv(hh + 1, ti + 1)
                if cfg.sample:
                    sample_attn(l, hh, kind, qi)
            if pend[0] is not None:
                pend[0]()
                pend[0] = None

        def sample_tables(l):
            S.op("dve", lambda e: e.memset(lnps[:, :, :], 0.0), writes=[b_lnps])
            S.dma("sync", lnps[0:8, :, :], clf[l, :, :, :], reads=[], writes=[b_lnps])
            S.dma("sync", lnps[8:9, :, 0:DEC_T], slfs[:, :].rearrange("(o h) p -> o h p", o=1), reads=[b_slfs],
                  writes=[b_lnps])
            cum_core(lnps, b_lnps, 9, ncbs, b_ncbs, 9)

        def sample_attn(l, hh, kind, qi):
            c0 = NTOK
            QW = DEC_T
            S.dma("pool", kts[:, 0:PAST], ckT[l, hh, :, :], writes=[b_kts])
            S.dma("sync", kts[:, PAST:PAST + DEC_T], sks[hh, :, :], reads=[b_sks[hh]], writes=[b_kts])
            S.dma("pool", vs[:, 0:8, :], cv[l, :, hh * 128:(hh + 1) * 128].rearrange("(n p) d -> p n d", p=128),
                  writes=[b_vs])
            S.dma("sync", vs[0:DEC_T, 8, :], svs[:, hh * 128:(hh + 1) * 128], reads=[b_svs[hh]], writes=[b_vs])
            qi = load_qg(hh, NT, c0, QW)
            q_ap = qh[qi][:, :QW]
            mk = smask_t[0:DEC_T, (0 if kind == "sb" else 32):(0 if kind == "sb" else 32) + 32]
            blocks = [(kts[:, PAST:PAST + DEC_T], vs[0:DEC_T, 8, :], DEC_T, mk, [b_kts, b_vs])]
            for n in range(7, -1, -1):
                blocks.append((kts[:, n * 128:(n + 1) * 128], vs[:, n, :], 128, None, [b_kts, b_vs]))
            bias_of = None
            if kind == "fox":
                h = hh - 4
                pb, bpb = pbank[5], b_pb[5]
                S.op("pe", lambda e, h=h: e.matmul(pb[:, 0:16], lhsT=sel31_f, rhs=ncbs[:, h, :], start=True, stop=True),
                     reads=[b_ncbs, b_const], writes=[bpb])
                S.op("act", lambda e: e.activation(out=xs_ref[:, 0:16], in_=pb[:, 0:16], func=AF.Identity), reads=[bpb],
                     writes=[b_xsref])
                S.op("dve", lambda e, h=h: e.tensor_scalar(out=bt[0][:, 0, 0:9], in0=ncbs[:, h, 0:9], scalar1=xs_ref[:, 8:9],
                                                           scalar2=0.0, op0=ALU.subtract, op1=ALU.min),
                     reads=[b_ncbs, b_xsref], writes=[b_bt[0]])
                nidx = [8] + list(range(7, -1, -1))
                bias_of = lambda si, sq: (bt[0][:(DEC_T if si == 0 else 128), 0, nidx[si]:nidx[si] + 1], b_bt[0])
            ob = next_ob()
            hk = pend[0]
            pend[0] = None
            attn_tile(hh, QW, q_ap, b_qh[qi], blocks, bias_of, kind, ob, hook=hk)
            pend[0] = (lambda l=l, hh=hh, qi=qi, kind=kind, ob=ob: epilogue(
                l, hh, QW, c0, NT, gh[qi][:, :QW], b_gh[qi], kind, ob))

        stop = getattr(cfg, "stop", 99)
        if stop >= 2:
            for l in range(L):
                tile_pass(l)
                if stop >= 3:
                    phase_b(l)
            if stop >= 6:
                tile_pass(L)
        S.final_waits("sync")

        with nc.Block() as block:
            @block.tensor
            def _(e):
                for f in S.q["pe"]:
                    f(e)

            @block.scalar
            def _(e):
                for f in S.q["act"]:
                    f(e)

            @block.vector
            def _(e):
                for f in S.q["dve"]:
                    f(e)

            @block.gpsimd
            def _(e):
                for f in S.q["pool"]:
                    f(e)

            @block.sync
            def _(e):
                for f in S.q["sync"]:
                    f(e)
    return nc, S


def _consts():
    k = np.arange(128)
    ones = np.ones((128, 128), np.float32)
    tri = (k[:, None] >= k[None, :]).astype(np.float32)
    omt = 1.0 - tri
    cbf = np.concatenate([ones, tri, omt], axis=1).astype(ml_dtypes.bfloat16)
    ident = np.eye(128, dtype=np.float32)
    triinc = (k[:, None] <= k[None, :]).astype(np.float32)
    su = (k[:, None] < k[None, :]).astype(np.float32)
    sel = np.zeros((128, 128), np.float32)
    sel[127, :] = 1.0
    sel31 = np.zeros((128, 128), np.float32)
    sel31[31, :] = 1.0
    cf32 = np.concatenate([ident, ones, triinc, su, sel, sel31], axis=1).astype(np.float32)
    return cbf, cf32


def _masks(j):
    k = np.arange(128)
    out = np.zeros((128, 2, 16, 512), np.float32)
    tri_sb = (k[:, None] < k[None, :]).astype(np.float32)
    tri_fx = (k[:, None] <= k[None, :]).astype(np.float32)
    for kind, tri in enumerate((tri_sb, tri_fx)):
        for m in range(16):
            d = m - 4 * j
            for s in range(4):
                if d < s:
                    out[:, kind, m, s * 128:(s + 1) * 128] = 1.0
                elif d == s:
                    out[:, kind, m, s * 128:(s + 1) * 128] = tri
    return out.reshape(128, 2 * 16 * 512).astype(ml_dtypes.bfloat16)


def _smask():
    k = np.arange(32)
    out = np.zeros((128, 64), np.float32)
    out[:32, 0:32] = (k[:, None] < k[None, :])
    out[:32, 32:64] = (k[:, None] <= k[None, :])
    return out.astype(ml_dtypes.bfloat16)


_PROG_CACHE = {}


def run(cfg, inputs):
    NT, L, NTOK, NTOT = cfg.NT, cfg.DEPTH, cfg.NTOK, cfg.NTOT
    f32 = np.float32
    g = {k: np.asarray(v) for k, v in inputs.items()}
    key = (NT, L, cfg.sample)
    if key not in _PROG_CACHE:
        _PROG_CACHE[key] = build_program(cfg)
    nc, S = _PROG_CACHE[key]

    w_in = g["w_in"].astype(f32, copy=False)
    w_in_r = w_in[:, :, :4096].reshape(L, KC, 128, 32, 128).transpose(0, 3, 2, 1, 4).reshape(L * 32 * 128 * 2, 1024)
    w_f_r = w_in[:, :, 4096:4100].reshape(L, KC, 128, 4).transpose(0, 2, 1, 3).reshape(L * 128, 64)
    w_out_r = g["w_out"].reshape(L, 8, 128, 16, 128).transpose(0, 3, 2, 1, 4).reshape(L * 16 * 128, 1024)
    w_g_r = g["w_ple_gate"].reshape(L, KC, 128, 16, 128).transpose(0, 3, 2, 1, 4).reshape(L * 16 * 128 * 2, 1024)
    w_ple_r = g["w_ple"].reshape(L, 2, 128, 16, 128).transpose(0, 3, 2, 1, 4).reshape(L * 16 * 128, 256)
    w_in_r, w_f_r, w_out_r, w_g_r, w_ple_r = (np.ascontiguousarray(a, dtype=f32) for a in
                                              (w_in_r, w_f_r, w_out_r, w_g_r, w_ple_r))
    cbf, cf32 = _consts()
    smask = _smask()
    NPAR = L * 48 + 16 + 8
    in_maps = []
    for c in range(8):
        b, j = c // 4, c % 4
        tiles = [4 * i + j for i in range(NT)]
        xp = g["x_prompt"][b].reshape(4 * NT, 512, D)[tiles].reshape(NTOK, D)
        xs = g["x_sample"][c]
        xTc = np.ascontiguousarray(np.concatenate([xp, xs], axis=0).T, dtype=f32)
        pp = g["p_prompt"][:, b].reshape(L, 4 * NT, 512, PLE)[:, tiles].reshape(L, NTOK, PLE)
        pTc = np.ascontiguousarray(np.concatenate([pp, g["p_sample"][:, c]], axis=1).transpose(0, 2, 1), dtype=f32)
        par = np.zeros((128, NPAR), f32)
        for l in range(L):
            par[:, l * 48: l * 48 + 16] = g["g_attn_norm"][l].reshape(KC, 128).T
            par[:, l * 48 + 16: l * 48 + 32] = g["g_ple_norm"][l].reshape(KC, 128).T
            par[:, l * 48 + 32: l * 48 + 36] = g["g_out_sb"][l].reshape(4, 128).T
            par[:, l * 48 + 36: l * 48 + 40] = g["g_out_fox"][l].reshape(4, 128).T
            par[0:4, l * 48 + 40] = -g["b_forget"][l]
        par[:, L * 48: L * 48 + 16] = g["g_final"].reshape(KC, 128).T
        par[:, L * 48 + 16 + j] = 1.0
        m = {"xT": xTc, "pT": pTc, "w_in_r": w_in_r, "w_f_r": w_f_r, "w_out_r": w_out_r, "w_g_r": w_g_r,
             "w_ple_r": w_ple_r, "params": par, "cbf": cbf, "cf32": cf32, "masks": _masks(j), "smask": smask}
        if cfg.sample:
            ck = np.concatenate([g["cache_sb_k"][:, c], g["cache_fox_k"][:, c]], axis=2)
            m["ckT"] = np.ascontiguousarray(ck.transpose(0, 2, 3, 1), dtype=f32)
            cvv = np.concatenate([g["cache_sb_v"][:, c], g["cache_fox_v"][:, c]], axis=2)
            m["cv"] = np.ascontiguousarray(cvv.reshape(L, PAST, NH * HD), dtype=f32)
            m["clf"] = np.ascontiguousarray(g["cache_fox_logf"][:, c].reshape(L, 8, 128, 4).transpose(0, 1, 3, 2), dtype=f32)
        in_maps.append(m)

    res = run_bass_kernel_spmd(nc, in_maps, core_ids=list(range(8)), trace=getattr(cfg, "trace", False))
    R = res.results
    global LAST_R, LAST_RES
    LAST_R = R
    LAST_RES = res
    SEQ = cfg.SEQ
    y_p = np.zeros((2, SEQ, D), f32)
    y_s = np.zeros((8, DEC_T, D), f32)
    kp = np.zeros((L, 2, SEQ, NH, HD), f32)
    vp = np.zeros((L, 2, SEQ, NH, HD), f32)
    lp = np.zeros((L, 2, SEQ, 4), f32)
    ks = np.zeros((L, 8, DEC_T, NH, HD), f32)
    vsa = np.zeros((L, 8, DEC_T, NH, HD), f32)
    ls = np.zeros((L, 8, DEC_T, 4), f32)
    for c in range(8):
        b, j = c // 4, c % 4
        r = R[c]
        yt = r["yT"].T
        kt = r["kT_out"].transpose(0, 3, 1, 2)
        vt = r["v_out"].reshape(L, NTOT, NH, HD)
        lt = r["lf_out"].transpose(0, 2, 1)
        for i in range(NT):
            gt = 4 * i + j
            y_p[b, gt * 512:(gt + 1) * 512] = yt[i * 512:(i + 1) * 512]
            kp[:, b, gt * 512:(gt + 1) * 512] = kt[:, i * 512:(i + 1) * 512]
            vp[:, b, gt * 512:(gt + 1) * 512] = vt[:, i * 512:(i + 1) * 512]
            lp[:, b, gt * 512:(gt + 1) * 512] = lt[:, i * 512:(i + 1) * 512]
        y_s[c] = yt[NTOK:]
        ks[:, c] = kt[:, NTOK:]
        vsa[:, c] = vt[:, NTOK:]
        ls[:, c] = lt[:, NTOK:]
    return (y_p, y_s, kp[..., 0:4, :], kp[..., 4:8, :], vp[..., 0:4, :], vp[..., 4:8, :], lp,
            ks[..., 0:4, :], ks[..., 4:8, :], vsa[..., 0:4, :], vsa[..., 4:8, :], ls)


def kernel(**inputs):
    cfg = Cfg(NT=8, DEPTH=4, sample=True)
    outs = run(cfg, inputs)
    y_p, y_s, skp, fkp, svp, fvp, lp, sks, fks, svs, fvs, ls = outs
    c = np.ascontiguousarray
    return (c(y_p), c(y_s), c(skp), c(svp), c(fkp), c(fvp), c(lp), c(sks), c(svs), c(fks), c(fvs), c(ls))
```

```python
import contextlib
import numpy as np
import ml_dtypes
import concourse.bass as bass
import concourse.mybir as mybir
from concourse.bass_utils import run_bass_kernel_spmd

F32 = mybir.dt.float32
BF16 = mybir.dt.bfloat16
AF = mybir.ActivationFunctionType
ALU = mybir.AluOpType

D = 2048
KC = 16
HD = 128
NH = 8
PLE = 256
EPS = 1e-6
DEC_T = 32
PAST = 1024
QSCALE = HD ** -0.5


class Cfg:
    def __init__(self, NT=8, DEPTH=4, sample=True):
        self.NT = NT
        self.DEPTH = DEPTH
        self.sample = sample
        self.NTOK = NT * 512
        self.NTOT = self.NTOK + DEC_T
        self.NB = NT * 16
        self.SEQ = NT * 4 * 512


class Buf:
    __slots__ = ("name", "w", "r", "excl")

    def __init__(self, name, excl=False):
        self.name = name
        self.w = None
        self.r = {}
        self.excl = excl


ENGS = ("pe", "act", "dve", "pool", "sync")


def _flat(x):
    out = []
    for b in x:
        if isinstance(b, (list, tuple)):
            out.extend(_flat(b))
        else:
            out.append(b)
    return out


class Sched:
    def __init__(self, nc, sems, dsems):
        self.nc = nc
        self.sem = sems
        self.q = {e: [] for e in ENGS}
        self.cnt = {e: 0 for e in ENGS}
        self.waited = {e: {} for e in ENGS}
        self.dsem = dsems
        self.duse = {k: [0] * len(v) for k, v in dsems.items()}
        self.dnext = {k: 0 for k in dsems}
        self.n_ops = 0
        self.n_waits = 0

    def _semobj(self, key):
        if isinstance(key, tuple):
            return self.dsem[key[0]][key[1]]
        return self.sem[key]

    def _collect(self, reads, writes):
        need = {}

        def add(tok):
            if tok is None:
                return
            k, v = tok
            if need.get(k, 0) < v:
                need[k] = v

        for b in reads:
            add(b.w)
        for b in writes:
            add(b.w)
            for k, v in b.r.items():
                add((k, v))
        return need

    def _emit_waits(self, eng, need):
        wl = []
        wd = self.waited[eng]
        for k, v in need.items():
            if eng == "pe" and k == "pe":
                continue
            if wd.get(k, 0) >= v:
                continue
            wd[k] = v
            wl.append((self._semobj(k), v))
        return wl

    def _commit(self, tok, reads, writes):
        k, v = tok
        for b in reads:
            if b.r.get(k, 0) < v:
                b.r[k] = v
        for b in writes:
            b.w = tok
            b.r = {}

    def op(self, eng, fn, reads=(), writes=()):
        reads, writes = _flat(reads), _flat(writes)
        ex = [b for b in reads if b.excl]
        if ex:
            writes = list(writes) + ex
        need = self._collect(reads, writes)
        wl = self._emit_waits(eng, need)
        self.cnt[eng] += 1
        tok = (eng, self.cnt[eng])
        sem = self.sem[eng]
        self.n_ops += 1
        self.n_waits += len(wl)

        def run(e, wl=wl, fn=fn, sem=sem):
            for s, v in wl:
                e.wait_ge(s, v)
            fn(e).then_inc(sem, 1)

        self.q[eng].append(run)
        self._commit(tok, reads, writes)

    def dma(self, queue, out, in_, reads=(), writes=(), **kw):
        eng = "pool" if queue == "cast" else queue
        reads, writes = _flat(reads), _flat(writes)
        pool = self.dsem[queue]
        i = self.dnext[queue]
        self.dnext[queue] = (i + 1) % len(pool)
        self.duse[queue][i] += 1
        use = self.duse[queue][i]
        key = (queue, i)
        need = self._collect(reads, writes)
        if use > 1:
            if need.get(key, 0) < 16 * (use - 1):
                need[key] = 16 * (use - 1)
        wl = self._emit_waits(eng, need)
        sem = pool[i]
        self.n_ops += 1
        self.n_waits += len(wl)

        def run(e, wl=wl, sem=sem, out=out, in_=in_, kw=kw):
            for s, v in wl:
                e.wait_ge(s, v)
            e.dma_start(out=out, in_=in_, **kw).then_inc(sem, 16)

        self.q[eng].append(run)
        self._commit((key, 16 * use), reads, writes)

    def custom(self, eng, fn, semobj_key, val, reads=(), writes=()):
        reads, writes = _flat(reads), _flat(writes)
        need = self._collect(reads, writes)
        wl = self._emit_waits(eng, need)

        def run(e, wl=wl, fn=fn):
            for s, v in wl:
                e.wait_ge(s, v)
            fn(e)

        self.q[eng].append(run)
        self._commit((semobj_key, val), reads, writes)

    def final_waits(self, eng):
        wl = []
        for qn, pool in self.dsem.items():
            for i, s in enumerate(pool):
                if self.duse[qn][i] > 0:
                    wl.append((s, 16 * self.duse[qn][i]))

        def run(e, wl=wl):
            for s, v in wl:
                e.wait_ge(s, v)

        self.q[eng].append(run)


def build_program(cfg):
    NT, L, NTOK, NTOT, NB = cfg.NT, cfg.DEPTH, cfg.NTOK, cfg.NTOT, cfg.NB
    nc = bass.Bass("TRN2", target_bir_lowering=False)

    def din(name, shape, dt=F32):
        return nc.dram_tensor(name, list(shape), dt, kind="ExternalInput")

    def dout(name, shape, dt=F32):
        return nc.dram_tensor(name, list(shape), dt, kind="ExternalOutput")

    def dint(name, shape, dt):
        return nc.dram_tensor(name, list(shape), dt)

    xT = din("xT", [D, NTOT])
    pT = din("pT", [L, PLE, NTOT])
    w_in_r = din("w_in_r", [L * 32 * 128 * 2, 1024])
    w_f_r = din("w_f_r", [L * 128, 64])
    w_out_r = din("w_out_r", [L * 16 * 128, 1024])
    w_g_r = din("w_g_r", [L * 16 * 128 * 2, 1024])
    w_ple_r = din("w_ple_r", [L * 16 * 128, 256])
    NPAR = L * 48 + 16 + 8
    params = din("params", [128, NPAR])
    cbf = din("cbf", [128, 384], BF16)
    cf32 = din("cf32", [128, 6 * 128])
    masks = din("masks", [128, 2 * 16 * 512], BF16)
    smask = din("smask", [128, 2 * 32], BF16)
    if cfg.sample:
        ckT = din("ckT", [L, NH, HD, PAST])
        cv = din("cv", [L, PAST, NH * HD])
        clf = din("clf", [L, 8, 4, 128])

    yT = dout("yT", [D, NTOT])
    kT_out = dout("kT_out", [L, NH, HD, NTOT])
    v_out = dout("v_out", [L, NTOT, NH * HD])
    lf_out = dout("lf_out", [L, 4, NTOT])

    wb_in = dint("wb_in", [L * 32 * 128 * 2, 1024], BF16)
    wb_f = dint("wb_f", [L * 128, 64], BF16)
    wb_out = dint("wb_out", [L * 16 * 128, 1024], BF16)
    wb_g = dint("wb_g", [L * 16 * 128 * 2, 1024], BF16)
    wb_ple = dint("wb_ple", [L * 16 * 128, 256], BF16)
    xS = dint("xS", [D, NTOT], F32)
    qS = dint("qS", [NH, HD, NTOT], BF16)
    gS = dint("gS", [NH, HD, NTOT], BF16)
    ogS = dout("ogS", [NH, HD, NTOT], BF16) if getattr(cfg, "debug", False) else dint("ogS", [NH, HD, NTOT], BF16)
    skK = [[dint(f"skK{l}_{t}", [1024, 512], BF16) for t in range(NT)] for l in range(L)]
    skV = [[dint(f"skV{l}_{t}", [512, 1024], BF16) for t in range(NT)] for l in range(L)]
    gK = [[dint(f"gK{l}_{t}", [4 * 1024, 512], BF16) for t in range(NT)] for l in range(L)]
    gV = [[dint(f"gV{l}_{t}", [4 * 512, 1024], BF16) for t in range(NT)] for l in range(L)]
    slf = [dint(f"slf{l}", [4, NTOK], F32) for l in range(L)]
    glf = [dint(f"glf{l}", [16, NTOK], F32) for l in range(L)]
    sks = dint("sks", [NH, HD, DEC_T], BF16)
    svs = dint("svs", [DEC_T, NH * HD], BF16)
    slfs = dint("slfs", [4, DEC_T], F32)

    es = contextlib.ExitStack()
    with es:
        def sb(name, shape, dt):
            return es.enter_context(nc.sbuf_tensor(name, list(shape), dt))

        def ps(name):
            return es.enter_context(nc.psum_tensor(name, [128, 512], F32))

        sems = {e: es.enter_context(nc.semaphore("s_" + e)) for e in ("pe", "act", "dve", "pool")}
        dsems = {
            "sync": [es.enter_context(nc.semaphore(f"ds{i}")) for i in range(24)],
            "pool": [es.enter_context(nc.semaphore(f"dp{i}")) for i in range(12)],
            "cast": [es.enter_context(nc.semaphore(f"dc{i}")) for i in range(3)],
        }
        cc_sem = es.enter_context(nc.semaphore("cc"))
        sems["cc"] = cc_sem
        S = Sched(nc, sems, dsems)

        BIG = sb("BIG", [128, 32768], BF16)
        KT = BIG[:, 0:16384]
        VV = BIG[:, 16384:32768]
        xt = BIG[:, 0:16384].bitcast(F32).rearrange("p (k t) -> p k t", t=512)
        hT = BIG[:, 16384:24576].rearrange("p (k t) -> p k t", t=512)
        B_KT = [Buf(f"KT{g}") for g in range(8)]
        B_VVg = [Buf(f"VV{g}") for g in range(8)]
        B_VV0, B_VV1 = B_VVg[0:4], B_VVg[4:8]
        NWS = 6
        wring = [sb(f"w{i}", [128, 2048], BF16) for i in range(NWS)]
        b_wring = [Buf(f"w{i}") for i in range(NWS)]
        ogt = sb("ogt", [128, 8, 512], BF16)
        b_ogt = Buf("ogt")
        ptb = sb("ptb", [128, 2, 512], BF16)
        b_ptb = Buf("ptb")
        mask_t = sb("mask_t", [128, 2 * 16 * 512], BF16)
        smask_t = sb("smask_t", [128, 64], BF16)
        cb = sb("cb", [128, 384], BF16)
        cf = sb("cf", [128, 768], F32)
        par = sb("par", [128, NPAR], F32)
        b_const = Buf("const")
        ones_b, tri_b, omt_b = cb[:, 0:128], cb[:, 128:256], cb[:, 256:384]
        ident_f, ones_f, triinc_f, su_f, sel127_f, sel31_f = (cf[:, i * 128:(i + 1) * 128] for i in range(6))
        NST = 3
        stf = [sb(f"stf{i}", [128, 512], F32) for i in range(NST)]
        b_stf = [Buf(f"stf{i}") for i in range(NST)]
        stb = [sb(f"stb{i}", [128, 512], BF16) for i in range(NST)]
        b_stb = [Buf(f"stb{i}") for i in range(NST)]
        sqb = [sb(f"sqb{i}", [128, 512], BF16) for i in range(2)]
        b_sqb = [Buf(f"sqb{i}") for i in range(2)]
        rstd = sb("rstd", [128, 512], F32)
        b_rstd = Buf("rstd")
        tmpf = [sb(f"tmpf{i}", [128, 512], F32) for i in range(2)]
        b_tmpf = [Buf(f"tmpf{i}") for i in range(2)]
        qh = [sb(f"qh{i}", [128, 512], BF16) for i in range(2)]
        b_qh = [Buf(f"qh{i}") for i in range(2)]
        gh = [sb(f"gh{i}", [128, 512], BF16) for i in range(2)]
        b_gh = [Buf(f"gh{i}") for i in range(2)]
        qg_ctr = [0]

        def load_qg(hh, ti, c0, QW):
            i = qg_ctr[0] % 2
            qg_ctr[0] += 1
            S.dma("sync", qh[i][:, :QW], qS[hh, :, c0:c0 + QW], reads=[b_qS[hh][ti]], writes=[b_qh[i]])
            S.dma("sync", gh[i][:, :QW], gS[hh, :, c0:c0 + QW], reads=[b_gS[hh][ti]], writes=[b_gh[i]])
            return i
        NE = 4
        e_t = [sb(f"e{i}", [128, 512], F32) for i in range(NE)]
        b_e = [Buf(f"e{i}") for i in range(NE)]
        sp_t = [sb(f"sp{i}", [128, 512], BF16) for i in range(NE)]
        b_sp = [Buf(f"sp{i}") for i in range(NE)]
        t_t = [sb(f"t{i}", [128, 512], F32) for i in range(NE)]
        b_t = [Buf(f"t{i}") for i in range(NE)]
        NW = 3
        w_t = [sb(f"wt{i}", [128, 512], BF16) for i in range(NW)]
        b_w = [Buf(f"wt{i}") for i in range(NW)]
        ncb = sb("ncb", [128, 4, 128], F32)
        b_ncb = Buf("ncb")
        lnp = sb("lnp", [128, 4, 128], F32)
        b_lnp = Buf("lnp")
        ltb = sb("ltb", [128, 128], F32)
        b_ltb = Buf("ltb")
        tott = sb("tott", [128, 128], F32)
        b_tott = Buf("tott")
        xn = sb("xn", [128, 128], F32)
        b_xn = Buf("xn")
        ncref = sb("ncref", [128, 128], F32)
        b_ncref = Buf("ncref")
        bt = [sb(f"bt{i}", [128, 4, 128], F32) for i in range(2)]
        b_bt = [Buf(f"bt{i}") for i in range(2)]
        lfst = sb("lfst", [4, 512], F32)
        b_lfst = Buf("lfst")
        kts = sb("kts", [128, PAST + DEC_T], BF16)
        b_kts = Buf("kts")
        vs = sb("vs", [128, 9, 128], BF16)
        b_vs = Buf("vs")
        lnps = sb("lnps", [128, 4, 128], F32)
        b_lnps = Buf("lnps")
        ncbs = sb("ncbs", [128, 4, 16], F32)
        b_ncbs = Buf("ncbs")
        xs_ref = sb("xs_ref", [128, 16], F32)
        b_xsref = Buf("xs_ref")

        pbank = [ps(f"pb{i}") for i in range(8)]
        b_pb = [Buf(f"pb{i}", excl=True) for i in range(8)]

        b_xS = [Buf(f"xS{t}") for t in range(NT + 1)]
        b_qS = [[Buf(f"qS{h}_{t}") for t in range(NT + 1)] for h in range(NH)]
        b_gS = [[Buf(f"gS{h}_{t}") for t in range(NT + 1)] for h in range(NH)]
        b_ogS = [[Buf(f"ogS{h}_{t}") for t in range(NT + 1)] for h in range(NH)]
        b_skK = [[[] for _ in range(NT)] for _ in range(L)]
        b_skV = [[[] for _ in range(NT)] for _ in range(L)]
        b_gK = [[Buf(f"gK{l}_{t}") for t in range(NT)] for l in range(L)]
        b_gV = [[Buf(f"gV{l}_{t}") for t in range(NT)] for l in range(L)]
        b_slf = [[] for _ in range(L)]
        b_glf = [Buf(f"glf{l}") for l in range(L)]
        b_wb = {}
        b_sks, b_svs, b_slfs = [Buf(f"sks{h}") for h in range(NH)], [Buf(f"svs{h}") for h in range(NH)], Buf("slfs")

        def pcol(l, what, k=None):
            base = l * 48
            if what == "g_attn":
                return par[:, base + k: base + k + 1]
            if what == "g_ple":
                return par[:, base + 16 + k: base + 16 + k + 1]
            if what == "g_o":
                return par[:, base + 32 + k: base + 32 + k + 1]
            if what == "nbf":
                return par[0:4, base + 40: base + 41]
            raise KeyError(what)

        def gfin(k):
            return par[:, L * 48 + k: L * 48 + k + 1]

        onehot = par[:, L * 48 + 16: L * 48 + 20]

        S.dma("sync", cb[:], cbf[:, :], writes=[b_const])
        S.dma("sync", cf[:], cf32[:, :], writes=[b_const])
        S.dma("sync", par[:], params[:, :], writes=[b_const])
        S.dma("sync", mask_t[:], masks[:, :], writes=[b_const])
        S.dma("sync", smask_t[:], smask[:, :], writes=[b_const])

        def cast_rows(dst, src, r0, r1, key, step=2048):
            bl = []
            for a in range(r0, r1, step):
                b = min(a + step, r1)
                bb = Buf(f"wb_{key}_{a}")
                S.dma("cast", dst[a:b, :], src[a:b, :], writes=[bb])
                bl.append(bb)
            return bl

        def cast_layer(l, part):
            if part == 0:
                b_wb[("in", l)] = cast_rows(wb_in, w_in_r, l * 8192, l * 8192 + 4096, f"in{l}")
                b_wb[("f", l)] = cast_rows(wb_f, w_f_r, l * 128, (l + 1) * 128, f"f{l}")
            elif part == 1:
                b_wb[("in", l)] += cast_rows(wb_in, w_in_r, l * 8192 + 4096, (l + 1) * 8192, f"in{l}b")
            elif part == 2:
                b_wb[("out", l)] = cast_rows(wb_out, w_out_r, l * 2048, (l + 1) * 2048, f"out{l}")
                b_wb[("ple", l)] = cast_rows(wb_ple, w_ple_r, l * 2048, (l + 1) * 2048, f"ple{l}")
            elif part == 3:
                b_wb[("g", l)] = cast_rows(wb_g, w_g_r, l * 4096, (l + 1) * 4096, f"g{l}")

        for part in range(4):
            cast_layer(0, part)

        wctr = [0]

        def load_w(kind, l, c):
            i = wctr[0] % NWS
            wctr[0] += 1
            slot, bs = wring[i], b_wring[i]
            if kind == "in":
                r = (l * 32 + c) * 256
                S.dma("sync", slot[:, :], wb_in[r:r + 256, :].rearrange("(p t) x -> p (t x)", t=2),
                      reads=b_wb[("in", l)], writes=[bs])
                return slot[:, :].rearrange("p (k n) -> p k n", n=128), bs
            if kind == "g":
                r = (l * 16 + c) * 256
                S.dma("sync", slot[:, :], wb_g[r:r + 256, :].rearrange("(p t) x -> p (t x)", t=2),
                      reads=b_wb[("g", l)], writes=[bs])
                return slot[:, :].rearrange("p (k n) -> p k n", n=128), bs
            if kind == "out":
                r = (l * 16 + c) * 128
                S.dma("sync", slot[:, 0:1024], wb_out[r:r + 128, :], reads=b_wb[("out", l)], writes=[bs])
                return slot[:, 0:1024].rearrange("p (k n) -> p k n", n=128), bs
            if kind == "ple":
                r = (l * 16 + c) * 128
                S.dma("sync", slot[:, 0:256], wb_ple[r:r + 128, :], reads=b_wb[("ple", l)], writes=[bs])
                return slot[:, 0:256].rearrange("p (k n) -> p k n", n=128), bs
            if kind == "f":
                S.dma("sync", slot[:, 0:64], wb_f[l * 128:(l + 1) * 128, :], reads=b_wb[("f", l)], writes=[bs])
                return slot[:, 0:64].rearrange("p (k n) -> p k n", n=4), bs
            raise KeyError(kind)

        pa_ctr = [0]

        def pa_bank():
            i = pa_ctr[0] % 3
            pa_ctr[0] += 1
            return pbank[i], b_pb[i]

        st_ctr = [0]

        def stage():
            i = st_ctr[0] % NST
            st_ctr[0] += 1
            return stf[i], b_stf[i], stb[i], b_stb[i]

        def rms_stats(TW):
            acc, bacc = pbank[5], b_pb[5]
            for k in range(KC):
                i = k % 2
                S.op("act", lambda e, k=k, i=i: e.activation(out=sqb[i][:, :TW], in_=xt[:, k, :TW], func=AF.Square),
                     reads=[B_KT], writes=[b_sqb[i]])
                S.op("pe", lambda e, k=k, i=i: e.matmul(acc[:, :TW], lhsT=ones_b, rhs=sqb[i][:, :TW],
                                                        start=(k == 0), stop=(k == KC - 1)),
                     reads=[b_sqb[i], b_const], writes=[bacc])
            S.op("act", lambda e: e.activation(out=rstd[:, :TW], in_=acc[:, :TW], func=AF.Sqrt, bias=EPS_t[:, 0:1],
                                               scale=1.0 / D),
                 reads=[bacc, b_const], writes=[b_rstd])
            S.op("dve", lambda e: e.reciprocal(out=rstd[:, :TW], in_=rstd[:, :TW]), reads=[b_rstd], writes=[b_rstd])

        def make_h(TW, gname, l):
            for k in range(KC):
                g = pcol(l, gname, k) if gname != "fin" else gfin(k)
                S.op("dve", lambda e, k=k, g=g: e.scalar_tensor_tensor(
                    out=hT[:, k, :TW], in0=xt[:, k, :TW], scalar=g, in1=rstd[:, :TW], op0=ALU.mult, op1=ALU.mult),
                     reads=[B_KT, b_rstd, b_const], writes=[B_VV0])

        S.op("dve", lambda e: e.memset(xn[:, :], 0.0), writes=[b_xn])
        EPS_t = sb("eps_t", [128, 2], F32)
        S.op("dve", lambda e: e.memset(EPS_t[:, 0:1], EPS), writes=[b_const])
        S.op("dve", lambda e: e.memset(EPS_t[:, 1:2], 1.0), writes=[b_const])

        def phase_c(l, ti, TW, c0):
            S.dma("sync", ogt[:, :, :TW], ogS[:, :, c0:c0 + TW].rearrange("h p t -> p h t"),
                  reads=[b_ogS[h][ti] for h in range(NH)], writes=[b_ogt])
            S.dma("pool", ptb[:, :, :TW], pT[l, :, c0:c0 + TW].rearrange("(k p) t -> p k t", p=128),
                  writes=[b_ptb])
            for c in range(KC):
                wv, bw = load_w("out", l, c)
                pb, bpb = pa_bank()
                for k in range(8):
                    S.op("pe", lambda e, k=k, wv=wv, pb=pb: e.matmul(pb[:, :TW], lhsT=wv[:, k, :], rhs=ogt[:, k, :TW],
                                                                      start=(k == 0), stop=(k == 7)),
                         reads=[bw, b_ogt], writes=[bpb])
                S.op("dve", lambda e, c=c, pb=pb: e.tensor_tensor(out=xt[:, c, :TW], in0=pb[:, :TW], in1=xt[:, c, :TW],
                                                                  op=ALU.add),
                     reads=[bpb, B_KT], writes=[B_KT])
            rms_stats(TW)
            make_h(TW, "g_ple", l)
            for c in range(KC):
                wv, bw = load_w("g", l, c)
                pb, bpb = pa_bank()
                for k in range(KC):
                    S.op("pe", lambda e, k=k, wv=wv, pb=pb: e.matmul(pb[:, :TW], lhsT=wv[:, k, :], rhs=hT[:, k, :TW],
                                                                      start=(k == 0), stop=(k == KC - 1)),
                         reads=[bw, B_VV0], writes=[bpb])
                i = c % 2
                S.op("act", lambda e, pb=pb, i=i: e.activation(out=tmpf[i][:, :TW], in_=pb[:, :TW], func=AF.Sigmoid),
                     reads=[bpb], writes=[b_tmpf[i]])
                wv2, bw2 = load_w("ple", l, c)
                pb2, bpb2 = pa_bank()
                for k in range(2):
                    S.op("pe", lambda e, k=k, wv2=wv2, pb2=pb2: e.matmul(pb2[:, :TW], lhsT=wv2[:, k, :],
                                                                          rhs=ptb[:, k, :TW], start=(k == 0), stop=(k == 1)),
                         reads=[bw2, b_ptb], writes=[bpb2])
                S.op("dve", lambda e, pb2=pb2, i=i: e.tensor_tensor(out=tmpf[i][:, :TW], in0=pb2[:, :TW],
                                                                    in1=tmpf[i][:, :TW], op=ALU.mult),
                     reads=[bpb2, b_tmpf[i]], writes=[b_tmpf[i]])
                S.op("dve", lambda e, c=c, i=i: e.tensor_tensor(out=xt[:, c, :TW], in0=xt[:, c, :TW],
                                                                in1=tmpf[i][:, :TW], op=ALU.add),
                     reads=[b_tmpf[i], B_KT], writes=[B_KT])

        def phase_a(l, ti, TW, c0, after_h=None):
            is_s = ti == NT
            AP_ = getattr(cfg, "aparts", 255)
            rms_stats(TW)
            make_h(TW, "g_attn", l)
            if after_h is not None:
                after_h()
            if not AP_ & 1:
                return
            nsub = max(TW // 128, 1)
            SW = min(TW, 128)
            pdma = S.dma if AP_ & 8 else (lambda *a, **k: None)
            kinds_ok = getattr(cfg, 'kinds', 'qkvz')
            for c in range(32):
                grp, hh4 = c // 4, c % 4
                kind = ("q", "k", "v", "z")[grp % 4]
                if kind not in kinds_ok:
                    continue
                hh = hh4 + (4 if grp >= 4 else 0)
                wv, bw = load_w("in", l, c)
                pb, bpb = pa_bank()
                sf, bsf, sbb, bsb = stage()
                if kind != "v":
                    for k in range(KC):
                        S.op("pe", lambda e, k=k, wv=wv, pb=pb: e.matmul(pb[:, :TW], lhsT=wv[:, k, :], rhs=hT[:, k, :TW],
                                                                          start=(k == 0), stop=(k == KC - 1)),
                             reads=[bw, B_VV0], writes=[bpb])
                else:
                    for s in range(nsub):
                        for k in range(KC):
                            S.op("pe", lambda e, k=k, s=s, wv=wv, pb=pb: e.matmul(
                                pb[:SW, s * 128:(s + 1) * 128], lhsT=hT[:, k, s * 128:s * 128 + SW], rhs=wv[:, k, :],
                                start=(k == 0), stop=(k == KC - 1)),
                                 reads=[bw, B_VV0], writes=[bpb])
                if kind == "q":
                    S.op("act", lambda e, pb=pb, sbb=sbb: e.activation(out=sbb[:, :TW], in_=pb[:, :TW], func=AF.Identity,
                                                                         scale=QSCALE),
                         reads=[bpb], writes=[bsb])
                    pdma("pool", qS[hh, :, c0:c0 + TW], sbb[:, :TW], reads=[bsb], writes=[b_qS[hh][ti]])
                elif kind == "z":
                    S.op("act", lambda e, pb=pb, sbb=sbb: e.activation(out=sbb[:, :TW], in_=pb[:, :TW], func=AF.Silu),
                         reads=[bpb], writes=[bsb])
                    pdma("pool", gS[hh, :, c0:c0 + TW], sbb[:, :TW], reads=[bsb], writes=[b_gS[hh][ti]])
                elif kind == "k":
                    S.op("act", lambda e, pb=pb, sf=sf: e.activation(out=sf[:, :TW], in_=pb[:, :TW], func=AF.Identity),
                         reads=[bpb], writes=[bsf])
                    S.op("dve", lambda e, pb=pb, sbb=sbb: e.tensor_copy(out=sbb[:, :TW], in_=pb[:, :TW]),
                         reads=[bpb], writes=[bsb])
                    pdma("pool", kT_out[l, hh, :, c0:c0 + TW], sf[:, :TW], reads=[bsf])
                    if not is_s:
                        bb = Buf("skvp")
                        b_skK[l][ti].append(bb)
                        pdma("pool", skK[l][ti][hh * 128:(hh + 1) * 128, :], sbb[:, :TW], reads=[bsb], writes=[bb])
                    else:
                        pdma("pool", sks[hh, :, :], sbb[:, :TW], reads=[bsb], writes=[b_sks[hh]])
                else:
                    W4 = nsub * 128
                    S.op("act", lambda e, pb=pb, sf=sf: e.activation(out=sf[:SW, :W4], in_=pb[:SW, :W4], func=AF.Identity),
                         reads=[bpb], writes=[bsf])
                    S.op("dve", lambda e, pb=pb, sbb=sbb: e.tensor_copy(out=sbb[:SW, :W4], in_=pb[:SW, :W4]),
                         reads=[bpb], writes=[bsb])
                    pdma("pool", v_out[l, c0:c0 + TW, hh * 128:(hh + 1) * 128].rearrange("(s t) d -> t s d", t=SW),
                          sf[:SW, :W4].rearrange("t (s d) -> t s d", d=128), reads=[bsf])
                    if not is_s:
                        bb = Buf("skvp")
                        b_skV[l][ti].append(bb)
                        pdma("pool", skV[l][ti][:, hh * 128:(hh + 1) * 128].rearrange("(s t) d -> t s d", t=SW),
                              sbb[:SW, :W4].rearrange("t (s d) -> t s d", d=128), reads=[bsb], writes=[bb])
                    else:
                        pdma("pool", svs[:, hh * 128:(hh + 1) * 128], sbb[:SW, :128], reads=[bsb], writes=[b_svs[hh]])
            if not AP_ & 2:
                return
            wv, bw = load_w("f", l, 0)
            pb, bpb = pa_bank()
            for k in range(KC):
                S.op("pe", lambda e, k=k, wv=wv, pb=pb: e.matmul(pb[0:4, :TW], lhsT=wv[:, k, :], rhs=hT[:, k, :TW],
                                                                  start=(k == 0), stop=(k == KC - 1)),
                     reads=[bw, B_VV0], writes=[bpb])
            S.op("act", lambda e, pb=pb: e.activation(out=lfst[:, :TW], in_=pb[0:4, :TW], func=AF.Exp,
                                                      bias=pcol(l, "nbf"), scale=-1.0),
                 reads=[bpb, b_const], writes=[b_lfst])
            S.op("act", lambda e: e.activation(out=lfst[:, :TW], in_=lfst[:, :TW], func=AF.Ln, bias=EPS_t[0:4, 1:2]),
                 reads=[b_lfst, b_const], writes=[b_lfst])
            S.op("dve", lambda e: e.tensor_scalar(out=lfst[:, :TW], in0=lfst[:, :TW], scalar1=-1.0, scalar2=None,
                                                  op0=ALU.mult),
                 reads=[b_lfst], writes=[b_lfst])
            S.dma("pool", lf_out[l, :, c0:c0 + TW], lfst[:, :TW], reads=[b_lfst])
            if not is_s:
                bb = Buf("slfp")
                b_slf[l].append(bb)
                S.dma("pool", slf[l][:, c0:c0 + TW], lfst[:, :TW], reads=[b_lfst], writes=[bb])
            else:
                S.dma("pool", slfs[:, :], lfst[:, :TW], reads=[b_lfst], writes=[b_slfs])
            if not AP_ & 4:
                return
            if not is_s:
                all_gather(skK[l][ti], gK[l][ti], b_skK[l][ti], b_gK[l][ti])
                all_gather(skV[l][ti], gV[l][ti], b_skV[l][ti], b_gV[l][ti])
                if ti == NT - 1:
                    all_gather(slf[l], glf[l], b_slf[l], b_glf[l])

        def final_norm(ti, TW, c0):
            rms_stats(TW)
            for k in range(KC):
                i = k % 2
                S.op("dve", lambda e, k=k, i=i: e.scalar_tensor_tensor(
                    out=tmpf[i][:, :TW], in0=xt[:, k, :TW], scalar=gfin(k), in1=rstd[:, :TW], op0=ALU.mult, op1=ALU.mult),
                     reads=[B_KT, b_rstd, b_const], writes=[b_tmpf[i]])
                S.dma("pool", yT[k * 128:(k + 1) * 128, c0:c0 + TW], tmpf[i][:, :TW], reads=[b_tmpf[i]])

        def tile_pass(l):
            tiles = list(range(NT)) + ([NT] if cfg.sample else [])

            def load_x(ti, queue):
                TW = 512 if ti < NT else DEC_T
                c0 = ti * 512
                src = xT if l == 0 else xS
                S.dma(queue, xt[:, :, :TW], src[:, c0:c0 + TW].rearrange("(k p) t -> p k t", p=128),
                      reads=([b_xS[ti]] if l > 0 else []), writes=[B_KT])

            for idx, ti in enumerate(tiles):
                TW = 512 if ti < NT else DEC_T
                c0 = ti * 512
                if idx == 0 or l == L:
                    load_x(ti, "sync")
                if l > 0:
                    phase_c(l - 1, ti, TW, c0)
                if l < L:
                    S.dma("pool", xS[:, c0:c0 + TW].rearrange("(k p) t -> p k t", p=128), xt[:, :, :TW],
                          reads=[B_KT], writes=[b_xS[ti]])
                    nxt = tiles[idx + 1] if idx + 1 < len(tiles) else None
                    phase_a(l, ti, TW, c0,
                            after_h=(lambda nxt=nxt: load_x(nxt, "pool")) if nxt is not None else None)
                else:
                    final_norm(ti, TW, c0)

        def all_gather(src, dst, reads, bdst):
            S.cc_n += 1
            n = S.cc_n
            S.custom("pool", lambda e, src=src, dst=dst: e.collective_compute(
                "AllGather", ALU.bypass, replica_groups=[[0, 1, 2, 3], [4, 5, 6, 7]],
                ins=[src.ap().opt()], outs=[dst.ap().opt()]).then_inc(cc_sem, 1),
                     "cc", n, reads=reads, writes=[bdst])

        S.cc_n = 0

        def cumsum_tables(l):
            for ip in range(NT):
                for jp in range(4):
                    S.dma("sync", lnp[16 * ip + 4 * jp:16 * ip + 4 * jp + 4, :, :],
                          glf[l][jp * 4:(jp + 1) * 4, 512 * ip:512 * ip + 512].rearrange("h (s p) -> s h p", p=128),
                          reads=[b_glf[l]], writes=[b_lnp])
            cum_core(lnp, b_lnp, NB, ncb, b_ncb, NB)
            for h in range(4):
                for jp in range(4):
                    src = ncb[:, h, :NB].rearrange("p (i m) -> p i m", m=16)[:, :, 4 * jp:4 * jp + 4]
                    dst = xn[:, h * 32:h * 32 + NT * 4].rearrange("p (i s) -> p i s", s=4)
                    if jp == 0:
                        S.op("dve", lambda e, src=src, dst=dst: e.tensor_scalar(
                            out=dst, in0=src, scalar1=onehot[:, 0:1], scalar2=None, op0=ALU.mult),
                             reads=[b_ncb, b_const], writes=[b_xn])
                    else:
                        S.op("dve", lambda e, src=src, dst=dst, jp=jp: e.scalar_tensor_tensor(
                            out=dst, in0=src, scalar=onehot[:, jp:jp + 1], in1=dst, op0=ALU.mult, op1=ALU.add),
                             reads=[b_ncb, b_const, b_xn], writes=[b_xn])
            pb, bpb = pbank[6], b_pb[6]
            S.op("pe", lambda e: e.matmul(pb[:, 0:128], lhsT=sel127_f, rhs=xn[:, :], start=True, stop=True),
                 reads=[b_xn, b_const], writes=[bpb])
            S.op("act", lambda e: e.activation(out=ncref[:, :], in_=pb[:, 0:128], func=AF.Identity), reads=[bpb],
                 writes=[b_ncref])

        def cum_core(Lsrc, bL, nb, dst, bdst, nbp):
            pb, bpb = pbank[6], b_pb[6]
            pb2, bpb2 = pbank[7], b_pb[7]
            for h in range(4):
                S.op("pe", lambda e, h=h: e.transpose(pb[:, 0:nb], Lsrc[:nb, h, :], ident_f[:nb, :nb]),
                     reads=[bL, b_const], writes=[bpb])
                S.op("act", lambda e: e.activation(out=ltb[:, :nb], in_=pb[:, 0:nb], func=AF.Identity), reads=[bpb],
                     writes=[b_ltb])
                S.op("pe", lambda e: e.matmul(pb2[:nb, 0:128], lhsT=ltb[:, :nb], rhs=ones_f, start=True, stop=True),
                     reads=[b_ltb, b_const], writes=[bpb2])
                S.op("act", lambda e: e.activation(out=tott[:nb, :], in_=pb2[:nb, 0:128], func=AF.Identity), reads=[bpb2],
                     writes=[b_tott])
                S.op("pe", lambda e: e.matmul(pb[:, 128:128 + nb], lhsT=triinc_f, rhs=ltb[:, :nb], start=True, stop=False),
                     reads=[b_ltb, b_const], writes=[bpb])
                S.op("pe", lambda e: e.matmul(pb[:, 128:128 + nb], lhsT=tott[:nb, :], rhs=su_f[:nb, :nb], start=False,
                                              stop=True),
                     reads=[b_tott, b_const], writes=[bpb])
                S.op("act", lambda e, h=h: e.activation(out=dst[:, h, :nb], in_=pb[:, 128:128 + nb], func=AF.Identity,
                                                        scale=-1.0),
                     reads=[bpb], writes=[bdst])

        ectr = [0]
        wctr2 = [0]

        acc_t = [sb(f"acc{i}", [128, 512], BF16) for i in range(2)]
        b_acc = [Buf(f"acc{i}") for i in range(2)]
        rc = {"z": 0, "e": 0, "t": 0, "w": 0, "p": 0, "a": 0}

        def attn_tile(hh, QW, q_ap, bq, steps, bias_of, kind, ob=0, hook=None):
            Zb = (0, 1, 2)
            Pb = (6, 7)
            O, bO = pbank[3 + ob], b_pb[3 + ob]
            DEN, bDEN = pbank[6 + ob], b_pb[6 + ob]
            ns = len(steps)
            R = [dict() for _ in range(ns)]

            def qk(si):
                kt_ap, v_ap, KP, m_ap, rds = steps[si]
                zi = Zb[rc["z"] % 3]
                rc["z"] += 1
                R[si]["z"] = (pbank[zi], b_pb[zi])
                z = pbank[zi]
                S.op("pe", lambda e, z=z, kt_ap=kt_ap, KP=KP: e.matmul(z[:KP, :QW], lhsT=kt_ap, rhs=q_ap, start=True,
                                                                        stop=True),
                     reads=rds + [bq], writes=[b_pb[zi]])

            def pv(si):
                kt_ap, v_ap, KP, m_ap, rds = steps[si]
                wt, bw = R[si]["w"]
                first, last = si == 0, si == ns - 1
                S.op("pe", lambda e, wt=wt, v_ap=v_ap, KP=KP, first=first, last=last: e.matmul(
                    O[:, :QW], lhsT=v_ap, rhs=wt[:KP, :QW], start=first, stop=last),
                     reads=rds + [bw], writes=[bO])
                if kind == "fox":
                    S.op("pe", lambda e, wt=wt, KP=KP, first=first, last=last: e.matmul(
                        DEN[:, :QW], lhsT=ones_b[:KP, :], rhs=wt[:KP, :QW], start=first, stop=last),
                         reads=[bw, b_const], writes=[bDEN])

            def take_w(si):
                wi = rc["w"] % NW
                rc["w"] += 1
                R[si]["w"] = (w_t[wi], b_w[wi])
                return w_t[wi], b_w[wi]

            def fox_x(si):
                kt_ap, v_ap, KP, m_ap, rds = steps[si]
                z, bz = R[si]["z"]
                wt, bw = take_w(si)
                sw_ = min(QW, 256)
                for sq in range(max(QW // 256, 1)):
                    bcol, bbt = bias_of(si, sq)
                    S.op("act", lambda e, z=z, wt=wt, KP=KP, bcol=bcol, sq=sq, sw_=sw_: e.activation(
                        out=wt[:KP, sq * sw_:(sq + 1) * sw_], in_=z[:KP, sq * sw_:(sq + 1) * sw_], func=AF.Exp,
                        bias=bcol),
                         reads=[bz, bbt], writes=[bw])
                if m_ap is not None:
                    S.op("dve", lambda e, wt=wt, m_ap=m_ap, KP=KP: e.tensor_tensor(
                        out=wt[:KP, :QW], in0=wt[:KP, :QW], in1=m_ap, op=ALU.mult),
                         reads=[bw, b_const], writes=[bw])

            hook_at = min(3, ns - 1)
            if kind == "fox":
                qk(0)
                if ns > 1:
                    qk(1)
                for si in range(ns):
                    if si + 2 < ns:
                        qk(si + 2)
                    fox_x(si)
                    if si >= 1:
                        pv(si - 1)
                    if si == hook_at and hook is not None:
                        hook()
                pv(ns - 1)
                return

            def sb_el(si):
                kt_ap, v_ap, KP, m_ap, rds = steps[si]
                z, bz = R[si]["z"]
                ei = rc["e"] % NE
                rc["e"] += 1
                et, be, spt, bsp = e_t[ei], b_e[ei], sp_t[ei], b_sp[ei]
                R[si]["e"] = (et, be)
                R[si]["sp"] = (spt, bsp)
                S.op("act", lambda e, z=z, et=et, KP=KP: e.activation(out=et[:KP, :QW], in_=z[:KP, :QW], func=AF.Exp),
                     reads=[bz], writes=[be])
                if m_ap is not None:
                    S.op("dve", lambda e, et=et, m_ap=m_ap, KP=KP: e.tensor_tensor(
                        out=et[:KP, :QW], in0=et[:KP, :QW], in1=m_ap, op=ALU.mult),
                         reads=[be, b_const], writes=[be])
                S.op("act", lambda e, et=et, spt=spt, KP=KP: e.activation(out=spt[:KP, :QW], in_=et[:KP, :QW],
                                                                          func=AF.Ln, bias=EPS_t[:KP, 1:2]),
                     reads=[be, b_const], writes=[bsp])

            def sb_p(si):
                kt_ap, v_ap, KP, m_ap, rds = steps[si]
                spt, bsp = R[si]["sp"]
                pi = Pb[rc["p"] % 2]
                rc["p"] += 1
                P, bP = pbank[pi], b_pb[pi]
                R[si]["p"] = (P, bP)
                S.op("pe", lambda e, spt=spt, KP=KP, P=P, si=si: e.matmul(
                    P[:KP, :QW], lhsT=tri_b[:KP, :KP], rhs=spt[:KP, :QW], start=True, stop=(si == 0)),
                     reads=[bsp, b_const], writes=[bP])
                if si > 0:
                    at, ba = R[si]["acc"]
                    S.op("pe", lambda e, at=at, KP=KP, P=P: e.matmul(
                        P[:KP, :QW], lhsT=ones_b[:, :KP], rhs=at[:, :QW], start=False, stop=True),
                         reads=[ba, b_const], writes=[bP])

            def sb_acc(si):
                kt_ap, v_ap, KP, m_ap, rds = steps[si]
                spt, bsp = R[si]["sp"]
                ai = rc["a"] % 2
                rc["a"] += 1
                an, ban = acc_t[ai], b_acc[ai]
                R[si + 1]["acc"] = (an, ban)
                if si == 0:
                    if KP < 128:
                        S.op("pool", lambda e, an=an: e.memset(an[:, :QW], 0.0), writes=[ban])
                    S.op("pool", lambda e, an=an, spt=spt, KP=KP: e.tensor_copy(out=an[:KP, :QW], in_=spt[:KP, :QW]),
                         reads=[bsp], writes=[ban])
                else:
                    ac, bac = R[si]["acc"]
                    if KP < 128:
                        raise NotImplementedError
                    S.op("pool", lambda e, an=an, ac=ac, spt=spt: e.tensor_tensor(
                        out=an[:, :QW], in0=ac[:, :QW], in1=spt[:, :QW], op=ALU.add),
                         reads=[bac, bsp], writes=[ban])

            def sb_tw(si):
                kt_ap, v_ap, KP, m_ap, rds = steps[si]
                et, be = R[si]["e"]
                P, bP = R[si]["p"]
                ti_ = rc["t"] % NE
                rc["t"] += 1
                tt, btt = t_t[ti_], b_t[ti_]
                wt, bw = take_w(si)
                S.op("act", lambda e, tt=tt, KP=KP, P=P: e.activation(out=tt[:KP, :QW], in_=P[:KP, :QW], func=AF.Exp,
                                                                     scale=-1.0),
                     reads=[bP], writes=[btt])
                S.op("dve", lambda e, wt=wt, et=et, tt=tt, KP=KP: e.tensor_tensor(
                    out=wt[:KP, :QW], in0=et[:KP, :QW], in1=tt[:KP, :QW], op=ALU.mult),
                     reads=[be, btt], writes=[bw])

            qk(0)
            sb_el(0)
            if ns > 1:
                qk(1)
                sb_el(1)
            for si in range(ns):
                if si + 2 < ns:
                    qk(si + 2)
                    sb_el(si + 2)
                sb_p(si)
                if si + 1 < ns:
                    sb_acc(si)
                sb_tw(si)
                if si >= 1:
                    pv(si - 1)
                if si == hook_at and hook is not None:
                    hook()
            pv(ns - 1)

        def epilogue(l, hh, QW, c0, ti, gate_ap, bgate, kind, ob=0):
            O, bO = pbank[3 + ob], b_pb[3 + ob]
            DEN, bDEN = pbank[6 + ob], b_pb[6 + ob]
            SS, bSS = pbank[5], b_pb[5]
            i = ectr[0] % 2
            if kind == "fox":
                S.op("dve", lambda e: e.reciprocal(out=tmpf[0][:, :QW], in_=DEN[:, :QW]), reads=[bDEN], writes=[b_tmpf[0]])
                S.op("dve", lambda e: e.tensor_tensor(out=tmpf[1][:, :QW], in0=O[:, :QW], in1=tmpf[0][:, :QW], op=ALU.mult),
                     reads=[bO, b_tmpf[0]], writes=[b_tmpf[1]])
                u_ap, bu = tmpf[1], b_tmpf[1]
            else:
                u_ap, bu = O, bO
            S.op("act", lambda e: e.activation(out=sqb[i][:, :QW], in_=u_ap[:, :QW], func=AF.Square), reads=[bu],
                 writes=[b_sqb[i]])
            S.op("pe", lambda e: e.matmul(SS[:, :QW], lhsT=ones_b, rhs=sqb[i][:, :QW], start=True, stop=True),
                 reads=[b_sqb[i], b_const], writes=[bSS])
            S.op("act", lambda e: e.activation(out=tmpf[0][:, :QW], in_=SS[:, :QW], func=AF.Sqrt, bias=EPS_t[:, 0:1],
                                               scale=1.0 / 128),
                 reads=[bSS, b_const], writes=[b_tmpf[0]])
            S.op("dve", lambda e: e.reciprocal(out=tmpf[0][:, :QW], in_=tmpf[0][:, :QW]), reads=[b_tmpf[0]],
                 writes=[b_tmpf[0]])
            S.op("dve", lambda e: e.scalar_tensor_tensor(out=tmpf[1][:, :QW], in0=u_ap[:, :QW], scalar=pcol(l, "g_o", hh),
                                                         in1=tmpf[0][:, :QW], op0=ALU.mult, op1=ALU.mult),
                 reads=[bu, b_tmpf[0], b_const], writes=[b_tmpf[1]])
            sf, bsf, sbb, bsb = stage()
            S.op("dve", lambda e, sbb=sbb: e.tensor_tensor(out=sbb[:, :QW], in0=tmpf[1][:, :QW], in1=gate_ap, op=ALU.mult),
                 reads=[b_tmpf[1], bgate], writes=[bsb])
            S.dma("pool", ogS[hh, :, c0:c0 + QW], sbb[:, :QW], reads=[bsb], writes=[b_ogS[hh][ti]])

        ob_ctr = [0]
        pend = [None]

        def next_ob():
            i = ob_ctr[0] % 2
            ob_ctr[0] += 1
            return i

        def phase_b(l):
            cumsum_tables(l)
            if cfg.sample:
                sample_tables(l)
            if getattr(cfg, "stop", 99) < 4:
                return
            for hh in range(NH if getattr(cfg, "stop", 99) >= 5 else 1):
                kind = "sb" if hh < 4 else "fox"
                if l + 1 < L and hh < 4:
                    cast_layer(l + 1, hh)
                qi = hh % 2
                def load_kv(hx, ip):
                    for jp in range(4):
                        n0 = 16 * ip + 4 * jp
                        S.dma("sync", KT[:, n0 * 128:(n0 + 4) * 128],
                              gK[l][ip][jp * 1024 + hx * 128: jp * 1024 + hx * 128 + 128, :],
                              reads=[b_gK[l][ip]], writes=[B_KT[ip]])
                        S.dma("sync", VV[:, n0 * 128:(n0 + 4) * 128].rearrange("p (s d) -> p s d", d=128),
                              gV[l][ip][jp * 512:(jp + 1) * 512, hx * 128:(hx + 1) * 128].rearrange("(s p) d -> p s d", p=128),
                              reads=[b_gV[l][ip]], writes=[B_VVg[ip]])

                for ip in (range(NT) if hh == 0 else range(1)):
                    load_kv(hh, ip)
                def fox_bias(ti):
                    h = hh - 4
                    bi = ti % 2
                    nkb_ = 16 * ti + 16
                    for sq in range(2):
                        cix = h * 32 + ti * 4 + 2 * sq
                        S.op("dve", lambda e, h=h, bi=bi, sq=sq, cix=cix, nkb_=nkb_: e.tensor_scalar(
                            out=bt[bi][:, sq, :nkb_], in0=ncb[:, h, :nkb_], scalar1=ncref[:, cix:cix + 1],
                            scalar2=60.0, op0=ALU.subtract, op1=ALU.min),
                             reads=[b_ncb, b_ncref], writes=[b_bt[bi]])

                if kind == "fox":
                    fox_bias(NT - 1)
                for ti in range(NT - 1, -1, -1):
                    nkb = 16 * ti + 16
                    order = list(range(nkb - 1, -1, -1)) if kind == "sb" else list(range(nkb))
                    steps = []
                    for n in order:
                        m = n - 16 * ti
                        m_ap = None
                        if m >= 0:
                            off = ((0 if kind == "sb" else 1) * 16 + m) * 512
                            m_ap = mask_t[:, off:off + 512]
                        steps.append((KT[:, n * 128:(n + 1) * 128], VV[:, n * 128:(n + 1) * 128], 128, m_ap,
                                      [B_KT[n // 16], B_VVg[n // 16]]))
                    bias_of = None
                    if kind == "fox":
                        bi = ti % 2
                        bias_of = lambda si, sq, bi=bi, order=order: (bt[bi][:, sq, order[si]:order[si] + 1], b_bt[bi])
                    qi = load_qg(hh, ti, ti * 512, 512)
                    ob = next_ob()
                    hk = pend[0]
                    pend[0] = None
                    if kind == "fox" and ti >= 1:
                        hk0 = hk
                        hk = lambda hk0=hk0, ti=ti: ((hk0() if hk0 is not None else None), fox_bias(ti - 1))
                    attn_tile(hh, 512, qh[qi][:, :], b_qh[qi], steps, bias_of, kind, ob, hook=hk)
                    pend[0] = (lambda l=l, hh=hh, ti=ti, qi=qi, kind=kind, ob=ob: epilogue(
                        l, hh, 512, ti * 512, ti, gh[qi][:, :], b_gh[qi], kind, ob))
                    if hh + 1 < NH and ti + 1 <= NT - 1:
                        load_kv(hh + 1, ti + 1)
                if cfg.sample:
                    sample_attn(l, hh, kind, qi)
            if pend[0] is not None:
                pend[0]()
                pend[0] = None

        def sample_tables(l):
            S.op("dve", lambda e: e.memset(lnps[:, :, :], 0.0), writes=[b_lnps])
            S.dma("sync", lnps[0:8, :, :], clf[l, :, :, :], reads=[], writes=[b_lnps])
            S.dma("sync", lnps[8:9, :, 0:DEC_T], slfs[:, :].rearrange("(o h) p -> o h p", o=1), reads=[b_slfs],
                  writes=[b_lnps])
            cum_core(lnps, b_lnps, 9, ncbs, b_ncbs, 9)

        def sample_attn(l, hh, kind, qi):
            c0 = NTOK
            QW = DEC_T
            S.dma("pool", kts[:, 0:PAST], ckT[l, hh, :, :], writes=[b_kts])
            S.dma("sync", kts[:, PAST:PAST + DEC_T], sks[hh, :, :], reads=[b_sks[hh]], writes=[b_kts])
            S.dma("pool", vs[:, 0:8, :], cv[l, :, hh * 128:(hh + 1) * 128].rearrange("(n p) d -> p n d", p=128),
                  writes=[b_vs])
            S.dma("sync", vs[0:DEC_T, 8, :], svs[:, hh * 128:(hh + 1) * 128], reads=[b_svs[hh]], writes=[b_vs])
            qi = load_qg(hh, NT, c0, QW)
            q_ap = qh[qi][:, :QW]
            mk = smask_t[0:DEC_T, (0 if kind == "sb" else 32):(0 if kind == "sb" else 32) + 32]
            blocks = [(kts[:, PAST:PAST + DEC_T], vs[0:DEC_T, 8, :], DEC_T, mk, [b_kts, b_vs])]
            for n in range(7, -1, -1):
                blocks.append((kts[:, n * 128:(n + 1) * 128], vs[:, n, :], 128, None, [b_kts, b_vs]))
            bias_of = None
            if kind == "fox":
                h = hh - 4
                pb, bpb = pbank[5], b_pb[5]
                S.op("pe", lambda e, h=h: e.matmul(pb[:, 0:16], lhsT=sel31_f, rhs=ncbs[:, h, :], start=True, stop=True),
                     reads=[b_ncbs, b_const], writes=[bpb])
                S.op("act", lambda e: e.activation(out=xs_ref[:, 0:16], in_=pb[:, 0:16], func=AF.Identity), reads=[bpb],
                     writes=[b_xsref])
                S.op("dve", lambda e, h=h: e.tensor_scalar(out=bt[0][:, 0, 0:9], in0=ncbs[:, h, 0:9], scalar1=xs_ref[:, 8:9],
                                                           scalar2=0.0, op0=ALU.subtract, op1=ALU.min),
                     reads=[b_ncbs, b_xsref], writes=[b_bt[0]])
                nidx = [8] + list(range(7, -1, -1))
                bias_of = lambda si, sq: (bt[0][:(DEC_T if si == 0 else 128), 0, nidx[si]:nidx[si] + 1], b_bt[0])
            ob = next_ob()
            hk = pend[0]
            pend[0] = None
            attn_tile(hh, QW, q_ap, b_qh[qi], blocks, bias_of, kind, ob, hook=hk)
            pend[0] = (lambda l=l, hh=hh, qi=qi, kind=kind, ob=ob: epilogue(
                l, hh, QW, c0, NT, gh[qi][:, :QW], b_gh[qi], kind, ob))

        stop = getattr(cfg, "stop", 99)
        if stop >= 2:
            for l in range(L):
                tile_pass(l)
                if stop >= 3:
                    phase_b(l)
            if stop >= 6:
                tile_pass(L)
        S.final_waits("sync")

        with nc.Block() as block:
            @block.tensor
            def _(e):
                for f in S.q["pe"]:
                    f(e)

            @block.scalar
            def _(e):
                for f in S.q["act"]:
                    f(e)

            @block.vector
            def _(e):
                for f in S.q["dve"]:
                    f(e)

            @block.gpsimd
            def _(e):
                for f in S.q["pool"]:
                    f(e)

            @block.sync
            def _(e):
                for f in S.q["sync"]:
                    f(e)
    return nc, S


def _consts():
    k = np.arange(128)
    ones = np.ones((128, 128), np.float32)
    tri = (k[:, None] >= k[None, :]).astype(np.float32)
    omt = 1.0 - tri
    cbf = np.concatenate([ones, tri, omt], axis=1).astype(ml_dtypes.bfloat16)
    ident = np.eye(128, dtype=np.float32)
    triinc = (k[:, None] <= k[None, :]).astype(np.float32)
    su = (k[:, None] < k[None, :]).astype(np.float32)
    sel = np.zeros((128, 128), np.float32)
    sel[127, :] = 1.0
    sel31 = np.zeros((128, 128), np.float32)
    sel31[31, :] = 1.0
    cf32 = np.concatenate([ident, ones, triinc, su, sel, sel31], axis=1).astype(np.float32)
    return cbf, cf32


def _masks(j):
    k = np.arange(128)
    out = np.zeros((128, 2, 16, 512), np.float32)
    tri_sb = (k[:, None] < k[None, :]).astype(np.float32)
    tri_fx = (k[:, None] <= k[None, :]).astype(np.float32)
    for kind, tri in enumerate((tri_sb, tri_fx)):
        for m in range(16):
            d = m - 4 * j
            for s in range(4):
                if d < s:
                    out[:, kind, m, s * 128:(s + 1) * 128] = 1.0
                elif d == s:
                    out[:, kind, m, s * 128:(s + 1) * 128] = tri
    return out.reshape(128, 2 * 16 * 512).astype(ml_dtypes.bfloat16)


def _smask():
    k = np.arange(32)
    out = np.zeros((128, 64), np.float32)
    out[:32, 0:32] = (k[:, None] < k[None, :])
    out[:32, 32:64] = (k[:, None] <= k[None, :])
    return out.astype(ml_dtypes.bfloat16)


_PROG_CACHE = {}


def run(cfg, inputs):
    NT, L, NTOK, NTOT = cfg.NT, cfg.DEPTH, cfg.NTOK, cfg.NTOT
    f32 = np.float32
    g = {k: np.asarray(v) for k, v in inputs.items()}
    key = (NT, L, cfg.sample)
    if key not in _PROG_CACHE:
        _PROG_CACHE[key] = build_program(cfg)
    nc, S = _PROG_CACHE[key]

    w_in = g["w_in"].astype(f32, copy=False)
    w_in_r = w_in[:, :, :4096].reshape(L, KC, 128, 32, 128).transpose(0, 3, 2, 1, 4).reshape(L * 32 * 128 * 2, 1024)
    w_f_r = w_in[:, :, 4096:4100].reshape(L, KC, 128, 4).transpose(0, 2, 1, 3).reshape(L * 128, 64)
    w_out_r = g["w_out"].reshape(L, 8, 128, 16, 128).transpose(0, 3, 2, 1, 4).reshape(L * 16 * 128, 1024)
    w_g_r = g["w_ple_gate"].reshape(L, KC, 128, 16, 128).transpose(0, 3, 2, 1, 4).reshape(L * 16 * 128 * 2, 1024)
    w_ple_r = g["w_ple"].reshape(L, 2, 128, 16, 128).transpose(0, 3, 2, 1, 4).reshape(L * 16 * 128, 256)
    w_in_r, w_f_r, w_out_r, w_g_r, w_ple_r = (np.ascontiguousarray(a, dtype=f32) for a in
                                              (w_in_r, w_f_r, w_out_r, w_g_r, w_ple_r))
    cbf, cf32 = _consts()
    smask = _smask()
    NPAR = L * 48 + 16 + 8
    in_maps = []
    for c in range(8):
        b, j = c // 4, c % 4
        tiles = [4 * i + j for i in range(NT)]
        xp = g["x_prompt"][b].reshape(4 * NT, 512, D)[tiles].reshape(NTOK, D)
        xs = g["x_sample"][c]
        xTc = np.ascontiguousarray(np.concatenate([xp, xs], axis=0).T, dtype=f32)
        pp = g["p_prompt"][:, b].reshape(L, 4 * NT, 512, PLE)[:, tiles].reshape(L, NTOK, PLE)
        pTc = np.ascontiguousarray(np.concatenate([pp, g["p_sample"][:, c]], axis=1).transpose(0, 2, 1), dtype=f32)
        par = np.zeros((128, NPAR), f32)
        for l in range(L):
            par[:, l * 48: l * 48 + 16] = g["g_attn_norm"][l].reshape(KC, 128).T
            par[:, l * 48 + 16: l * 48 + 32] = g["g_ple_norm"][l].reshape(KC, 128).T
            par[:, l * 48 + 32: l * 48 + 36] = g["g_out_sb"][l].reshape(4, 128).T
            par[:, l * 48 + 36: l * 48 + 40] = g["g_out_fox"][l].reshape(4, 128).T
            par[0:4, l * 48 + 40] = -g["b_forget"][l]
        par[:, L * 48: L * 48 + 16] = g["g_final"].reshape(KC, 128).T
        par[:, L * 48 + 16 + j] = 1.0
        m = {"xT": xTc, "pT": pTc, "w_in_r": w_in_r, "w_f_r": w_f_r, "w_out_r": w_out_r, "w_g_r": w_g_r,
             "w_ple_r": w_ple_r, "params": par, "cbf": cbf, "cf32": cf32, "masks": _masks(j), "smask": smask}
        if cfg.sample:
            ck = np.concatenate([g["cache_sb_k"][:, c], g["cache_fox_k"][:, c]], axis=2)
            m["ckT"] = np.ascontiguousarray(ck.transpose(0, 2, 3, 1), dtype=f32)
            cvv = np.concatenate([g["cache_sb_v"][:, c], g["cache_fox_v"][:, c]], axis=2)
            m["cv"] = np.ascontiguousarray(cvv.reshape(L, PAST, NH * HD), dtype=f32)
            m["clf"] = np.ascontiguousarray(g["cache_fox_logf"][:, c].reshape(L, 8, 128, 4).transpose(0, 1, 3, 2), dtype=f32)
        in_maps.append(m)

    res = run_bass_kernel_spmd(nc, in_maps, core_ids=list(range(8)), trace=getattr(cfg, "trace", False))
    R = res.results
    global LAST_R, LAST_RES
    LAST_R = R
    LAST_RES = res
    SEQ = cfg.SEQ
    y_p = np.zeros((2, SEQ, D), f32)
    y_s = np.zeros((8, DEC_T, D), f32)
    kp = np.zeros((L, 2, SEQ, NH, HD), f32)
    vp = np.zeros((L, 2, SEQ, NH, HD), f32)
    lp = np.zeros((L, 2, SEQ, 4), f32)
    ks = np.zeros((L, 8, DEC_T, NH, HD), f32)
    vsa = np.zeros((L, 8, DEC_T, NH, HD), f32)
    ls = np.zeros((L, 8, DEC_T, 4), f32)
    for c in range(8):
        b, j = c // 4, c % 4
        r = R[c]
        yt = r["yT"].T
        kt = r["kT_out"].transpose(0, 3, 1, 2)
        vt = r["v_out"].reshape(L, NTOT, NH, HD)
        lt = r["lf_out"].transpose(0, 2, 1)
        for i in range(NT):
            gt = 4 * i + j
            y_p[b, gt * 512:(gt + 1) * 512] = yt[i * 512:(i + 1) * 512]
            kp[:, b, gt * 512:(gt + 1) * 512] = kt[:, i * 512:(i + 1) * 512]
            vp[:, b, gt * 512:(gt + 1) * 512] = vt[:, i * 512:(i + 1) * 512]
            lp[:, b, gt * 512:(gt + 1) * 512] = lt[:, i * 512:(i + 1) * 512]
        y_s[c] = yt[NTOK:]
        ks[:, c] = kt[:, NTOK:]
        vsa[:, c] = vt[:, NTOK:]
        ls[:, c] = lt[:, NTOK:]
    return (y_p, y_s, kp[..., 0:4, :], kp[..., 4:8, :], vp[..., 0:4, :], vp[..., 4:8, :], lp,
            ks[..., 0:4, :], ks[..., 4:8, :], vsa[..., 0:4, :], vsa[..., 4:8, :], ls)


def kernel(**inputs):
    cfg = Cfg(NT=8, DEPTH=4, sample=True)
    outs = run(cfg, inputs)
    y_p, y_s, skp, fkp, svp, fvp, lp, sks, fks, svs, fvs, ls = outs
    c = np.ascontiguousarray
    return (c(y_p), c(y_s), c(skp), c(svp), c(fkp), c(fvp), c(lp), c(sks), c(svs), c(fks), c(fvs), c(ls))
```

```python
import contextlib
import numpy as np
import ml_dtypes
import concourse.bass as bass
import concourse.mybir as mybir
from concourse.bass_utils import run_bass_kernel_spmd

F32 = mybir.dt.float32
BF16 = mybir.dt.bfloat16
AF = mybir.ActivationFunctionType
ALU = mybir.AluOpType

D = 2048
KC = 16
HD = 128
NH = 8
PLE = 256
EPS = 1e-6
DEC_T = 32
PAST = 1024
QSCALE = HD ** -0.5


class Cfg:
    def __init__(self, NT=8, DEPTH=4, sample=True):
        self.NT = NT
        self.DEPTH = DEPTH
        self.sample = sample
        self.NTOK = NT * 512
        self.NTOT = self.NTOK + DEC_T
        self.NB = NT * 16
        self.SEQ = NT * 4 * 512


class Buf:
    __slots__ = ("name", "w", "r", "excl")

    def __init__(self, name, excl=False):
        self.name = name
        self.w = None
        self.r = {}
        self.excl = excl


ENGS = ("pe", "act", "dve", "pool", "sync")


def _flat(x):
    out = []
    for b in x:
        if isinstance(b, (list, tuple)):
            out.extend(_flat(b))
        else:
            out.append(b)
    return out


class Sched:
    def __init__(self, nc, sems, dsems):
        self.nc = nc
        self.sem = sems
        self.q = {e: [] for e in ENGS}
        self.cnt = {e: 0 for e in ENGS}
        self.waited = {e: {} for e in ENGS}
        self.dsem = dsems
        self.duse = {k: [0] * len(v) for k, v in dsems.items()}
        self.dnext = {k: 0 for k in dsems}
        self.n_ops = 0
        self.n_waits = 0

    def _semobj(self, key):
        if isinstance(key, tuple):
            return self.dsem[key[0]][key[1]]
        return self.sem[key]

    def _collect(self, reads, writes):
        need = {}

        def add(tok):
            if tok is None:
                return
            k, v = tok
            if need.get(k, 0) < v:
                need[k] = v

        for b in reads:
            add(b.w)
        for b in writes:
            add(b.w)
            for k, v in b.r.items():
                add((k, v))
        return need

    def _emit_waits(self, eng, need):
        wl = []
        wd = self.waited[eng]
        for k, v in need.items():
            if eng == "pe" and k == "pe":
                continue
            if wd.get(k, 0) >= v:
                continue
            wd[k] = v
            wl.append((self._semobj(k), v))
        return wl

    def _commit(self, tok, reads, writes):
        k, v = tok
        for b in reads:
            if b.r.get(k, 0) < v:
                b.r[k] = v
        for b in writes:
            b.w = tok
            b.r = {}

    def op(self, eng, fn, reads=(), writes=()):
        reads, writes = _flat(reads), _flat(writes)
        ex = [b for b in reads if b.excl]
        if ex:
            writes = list(writes) + ex
        need = self._collect(reads, writes)
        wl = self._emit_waits(eng, need)
        self.cnt[eng] += 1
        tok = (eng, self.cnt[eng])
        sem = self.sem[eng]
        self.n_ops += 1
        self.n_waits += len(wl)

        def run(e, wl=wl, fn=fn, sem=sem):
            for s, v in wl:
                e.wait_ge(s, v)
            fn(e).then_inc(sem, 1)

        self.q[eng].append(run)
        self._commit(tok, reads, writes)

    def dma(self, queue, out, in_, reads=(), writes=(), **kw):
        eng = "pool" if queue == "cast" else queue
        reads, writes = _flat(reads), _flat(writes)
        pool = self.dsem[queue]
        i = self.dnext[queue]
        self.dnext[queue] = (i + 1) % len(pool)
        self.duse[queue][i] += 1
        use = self.duse[queue][i]
        key = (queue, i)
        need = self._collect(reads, writes)
        if use > 1:
            if need.get(key, 0) < 16 * (use - 1):
                need[key] = 16 * (use - 1)
        wl = self._emit_waits(eng, need)
        sem = pool[i]
        self.n_ops += 1
        self.n_waits += len(wl)

        def run(e, wl=wl, sem=sem, out=out, in_=in_, kw=kw):
            for s, v in wl:
                e.wait_ge(s, v)
            e.dma_start(out=out, in_=in_, **kw).then_inc(sem, 16)

        self.q[eng].append(run)
        self._commit((key, 16 * use), reads, writes)

    def custom(self, eng, fn, semobj_key, val, reads=(), writes=()):
        reads, writes = _flat(reads), _flat(writes)
        need = self._collect(reads, writes)
        wl = self._emit_waits(eng, need)

        def run(e, wl=wl, fn=fn):
            for s, v in wl:
                e.wait_ge(s, v)
            fn(e)

        self.q[eng].append(run)
        self._commit((semobj_key, val), reads, writes)

    def final_waits(self, eng):
        wl = []
        for qn, pool in self.dsem.items():
            for i, s in enumerate(pool):
                if self.duse[qn][i] > 0:
                    wl.append((s, 16 * self.duse[qn][i]))

        def run(e, wl=wl):
            for s, v in wl:
                e.wait_ge(s, v)

        self.q[eng].append(run)


def build_program(cfg):
    NT, L, NTOK, NTOT, NB = cfg.NT, cfg.DEPTH, cfg.NTOK, cfg.NTOT, cfg.NB
    nc = bass.Bass("TRN2", target_bir_lowering=False)

    def din(name, shape, dt=F32):
        return nc.dram_tensor(name, list(shape), dt, kind="ExternalInput")

    def dout(name, shape, dt=F32):
        return nc.dram_tensor(name, list(shape), dt, kind="ExternalOutput")

    def dint(name, shape, dt):
        return nc.dram_tensor(name, list(shape), dt)

    xT = din("xT", [D, NTOT])
    pT = din("pT", [L, PLE, NTOT])
    w_in_r = din("w_in_r", [L * 32 * 128 * 2, 1024])
    w_f_r = din("w_f_r", [L * 128, 64])
    w_out_r = din("w_out_r", [L * 16 * 128, 1024])
    w_g_r = din("w_g_r", [L * 16 * 128 * 2, 1024])
    w_ple_r = din("w_ple_r", [L * 16 * 128, 256])
    NPAR = L * 48 + 16 + 8
    params = din("params", [128, NPAR])
    cbf = din("cbf", [128, 384], BF16)
    cf32 = din("cf32", [128, 6 * 128])
    masks = din("masks", [128, 2 * 16 * 512], BF16)
    smask = din("smask", [128, 2 * 32], BF16)
    if cfg.sample:
        ckT = din("ckT", [L, NH, HD, PAST])
        cv = din("cv", [L, PAST, NH * HD])
        clf = din("clf", [L, 8, 4, 128])

    yT = dout("yT", [D, NTOT])
    kT_out = dout("kT_out", [L, NH, HD, NTOT])
    v_out = dout("v_out", [L, NTOT, NH * HD])
    lf_out = dout("lf_out", [L, 4, NTOT])

    wb_in = dint("wb_in", [L * 32 * 128 * 2, 1024], BF16)
    wb_f = dint("wb_f", [L * 128, 64], BF16)
    wb_out = dint("wb_out", [L * 16 * 128, 1024], BF16)
    wb_g = dint("wb_g", [L * 16 * 128 * 2, 1024], BF16)
    wb_ple = dint("wb_ple", [L * 16 * 128, 256], BF16)
    xS = dint("xS", [D, NTOT], F32)
    qS = dint("qS", [NH, HD, NTOT], BF16)
    gS = dint("gS", [NH, HD, NTOT], BF16)
    ogS = dout("ogS", [NH, HD, NTOT], BF16) if getattr(cfg, "debug", False) else dint("ogS", [NH, HD, NTOT], BF16)
    skK = [[dint(f"skK{l}_{t}", [1024, 512], BF16) for t in range(NT)] for l in range(L)]
    skV = [[dint(f"skV{l}_{t}", [512, 1024], BF16) for t in range(NT)] for l in range(L)]
    gK = [[dint(f"gK{l}_{t}", [4 * 1024, 512], BF16) for t in range(NT)] for l in range(L)]
    gV = [[dint(f"gV{l}_{t}", [4 * 512, 1024], BF16) for t in range(NT)] for l in range(L)]
    slf = [dint(f"slf{l}", [4, NTOK], F32) for l in range(L)]
    glf = [dint(f"glf{l}", [16, NTOK], F32) for l in range(L)]
    sks = dint("sks", [NH, HD, DEC_T], BF16)
    svs = dint("svs", [DEC_T, NH * HD], BF16)
    slfs = dint("slfs", [4, DEC_T], F32)

    es = contextlib.ExitStack()
    with es:
        def sb(name, shape, dt):
            return es.enter_context(nc.sbuf_tensor(name, list(shape), dt))

        def ps(name):
            return es.enter_context(nc.psum_tensor(name, [128, 512], F32))

        sems = {e: es.enter_context(nc.semaphore("s_" + e)) for e in ("pe", "act", "dve", "pool")}
        dsems = {
            "sync": [es.enter_context(nc.semaphore(f"ds{i}")) for i in range(24)],
            "pool": [es.enter_context(nc.semaphore(f"dp{i}")) for i in range(12)],
            "cast": [es.enter_context(nc.semaphore(f"dc{i}")) for i in range(3)],
        }
        cc_sem = es.enter_context(nc.semaphore("cc"))
        sems["cc"] = cc_sem
        S = Sched(nc, sems, dsems)

        BIG = sb("BIG", [128, 32768], BF16)
        KT = BIG[:, 0:16384]
        VV = BIG[:, 16384:32768]
        xt = BIG[:, 0:16384].bitcast(F32).rearrange("p (k t) -> p k t", t=512)
        hT = BIG[:, 16384:24576].rearrange("p (k t) -> p k t", t=512)
        B_KT = [Buf(f"KT{g}") for g in range(8)]
        B_VVg = [Buf(f"VV{g}") for g in range(8)]
        B_VV0, B_VV1 = B_VVg[0:4], B_VVg[4:8]
        NWS = 6
        wring = [sb(f"w{i}", [128, 2048], BF16) for i in range(NWS)]
        b_wring = [Buf(f"w{i}") for i in range(NWS)]
        ogt = sb("ogt", [128, 8, 512], BF16)
        b_ogt = Buf("ogt")
        ptb = sb("ptb", [128, 2, 512], BF16)
        b_ptb = Buf("ptb")
        mask_t = sb("mask_t", [128, 2 * 16 * 512], BF16)
        smask_t = sb("smask_t", [128, 64], BF16)
        cb = sb("cb", [128, 384], BF16)
        cf = sb("cf", [128, 768], F32)
        par = sb("par", [128, NPAR], F32)
        b_const = Buf("const")
        ones_b, tri_b, omt_b = cb[:, 0:128], cb[:, 128:256], cb[:, 256:384]
        ident_f, ones_f, triinc_f, su_f, sel127_f, sel31_f = (cf[:, i * 128:(i + 1) * 128] for i in range(6))
        NST = 3
        stf = [sb(f"stf{i}", [128, 512], F32) for i in range(NST)]
        b_stf = [Buf(f"stf{i}") for i in range(NST)]
        stb = [sb(f"stb{i}", [128, 512], BF16) for i in range(NST)]
        b_stb = [Buf(f"stb{i}") for i in range(NST)]
        sqb = [sb(f"sqb{i}", [128, 512], BF16) for i in range(2)]
        b_sqb = [Buf(f"sqb{i}") for i in range(2)]
        rstd = sb("rstd", [128, 512], F32)
        b_rstd = Buf("rstd")
        tmpf = [sb(f"tmpf{i}", [128, 512], F32) for i in range(2)]
        b_tmpf = [Buf(f"tmpf{i}") for i in range(2)]
        qh = [sb(f"qh{i}", [128, 512], BF16) for i in range(2)]
        b_qh = [Buf(f"qh{i}") for i in range(2)]
        gh = [sb(f"gh{i}", [128, 512], BF16) for i in range(2)]
        b_gh = [Buf(f"gh{i}") for i in range(2)]
        qg_ctr = [0]

        def load_qg(hh, ti, c0, QW):
            i = qg_ctr[0] % 2
            qg_ctr[0] += 1
            S.dma("sync", qh[i][:, :QW], qS[hh, :, c0:c0 + QW], reads=[b_qS[hh][ti]], writes=[b_qh[i]])
            S.dma("sync", gh[i][:, :QW], gS[hh, :, c0:c0 + QW], reads=[b_gS[hh][ti]], writes=[b_gh[i]])
            return i
        NE = 4
        e_t = [sb(f"e{i}", [128, 512], F32) for i in range(NE)]
        b_e = [Buf(f"e{i}") for i in range(NE)]
        sp_t = [sb(f"sp{i}", [128, 512], BF16) for i in range(NE)]
        b_sp = [Buf(f"sp{i}") for i in range(NE)]
        t_t = [sb(f"t{i}", [128, 512], F32) for i in range(NE)]
        b_t = [Buf(f"t{i}") for i in range(NE)]
        NW = 3
        w_t = [sb(f"wt{i}", [128, 512], BF16) for i in range(NW)]
        b_w = [Buf(f"wt{i}") for i in range(NW)]
        ncb = sb("ncb", [128, 4, 128], F32)
        b_ncb = Buf("ncb")
        lnp = sb("lnp", [128, 4, 128], F32)
        b_lnp = Buf("lnp")
        ltb = sb("ltb", [128, 128], F32)
        b_ltb = Buf("ltb")
        tott = sb("tott", [128, 128], F32)
        b_tott = Buf("tott")
        xn = sb("xn", [128, 128], F32)
        b_xn = Buf("xn")
        ncref = sb("ncref", [128, 128], F32)
        b_ncref = Buf("ncref")
        bt = [sb(f"bt{i}", [128, 4, 128], F32) for i in range(2)]
        b_bt = [Buf(f"bt{i}") for i in range(2)]
        lfst = sb("lfst", [4, 512], F32)
        b_lfst = Buf("lfst")
        kts = sb("kts", [128, PAST + DEC_T], BF16)
        b_kts = Buf("kts")
        vs = sb("vs", [128, 9, 128], BF16)
        b_vs = Buf("vs")
        lnps = sb("lnps", [128, 4, 128], F32)
        b_lnps = Buf("lnps")
        ncbs = sb("ncbs", [128, 4, 16], F32)
        b_ncbs = Buf("ncbs")
        xs_ref = sb("xs_ref", [128, 16], F32)
        b_xsref = Buf("xs_ref")

        pbank = [ps(f"pb{i}") for i in range(8)]
        b_pb = [Buf(f"pb{i}", excl=True) for i in range(8)]

        b_xS = [Buf(f"xS{t}") for t in range(NT + 1)]
        b_qS = [[Buf(f"qS{h}_{t}") for t in range(NT + 1)] for h in range(NH)]
        b_gS = [[Buf(f"gS{h}_{t}") for t in range(NT + 1)] for h in range(NH)]
        b_ogS = [[Buf(f"ogS{h}_{t}") for t in range(NT + 1)] for h in range(NH)]
        b_skK = [[[] for _ in range(NT)] for _ in range(L)]
        b_skV = [[[] for _ in range(NT)] for _ in range(L)]
        b_gK = [[Buf(f"gK{l}_{t}") for t in range(NT)] for l in range(L)]
        b_gV = [[Buf(f"gV{l}_{t}") for t in range(NT)] for l in range(L)]
        b_slf = [[] for _ in range(L)]
        b_glf = [Buf(f"glf{l}") for l in range(L)]
        b_wb = {}
        b_sks, b_svs, b_slfs = [Buf(f"sks{h}") for h in range(NH)], [Buf(f"svs{h}") for h in range(NH)], Buf("slfs")

        def pcol(l, what, k=None):
            base = l * 48
            if what == "g_attn":
                return par[:, base + k: base + k + 1]
            if what == "g_ple":
                return par[:, base + 16 + k: base + 16 + k + 1]
            if what == "g_o":
                return par[:, base + 32 + k: base + 32 + k + 1]
            if what == "nbf":
                return par[0:4, base + 40: base + 41]
            raise KeyError(what)

        def gfin(k):
            return par[:, L * 48 + k: L * 48 + k + 1]

        onehot = par[:, L * 48 + 16: L * 48 + 20]

        S.dma("sync", cb[:], cbf[:, :], writes=[b_const])
        S.dma("sync", cf[:], cf32[:, :], writes=[b_const])
        S.dma("sync", par[:], params[:, :], writes=[b_const])
        S.dma("sync", mask_t[:], masks[:, :], writes=[b_const])
        S.dma("sync", smask_t[:], smask[:, :], writes=[b_const])

        def cast_rows(dst, src, r0, r1, key, step=2048):
            bl = []
            for a in range(r0, r1, step):
                b = min(a + step, r1)
                bb = Buf(f"wb_{key}_{a}")
                S.dma("cast", dst[a:b, :], src[a:b, :], writes=[bb])
                bl.append(bb)
            return bl

        def cast_layer(l, part):
            if part == 0:
                b_wb[("in", l)] = cast_rows(wb_in, w_in_r, l * 8192, l * 8192 + 4096, f"in{l}")
                b_wb[("f", l)] = cast_rows(wb_f, w_f_r, l * 128, (l + 1) * 128, f"f{l}")
            elif part == 1:
                b_wb[("in", l)] += cast_rows(wb_in, w_in_r, l * 8192 + 4096, (l + 1) * 8192, f"in{l}b")
            elif part == 2:
                b_wb[("out", l)] = cast_rows(wb_out, w_out_r, l * 2048, (l + 1) * 2048, f"out{l}")
                b_wb[("ple", l)] = cast_rows(wb_ple, w_ple_r, l * 2048, (l + 1) * 2048, f"ple{l}")
            elif part == 3:
                b_wb[("g", l)] = cast_rows(wb_g, w_g_r, l * 4096, (l + 1) * 4096, f"g{l}")

        for part in range(4):
            cast_layer(0, part)

        wctr = [0]

        def load_w(kind, l, c):
            i = wctr[0] % NWS
            wctr[0] += 1
            slot, bs = wring[i], b_wring[i]
            if kind == "in":
                r = (l * 32 + c) * 256
                S.dma("sync", slot[:, :], wb_in[r:r + 256, :].rearrange("(p t) x -> p (t x)", t=2),
                      reads=b_wb[("in", l)], writes=[bs])
                return slot[:, :].rearrange("p (k n) -> p k n", n=128), bs
            if kind == "g":
                r = (l * 16 + c) * 256
                S.dma("sync", slot[:, :], wb_g[r:r + 256, :].rearrange("(p t) x -> p (t x)", t=2),
                      reads=b_wb[("g", l)], writes=[bs])
                return slot[:, :].rearrange("p (k n) -> p k n", n=128), bs
            if kind == "out":
                r = (l * 16 + c) * 128
                S.dma("sync", slot[:, 0:1024], wb_out[r:r + 128, :], reads=b_wb[("out", l)], writes=[bs])
                return slot[:, 0:1024].rearrange("p (k n) -> p k n", n=128), bs
            if kind == "ple":
                r = (l * 16 + c) * 128
                S.dma("sync", slot[:, 0:256], wb_ple[r:r + 128, :], reads=b_wb[("ple", l)], writes=[bs])
                return slot[:, 0:256].rearrange("p (k n) -> p k n", n=128), bs
            if kind == "f":
                S.dma("sync", slot[:, 0:64], wb_f[l * 128:(l + 1) * 128, :], reads=b_wb[("f", l)], writes=[bs])
                return slot[:, 0:64].rearrange("p (k n) -> p k n", n=4), bs
            raise KeyError(kind)

        pa_ctr = [0]

        def pa_bank():
            i = pa_ctr[0] % 3
            pa_ctr[0] += 1
            return pbank[i], b_pb[i]

        st_ctr = [0]

        def stage():
            i = st_ctr[0] % NST
            st_ctr[0] += 1
            return stf[i], b_stf[i], stb[i], b_stb[i]

        def rms_stats(TW):
            acc, bacc = pbank[5], b_pb[5]
            for k in range(KC):
                i = k % 2
                S.op("act", lambda e, k=k, i=i: e.activation(out=sqb[i][:, :TW], in_=xt[:, k, :TW], func=AF.Square),
                     reads=[B_KT], writes=[b_sqb[i]])
                S.op("pe", lambda e, k=k, i=i: e.matmul(acc[:, :TW], lhsT=ones_b, rhs=sqb[i][:, :TW],
                                                        start=(k == 0), stop=(k == KC - 1)),
                     reads=[b_sqb[i], b_const], writes=[bacc])
            S.op("act", lambda e: e.activation(out=rstd[:, :TW], in_=acc[:, :TW], func=AF.Sqrt, bias=EPS_t[:, 0:1],
                                               scale=1.0 / D),
                 reads=[bacc, b_const], writes=[b_rstd])
            S.op("dve", lambda e: e.reciprocal(out=rstd[:, :TW], in_=rstd[:, :TW]), reads=[b_rstd], writes=[b_rstd])

        def make_h(TW, gname, l):
            for k in range(KC):
                g = pcol(l, gname, k) if gname != "fin" else gfin(k)
                S.op("dve", lambda e, k=k, g=g: e.scalar_tensor_tensor(
                    out=hT[:, k, :TW], in0=xt[:, k, :TW], scalar=g, in1=rstd[:, :TW], op0=ALU.mult, op1=ALU.mult),
                     reads=[B_KT, b_rstd, b_const], writes=[B_VV0])

        S.op("dve", lambda e: e.memset(xn[:, :], 0.0), writes=[b_xn])
        EPS_t = sb("eps_t", [128, 2], F32)
        S.op("dve", lambda e: e.memset(EPS_t[:, 0:1], EPS), writes=[b_const])
        S.op("dve", lambda e: e.memset(EPS_t[:, 1:2], 1.0), writes=[b_const])

        def phase_c(l, ti, TW, c0):
            S.dma("sync", ogt[:, :, :TW], ogS[:, :, c0:c0 + TW].rearrange("h p t -> p h t"),
                  reads=[b_ogS[h][ti] for h in range(NH)], writes=[b_ogt])
            S.dma("pool", ptb[:, :, :TW], pT[l, :, c0:c0 + TW].rearrange("(k p) t -> p k t", p=128),
                  writes=[b_ptb])
            for c in range(KC):
                wv, bw = load_w("out", l, c)
                pb, bpb = pa_bank()
                for k in range(8):
                    S.op("pe", lambda e, k=k, wv=wv, pb=pb: e.matmul(pb[:, :TW], lhsT=wv[:, k, :], rhs=ogt[:, k, :TW],
                                                                      start=(k == 0), stop=(k == 7)),
                         reads=[bw, b_ogt], writes=[bpb])
                S.op("dve", lambda e, c=c, pb=pb: e.tensor_tensor(out=xt[:, c, :TW], in0=pb[:, :TW], in1=xt[:, c, :TW],
                                                                  op=ALU.add),
                     reads=[bpb, B_KT], writes=[B_KT])
            rms_stats(TW)
            make_h(TW, "g_ple", l)
            for c in range(KC):
                wv, bw = load_w("g", l, c)
                pb, bpb = pa_bank()
                for k in range(KC):
                    S.op("pe", lambda e, k=k, wv=wv, pb=pb: e.matmul(pb[:, :TW], lhsT=wv[:, k, :], rhs=hT[:, k, :TW],
                                                                      start=(k == 0), stop=(k == KC - 1)),
                         reads=[bw, B_VV0], writes=[bpb])
                i = c % 2
                S.op("act", lambda e, pb=pb, i=i: e.activation(out=tmpf[i][:, :TW], in_=pb[:, :TW], func=AF.Sigmoid),
                     reads=[bpb], writes=[b_tmpf[i]])
                wv2, bw2 = load_w("ple", l, c)
                pb2, bpb2 = pa_bank()
                for k in range(2):
                    S.op("pe", lambda e, k=k, wv2=wv2, pb2=pb2: e.matmul(pb2[:, :TW], lhsT=wv2[:, k, :],
                                                                          rhs=ptb[:, k, :TW], start=(k == 0), stop=(k == 1)),
                         reads=[bw2, b_ptb], writes=[bpb2])
                S.op("dve", lambda e, pb2=pb2, i=i: e.tensor_tensor(out=tmpf[i][:, :TW], in0=pb2[:, :TW],
                                                                    in1=tmpf[i][:, :TW], op=ALU.mult),
                     reads=[bpb2, b_tmpf[i]], writes=[b_tmpf[i]])
                S.op("dve", lambda e, c=c, i=i: e.tensor_tensor(out=xt[:, c, :TW], in0=xt[:, c, :TW],
                                                                in1=tmpf[i][:, :TW], op=ALU.add),
                     reads=[b_tmpf[i], B_KT], writes=[B_KT])

        def phase_a(l, ti, TW, c0, after_h=None):
            is_s = ti == NT
            AP_ = getattr(cfg, "aparts", 255)
            rms_stats(TW)
            make_h(TW, "g_attn", l)
            if after_h is not None:
                after_h()
            if not AP_ & 1:
                return
            nsub = max(TW // 128, 1)
            SW = min(TW, 128)
            pdma = S.dma if AP_ & 8 else (lambda *a, **k: None)
            kinds_ok = getattr(cfg, 'kinds', 'qkvz')
            for c in range(32):
                grp, hh4 = c // 4, c % 4
                kind = ("q", "k", "v", "z")[grp % 4]
                if kind not in kinds_ok:
                    continue
                hh = hh4 + (4 if grp >= 4 else 0)
                wv, bw = load_w("in", l, c)
                pb, bpb = pa_bank()
                sf, bsf, sbb, bsb = stage()
                if kind != "v":
                    for k in range(KC):
                        S.op("pe", lambda e, k=k, wv=wv, pb=pb: e.matmul(pb[:, :TW], lhsT=wv[:, k, :], rhs=hT[:, k, :TW],
                                                                          start=(k == 0), stop=(k == KC - 1)),
                             reads=[bw, B_VV0], writes=[bpb])
                else:
                    for s in range(nsub):
                        for k in range(KC):
                            S.op("pe", lambda e, k=k, s=s, wv=wv, pb=pb: e.matmul(
                                pb[:SW, s * 128:(s + 1) * 128], lhsT=hT[:, k, s * 128:s * 128 + SW], rhs=wv[:, k, :],
                                start=(k == 0), stop=(k == KC - 1)),
                                 reads=[bw, B_VV0], writes=[bpb])
                if kind == "q":
                    S.op("act", lambda e, pb=pb, sbb=sbb: e.activation(out=sbb[:, :TW], in_=pb[:, :TW], func=AF.Identity,
                                                                         scale=QSCALE),
                         reads=[bpb], writes=[bsb])
                    pdma("pool", qS[hh, :, c0:c0 + TW], sbb[:, :TW], reads=[bsb], writes=[b_qS[hh][ti]])
                elif kind == "z":
                    S.op("act", lambda e, pb=pb, sbb=sbb: e.activation(out=sbb[:, :TW], in_=pb[:, :TW], func=AF.Silu),
                         reads=[bpb], writes=[bsb])
                    pdma("pool", gS[hh, :, c0:c0 + TW], sbb[:, :TW], reads=[bsb], writes=[b_gS[hh][ti]])
                elif kind == "k":
                    S.op("act", lambda e, pb=pb, sf=sf: e.activation(out=sf[:, :TW], in_=pb[:, :TW], func=AF.Identity),
                         reads=[bpb], writes=[bsf])
                    S.op("dve", lambda e, pb=pb, sbb=sbb: e.tensor_copy(out=sbb[:, :TW], in_=pb[:, :TW]),
                         reads=[bpb], writes=[bsb])
                    pdma("pool", kT_out[l, hh, :, c0:c0 + TW], sf[:, :TW], reads=[bsf])
                    if not is_s:
                        bb = Buf("skvp")
                        b_skK[l][ti].append(bb)
                        pdma("pool", skK[l][ti][hh * 128:(hh + 1) * 128, :], sbb[:, :TW], reads=[bsb], writes=[bb])
                    else:
                        pdma("pool", sks[hh, :, :], sbb[:, :TW], reads=[bsb], writes=[b_sks[hh]])
                else:
                    W4 = nsub * 128
                    S.op("act", lambda e, pb=pb, sf=sf: e.activation(out=sf[:SW, :W4], in_=pb[:SW, :W4], func=AF.Identity),
                         reads=[bpb], writes=[bsf])
                    S.op("dve", lambda e, pb=pb, sbb=sbb: e.tensor_copy(out=sbb[:SW, :W4], in_=pb[:SW, :W4]),
                         reads=[bpb], writes=[bsb])
                    pdma("pool", v_out[l, c0:c0 + TW, hh * 128:(hh + 1) * 128].rearrange("(s t) d -> t s d", t=SW),
                          sf[:SW, :W4].rearrange("t (s d) -> t s d", d=128), reads=[bsf])
                    if not is_s:
                        bb = Buf("skvp")
                        b_skV[l][ti].append(bb)
                        pdma("pool", skV[l][ti][:, hh * 128:(hh + 1) * 128].rearrange("(s t) d -> t s d", t=SW),
                              sbb[:SW, :W4].rearrange("t (s d) -> t s d", d=128), reads=[bsb], writes=[bb])
                    else:
                        pdma("pool", svs[:, hh * 128:(hh + 1) * 128], sbb[:SW, :128], reads=[bsb], writes=[b_svs[hh]])
            if not AP_ & 2:
                return
            wv, bw = load_w("f", l, 0)
            pb, bpb = pa_bank()
            for k in range(KC):
                S.op("pe", lambda e, k=k, wv=wv, pb=pb: e.matmul(pb[0:4, :TW], lhsT=wv[:, k, :], rhs=hT[:, k, :TW],
                                                                  start=(k == 0), stop=(k == KC - 1)),
                     reads=[bw, B_VV0], writes=[bpb])
            S.op("act", lambda e, pb=pb: e.activation(out=lfst[:, :TW], in_=pb[0:4, :TW], func=AF.Exp,
                                                      bias=pcol(l, "nbf"), scale=-1.0),
                 reads=[bpb, b_const], writes=[b_lfst])
            S.op("act", lambda e: e.activation(out=lfst[:, :TW], in_=lfst[:, :TW], func=AF.Ln, bias=EPS_t[0:4, 1:2]),
                 reads=[b_lfst, b_const], writes=[b_lfst])
            S.op("dve", lambda e: e.tensor_scalar(out=lfst[:, :TW], in0=lfst[:, :TW], scalar1=-1.0, scalar2=None,
                                                  op0=ALU.mult),
                 reads=[b_lfst], writes=[b_lfst])
            S.dma("pool", lf_out[l, :, c0:c0 + TW], lfst[:, :TW], reads=[b_lfst])
            if not is_s:
                bb = Buf("slfp")
                b_slf[l].append(bb)
                S.dma("pool", slf[l][:, c0:c0 + TW], lfst[:, :TW], reads=[b_lfst], writes=[bb])
            else:
                S.dma("pool", slfs[:, :], lfst[:, :TW], reads=[b_lfst], writes=[b_slfs])
            if not AP_ & 4:
                return
            if not is_s:
                all_gather(skK[l][ti], gK[l][ti], b_skK[l][ti], b_gK[l][ti])
                all_gather(skV[l][ti], gV[l][ti], b_skV[l][ti], b_gV[l][ti])
                if ti == NT - 1:
                    all_gather(slf[l], glf[l], b_slf[l], b_glf[l])

        def final_norm(ti, TW, c0):
            rms_stats(TW)
            for k in range(KC):
                i = k % 2
                S.op("dve", lambda e, k=k, i=i: e.scalar_tensor_tensor(
                    out=tmpf[i][:, :TW], in0=xt[:, k, :TW], scalar=gfin(k), in1=rstd[:, :TW], op0=ALU.mult, op1=ALU.mult),
                     reads=[B_KT, b_rstd, b_const], writes=[b_tmpf[i]])
                S.dma("pool", yT[k * 128:(k + 1) * 128, c0:c0 + TW], tmpf[i][:, :TW], reads=[b_tmpf[i]])

        def tile_pass(l):
            tiles = list(range(NT)) + ([NT] if cfg.sample else [])

            def load_x(ti, queue):
                TW = 512 if ti < NT else DEC_T
                c0 = ti * 512
                src = xT if l == 0 else xS
                S.dma(queue, xt[:, :, :TW], src[:, c0:c0 + TW].rearrange("(k p) t -> p k t", p=128),
                      reads=([b_xS[ti]] if l > 0 else []), writes=[B_KT])

            for idx, ti in enumerate(tiles):
                TW = 512 if ti < NT else DEC_T
                c0 = ti * 512
                if idx == 0 or l == L:
                    load_x(ti, "sync")
                if l > 0:
                    phase_c(l - 1, ti, TW, c0)
                if l < L:
                    S.dma("pool", xS[:, c0:c0 + TW].rearrange("(k p) t -> p k t", p=128), xt[:, :, :TW],
                          reads=[B_KT], writes=[b_xS[ti]])
                    nxt = tiles[idx + 1] if idx + 1 < len(tiles) else None
                    phase_a(l, ti, TW, c0,
                            after_h=(lambda nxt=nxt: load_x(nxt, "pool")) if nxt is not None else None)
                else:
                    final_norm(ti, TW, c0)

        def all_gather(src, dst, reads, bdst):
            S.cc_n += 1
            n = S.cc_n
            S.custom("pool", lambda e, src=src, dst=dst: e.collective_compute(
                "AllGather", ALU.bypass, replica_groups=[[0, 1, 2, 3], [4, 5, 6, 7]],
                ins=[src.ap().opt()], outs=[dst.ap().opt()]).then_inc(cc_sem, 1),
                     "cc", n, reads=reads, writes=[bdst])

        S.cc_n = 0

        def cumsum_tables(l):
            for ip in range(NT):
                for jp in range(4):
                    S.dma("sync", lnp[16 * ip + 4 * jp:16 * ip + 4 * jp + 4, :, :],
                          glf[l][jp * 4:(jp + 1) * 4, 512 * ip:512 * ip + 512].rearrange("h (s p) -> s h p", p=128),
                          reads=[b_glf[l]], writes=[b_lnp])
            cum_core(lnp, b_lnp, NB, ncb, b_ncb, NB)
            for h in range(4):
                for jp in range(4):
                    src = ncb[:, h, :NB].rearrange("p (i m) -> p i m", m=16)[:, :, 4 * jp:4 * jp + 4]
                    dst = xn[:, h * 32:h * 32 + NT * 4].rearrange("p (i s) -> p i s", s=4)
                    if jp == 0:
                        S.op("dve", lambda e, src=src, dst=dst: e.tensor_scalar(
                            out=dst, in0=src, scalar1=onehot[:, 0:1], scalar2=None, op0=ALU.mult),
                             reads=[b_ncb, b_const], writes=[b_xn])
                    else:
                        S.op("dve", lambda e, src=src, dst=dst, jp=jp: e.scalar_tensor_tensor(
                            out=dst, in0=src, scalar=onehot[:, jp:jp + 1], in1=dst, op0=ALU.mult, op1=ALU.add),
                             reads=[b_ncb, b_const, b_xn], writes=[b_xn])
            pb, bpb = pbank[6], b_pb[6]
            S.op("pe", lambda e: e.matmul(pb[:, 0:128], lhsT=sel127_f, rhs=xn[:, :], start=True, stop=True),
                 reads=[b_xn, b_const], writes=[bpb])
            S.op("act", lambda e: e.activation(out=ncref[:, :], in_=pb[:, 0:128], func=AF.Identity), reads=[bpb],
                 writes=[b_ncref])

        def cum_core(Lsrc, bL, nb, dst, bdst, nbp):
            pb, bpb = pbank[6], b_pb[6]
            pb2, bpb2 = pbank[7], b_pb[7]
            for h in range(4):
                S.op("pe", lambda e, h=h: e.transpose(pb[:, 0:nb], Lsrc[:nb, h, :], ident_f[:nb, :nb]),
                     reads=[bL, b_const], writes=[bpb])
                S.op("act", lambda e: e.activation(out=ltb[:, :nb], in_=pb[:, 0:nb], func=AF.Identity), reads=[bpb],
                     writes=[b_ltb])
                S.op("pe", lambda e: e.matmul(pb2[:nb, 0:128], lhsT=ltb[:, :nb], rhs=ones_f, start=True, stop=True),
                     reads=[b_ltb, b_const], writes=[bpb2])
                S.op("act", lambda e: e.activation(out=tott[:nb, :], in_=pb2[:nb, 0:128], func=AF.Identity), reads=[bpb2],
                     writes=[b_tott])
                S.op("pe", lambda e: e.matmul(pb[:, 128:128 + nb], lhsT=triinc_f, rhs=ltb[:, :nb], start=True, stop=False),
                     reads=[b_ltb, b_const], writes=[bpb])
                S.op("pe", lambda e: e.matmul(pb[:, 128:128 + nb], lhsT=tott[:nb, :], rhs=su_f[:nb, :nb], start=False,
                                              stop=True),
                     reads=[b_tott, b_const], writes=[bpb])
                S.op("act", lambda e, h=h: e.activation(out=dst[:, h, :nb], in_=pb[:, 128:128 + nb], func=AF.Identity,
                                                        scale=-1.0),
                     reads=[bpb], writes=[bdst])

        ectr = [0]
        wctr2 = [0]

        acc_t = [sb(f"acc{i}", [128, 512], BF16) for i in range(2)]
        b_acc = [Buf(f"acc{i}") for i in range(2)]
        rc = {"z": 0, "e": 0, "t": 0, "w": 0, "p": 0, "a": 0}

        def attn_tile(hh, QW, q_ap, bq, steps, bias_of, kind, ob=0, hook=None):
            Zb = (0, 1, 2)
            Pb = (6, 7)
            O, bO = pbank[3 + ob], b_pb[3 + ob]
            DEN, bDEN = pbank[6 + ob], b_pb[6 + ob]
            ns = len(steps)
            R = [dict() for _ in range(ns)]

            def qk(si):
                kt_ap, v_ap, KP, m_ap, rds = steps[si]
                zi = Zb[rc["z"] % 3]
                rc["z"] += 1
                R[si]["z"] = (pbank[zi], b_pb[zi])
                z = pbank[zi]
                S.op("pe", lambda e, z=z, kt_ap=kt_ap, KP=KP: e.matmul(z[:KP, :QW], lhsT=kt_ap, rhs=q_ap, start=True,
                                                                        stop=True),
                     reads=rds + [bq], writes=[b_pb[zi]])

            def pv(si):
                kt_ap, v_ap, KP, m_ap, rds = steps[si]
                wt, bw = R[si]["w"]
                first, last = si == 0, si == ns - 1
                S.op("pe", lambda e, wt=wt, v_ap=v_ap, KP=KP, first=first, last=last: e.matmul(
                    O[:, :QW], lhsT=v_ap, rhs=wt[:KP, :QW], start=first, stop=last),
                     reads=rds + [bw], writes=[bO])
                if kind == "fox":
                    S.op("pe", lambda e, wt=wt, KP=KP, first=first, last=last: e.matmul(
                        DEN[:, :QW], lhsT=ones_b[:KP, :], rhs=wt[:KP, :QW], start=first, stop=last),
                         reads=[bw, b_const], writes=[bDEN])

            def take_w(si):
                wi = rc["w"] % NW
                rc["w"] += 1
                R[si]["w"] = (w_t[wi], b_w[wi])
                return w_t[wi], b_w[wi]

            def fox_x(si):
                kt_ap, v_ap, KP, m_ap, rds = steps[si]
                z, bz = R[si]["z"]
                wt, bw = take_w(si)
                sw_ = min(QW, 256)
                for sq in range(max(QW // 256, 1)):
                    bcol, bbt = bias_of(si, sq)
                    S.op("act", lambda e, z=z, wt=wt, KP=KP, bcol=bcol, sq=sq, sw_=sw_: e.activation(
                        out=wt[:KP, sq * sw_:(sq + 1) * sw_], in_=z[:KP, sq * sw_:(sq + 1) * sw_], func=AF.Exp,
                        bias=bcol),
                         reads=[bz, bbt], writes=[bw])
                if m_ap is not None:
                    S.op("dve", lambda e, wt=wt, m_ap=m_ap, KP=KP: e.tensor_tensor(
                        out=wt[:KP, :QW], in0=wt[:KP, :QW], in1=m_ap, op=ALU.mult),
                         reads=[bw, b_const], writes=[bw])

            hook_at = min(3, ns - 1)
            if kind == "fox":
                qk(0)
                if ns > 1:
                    qk(1)
                for si in range(ns):
                    if si + 2 < ns:
                        qk(si + 2)
                    fox_x(si)
                    if si >= 1:
                        pv(si - 1)
                    if si == hook_at and hook is not None:
                        hook()
                pv(ns - 1)
                return

            def sb_e(si):
                kt_ap, v_ap, KP, m_ap, rds = steps[si]
                z, bz = R[si]["z"]
                ei = rc["e"] % NE
                rc["e"] += 1
                et, be, spt, bsp = e_t[ei], b_e[ei], sp_t[ei], b_sp[ei]
                R[si]["e"] = (et, be)
                R[si]["sp"] = (spt, bsp)
                S.op("act", lambda e, z=z, et=et, KP=KP: e.activation(out=et[:KP, :QW], in_=z[:KP, :QW], func=AF.Exp),
                     reads=[bz], writes=[be])
                if m_ap is not None:
                    S.op("dve", lambda e, et=et, m_ap=m_ap, KP=KP: e.tensor_tensor(
                        out=et[:KP, :QW], in0=et[:KP, :QW], in1=m_ap, op=ALU.mult),
                         reads=[be, b_const], writes=[be])

            def sb_l(si):
                kt_ap, v_ap, KP, m_ap, rds = steps[si]
                et, be = R[si]["e"]
                spt, bsp = R[si]["sp"]
                S.op("act", lambda e, et=et, spt=spt, KP=KP: e.activation(out=spt[:KP, :QW], in_=et[:KP, :QW],
                                                                          func=AF.Ln, bias=EPS_t[:KP, 1:2]),
                     reads=[be, b_const], writes=[bsp])

            def sb_p(si):
                kt_ap, v_ap, KP, m_ap, rds = steps[si]
                spt, bsp = R[si]["sp"]
                pi = Pb[rc["p"] % 2]
                rc["p"] += 1
                P, bP = pbank[pi], b_pb[pi]
                R[si]["p"] = (P, bP)
                S.op("pe", lambda e, spt=spt, KP=KP, P=P, si=si: e.matmul(
                    P[:KP, :QW], lhsT=tri_b[:KP, :KP], rhs=spt[:KP, :QW], start=True, stop=(si == 0)),
                     reads=[bsp, b_const], writes=[bP])
                if si > 0:
                    at, ba = R[si]["acc"]
                    S.op("pe", lambda e, at=at, KP=KP, P=P: e.matmul(
                        P[:KP, :QW], lhsT=ones_b[:, :KP], rhs=at[:, :QW], start=False, stop=True),
                         reads=[ba, b_const], writes=[bP])

            def sb_acc(si):
                kt_ap, v_ap, KP, m_ap, rds = steps[si]
                spt, bsp = R[si]["sp"]
                ai = rc["a"] % 2
                rc["a"] += 1
                an, ban = acc_t[ai], b_acc[ai]
                R[si + 1]["acc"] = (an, ban)
                if si == 0:
                    if KP < 128:
                        S.op("pool", lambda e, an=an: e.memset(an[:, :QW], 0.0), writes=[ban])
                    S.op("pool", lambda e, an=an, spt=spt, KP=KP: e.tensor_copy(out=an[:KP, :QW], in_=spt[:KP, :QW]),
                         reads=[bsp], writes=[ban])
                else:
                    ac, bac = R[si]["acc"]
                    if KP < 128:
                        raise NotImplementedError
                    S.op("pool", lambda e, an=an, ac=ac, spt=spt: e.tensor_tensor(
                        out=an[:, :QW], in0=ac[:, :QW], in1=spt[:, :QW], op=ALU.add),
                         reads=[bac, bsp], writes=[ban])

            def sb_tw(si):
                kt_ap, v_ap, KP, m_ap, rds = steps[si]
                et, be = R[si]["e"]
                P, bP = R[si]["p"]
                ti_ = rc["t"] % NE
                rc["t"] += 1
                tt, btt = t_t[ti_], b_t[ti_]
                wt, bw = take_w(si)
                S.op("act", lambda e, tt=tt, KP=KP, P=P: e.activation(out=tt[:KP, :QW], in_=P[:KP, :QW], func=AF.Exp,
                                                                     scale=-1.0),
                     reads=[bP], writes=[btt])
                S.op("dve", lambda e, wt=wt, et=et, tt=tt, KP=KP: e.tensor_tensor(
                    out=wt[:KP, :QW], in0=et[:KP, :QW], in1=tt[:KP, :QW], op=ALU.mult),
                     reads=[be, btt], writes=[bw])

            qk(0)
            sb_e(0)
            if ns > 1:
                qk(1)
                sb_e(1)
            sb_l(0)
            if ns > 1:
                sb_l(1)
            for si in range(ns):
                if si + 2 < ns:
                    qk(si + 2)
                    sb_e(si + 2)
                sb_p(si)
                if si + 1 < ns:
                    sb_acc(si)
                sb_tw(si)
                if si + 2 < ns:
                    sb_l(si + 2)
                if si >= 1:
                    pv(si - 1)
                if si == hook_at and hook is not None:
                    hook()
            pv(ns - 1)

        def epilogue(l, hh, QW, c0, ti, gate_ap, bgate, kind, ob=0):
            O, bO = pbank[3 + ob], b_pb[3 + ob]
            DEN, bDEN = pbank[6 + ob], b_pb[6 + ob]
            SS, bSS = pbank[5], b_pb[5]
            i = ectr[0] % 2
            if kind == "fox":
                S.op("dve", lambda e: e.reciprocal(out=tmpf[0][:, :QW], in_=DEN[:, :QW]), reads=[bDEN], writes=[b_tmpf[0]])
                S.op("dve", lambda e: e.tensor_tensor(out=tmpf[1][:, :QW], in0=O[:, :QW], in1=tmpf[0][:, :QW], op=ALU.mult),
                     reads=[bO, b_tmpf[0]], writes=[b_tmpf[1]])
                u_ap, bu = tmpf[1], b_tmpf[1]
            else:
                u_ap, bu = O, bO
            S.op("act", lambda e: e.activation(out=sqb[i][:, :QW], in_=u_ap[:, :QW], func=AF.Square), reads=[bu],
                 writes=[b_sqb[i]])
            S.op("pe", lambda e: e.matmul(SS[:, :QW], lhsT=ones_b, rhs=sqb[i][:, :QW], start=True, stop=True),
                 reads=[b_sqb[i], b_const], writes=[bSS])
            S.op("act", lambda e: e.activation(out=tmpf[0][:, :QW], in_=SS[:, :QW], func=AF.Sqrt, bias=EPS_t[:, 0:1],
                                               scale=1.0 / 128),
                 reads=[bSS, b_const], writes=[b_tmpf[0]])
            S.op("dve", lambda e: e.reciprocal(out=tmpf[0][:, :QW], in_=tmpf[0][:, :QW]), reads=[b_tmpf[0]],
                 writes=[b_tmpf[0]])
            S.op("dve", lambda e: e.scalar_tensor_tensor(out=tmpf[1][:, :QW], in0=u_ap[:, :QW], scalar=pcol(l, "g_o", hh),
                                                         in1=tmpf[0][:, :QW], op0=ALU.mult, op1=ALU.mult),
                 reads=[bu, b_tmpf[0], b_const], writes=[b_tmpf[1]])
            sf, bsf, sbb, bsb = stage()
            S.op("dve", lambda e, sbb=sbb: e.tensor_tensor(out=sbb[:, :QW], in0=tmpf[1][:, :QW], in1=gate_ap, op=ALU.mult),
                 reads=[b_tmpf[1], bgate], writes=[bsb])
            S.dma("pool", ogS[hh, :, c0:c0 + QW], sbb[:, :QW], reads=[bsb], writes=[b_ogS[hh][ti]])

        ob_ctr = [0]
        pend = [None]

        def next_ob():
            i = ob_ctr[0] % 2
            ob_ctr[0] += 1
            return i

        def phase_b(l):
            cumsum_tables(l)
            if cfg.sample:
                sample_tables(l)
            if getattr(cfg, "stop", 99) < 4:
                return
            for hh in range(NH if getattr(cfg, "stop", 99) >= 5 else 1):
                kind = "sb" if hh < 4 else "fox"
                if l + 1 < L and hh < 4:
                    cast_layer(l + 1, hh)
                qi = hh % 2
                def load_kv(hx, ip):
                    for jp in range(4):
                        n0 = 16 * ip + 4 * jp
                        S.dma("sync", KT[:, n0 * 128:(n0 + 4) * 128],
                              gK[l][ip][jp * 1024 + hx * 128: jp * 1024 + hx * 128 + 128, :],
                              reads=[b_gK[l][ip]], writes=[B_KT[ip]])
                        S.dma("sync", VV[:, n0 * 128:(n0 + 4) * 128].rearrange("p (s d) -> p s d", d=128),
                              gV[l][ip][jp * 512:(jp + 1) * 512, hx * 128:(hx + 1) * 128].rearrange("(s p) d -> p s d", p=128),
                              reads=[b_gV[l][ip]], writes=[B_VVg[ip]])

                for ip in (range(NT) if hh == 0 else range(1)):
                    load_kv(hh, ip)
                def fox_bias(ti):
                    h = hh - 4
                    bi = ti % 2
                    nkb_ = 16 * ti + 16
                    for sq in range(2):
                        cix = h * 32 + ti * 4 + 2 * sq
                        S.op("dve", lambda e, h=h, bi=bi, sq=sq, cix=cix, nkb_=nkb_: e.tensor_scalar(
                            out=bt[bi][:, sq, :nkb_], in0=ncb[:, h, :nkb_], scalar1=ncref[:, cix:cix + 1],
                            scalar2=60.0, op0=ALU.subtract, op1=ALU.min),
                             reads=[b_ncb, b_ncref], writes=[b_bt[bi]])

                if kind == "fox":
                    fox_bias(NT - 1)
                for ti in range(NT - 1, -1, -1):
                    nkb = 16 * ti + 16
                    order = list(range(nkb - 1, -1, -1)) if kind == "sb" else list(range(nkb))
                    steps = []
                    for n in order:
                        m = n - 16 * ti
                        m_ap = None
                        if m >= 0:
                            off = ((0 if kind == "sb" else 1) * 16 + m) * 512
                            m_ap = mask_t[:, off:off + 512]
                        steps.append((KT[:, n * 128:(n + 1) * 128], VV[:, n * 128:(n + 1) * 128], 128, m_ap,
                                      [B_KT[n // 16], B_VVg[n // 16]]))
                    bias_of = None
                    if kind == "fox":
                        bi = ti % 2
                        bias_of = lambda si, sq, bi=bi, order=order: (bt[bi][:, sq, order[si]:order[si] + 1], b_bt[bi])
                    qi = load_qg(hh, ti, ti * 512, 512)
                    ob = next_ob()
                    hk = pend[0]
                    pend[0] = None
                    if kind == "fox" and ti >= 1:
                        hk0 = hk
                        hk = lambda hk0=hk0, ti=ti: ((hk0() if hk0 is not None else None), fox_bias(ti - 1))
                    attn_tile(hh, 512, qh[qi][:, :], b_qh[qi], steps, bias_of, kind, ob, hook=hk)
                    pend[0] = (lambda l=l, hh=hh, ti=ti, qi=qi, kind=kind, ob=ob: epilogue(
                        l, hh, 512, ti * 512, ti, gh[qi][:, :], b_gh[qi], kind, ob))
                    if hh + 1 < NH and ti + 1 <= NT - 1:
                        load_kv(hh + 1, ti + 1)
                if cfg.sample:
                    sample_attn(l, hh, kind, qi)
            if pend[0] is not None:
                pend[0]()
                pend[0] = None

        def sample_tables(l):
            S.op("dve", lambda e: e.memset(lnps[:, :, :], 0.0), writes=[b_lnps])
            S.dma("sync", lnps[0:8, :, :], clf[l, :, :, :], reads=[], writes=[b_lnps])
            S.dma("sync", lnps[8:9, :, 0:DEC_T], slfs[:, :].rearrange("(o h) p -> o h p", o=1), reads=[b_slfs],
                  writes=[b_lnps])
            cum_core(lnps, b_lnps, 9, ncbs, b_ncbs, 9)

        def sample_attn(l, hh, kind, qi):
            c0 = NTOK
            QW = DEC_T
            S.dma("pool", kts[:, 0:PAST], ckT[l, hh, :, :], writes=[b_kts])
            S.dma("sync", kts[:, PAST:PAST + DEC_T], sks[hh, :, :], reads=[b_sks[hh]], writes=[b_kts])
            S.dma("pool", vs[:, 0:8, :], cv[l, :, hh * 128:(hh + 1) * 128].rearrange("(n p) d -> p n d", p=128),
                  writes=[b_vs])
            S.dma("sync", vs[0:DEC_T, 8, :], svs[:, hh * 128:(hh + 1) * 128], reads=[b_svs[hh]], writes=[b_vs])
            qi = load_qg(hh, NT, c0, QW)
            q_ap = qh[qi][:, :QW]
            mk = smask_t[0:DEC_T, (0 if kind == "sb" else 32):(0 if kind == "sb" else 32) + 32]
            blocks = [(kts[:, PAST:PAST + DEC_T], vs[0:DEC_T, 8, :], DEC_T, mk, [b_kts, b_vs])]
            for n in range(7, -1, -1):
                blocks.append((kts[:, n * 128:(n + 1) * 128], vs[:, n, :], 128, None, [b_kts, b_vs]))
            bias_of = None
            if kind == "fox":
                h = hh - 4
                pb, bpb = pbank[5], b_pb[5]
                S.op("pe", lambda e, h=h: e.matmul(pb[:, 0:16], lhsT=sel31_f, rhs=ncbs[:, h, :], start=True, stop=True),
                     reads=[b_ncbs, b_const], writes=[bpb])
                S.op("act", lambda e: e.activation(out=xs_ref[:, 0:16], in_=pb[:, 0:16], func=AF.Identity), reads=[bpb],
                     writes=[b_xsref])
                S.op("dve", lambda e, h=h: e.tensor_scalar(out=bt[0][:, 0, 0:9], in0=ncbs[:, h, 0:9], scalar1=xs_ref[:, 8:9],
                                                           scalar2=0.0, op0=ALU.subtract, op1=ALU.min),
                     reads=[b_ncbs, b_xsref], writes=[b_bt[0]])
                nidx = [8] + list(range(7, -1, -1))
                bias_of = lambda si, sq: (bt[0][:(DEC_T if si == 0 else 128), 0, nidx[si]:nidx[si] + 1], b_bt[0])
            ob = next_ob()
            hk = pend[0]
            pend[0] = None
            attn_tile(hh, QW, q_ap, b_qh[qi], blocks, bias_of, kind, ob, hook=hk)
            pend[0] = (lambda l=l, hh=hh, qi=qi, kind=kind, ob=ob: epilogue(
                l, hh, QW, c0, NT, gh[qi][:, :QW], b_gh[qi], kind, ob))

        stop = getattr(cfg, "stop", 99)
        if stop >= 2:
            for l in range(L):
                tile_pass(l)
                if stop >= 3:
                    phase_b(l)
            if stop >= 6:
                tile_pass(L)
        S.final_waits("sync")

        with nc.Block() as block:
            @block.tensor
            def _(e):
                for f in S.q["pe"]:
                    f(e)

            @block.scalar
            def _(e):
                for f in S.q["act"]:
                    f(e)

            @block.vector
            def _(e):
                for f in S.q["dve"]:
                    f(e)

            @block.gpsimd
            def _(e):
                for f in S.q["pool"]:
                    f(e)

            @block.sync
            def _(e):
                for f in S.q["sync"]:
                    f(e)
    return nc, S


def _consts():
    k = np.arange(128)
    ones = np.ones((128, 128), np.float32)
    tri = (k[:, None] >= k[None, :]).astype(np.float32)
    omt = 1.0 - tri
    cbf = np.concatenate([ones, tri, omt], axis=1).astype(ml_dtypes.bfloat16)
    ident = np.eye(128, dtype=np.float32)
    triinc = (k[:, None] <= k[None, :]).astype(np.float32)
    su = (k[:, None] < k[None, :]).astype(np.float32)
    sel = np.zeros((128, 128), np.float32)
    sel[127, :] = 1.0
    sel31 = np.zeros((128, 128), np.float32)
    sel31[31, :] = 1.0
    cf32 = np.concatenate([ident, ones, triinc, su, sel, sel31], axis=1).astype(np.float32)
    return cbf, cf32


def _masks(j):
    k = np.arange(128)
    out = np.zeros((128, 2, 16, 512), np.float32)
    tri_sb = (k[:, None] < k[None, :]).astype(np.float32)
    tri_fx = (k[:, None] <= k[None, :]).astype(np.float32)
    for kind, tri in enumerate((tri_sb, tri_fx)):
        for m in range(16):
            d = m - 4 * j
            for s in range(4):
                if d < s:
                    out[:, kind, m, s * 128:(s + 1) * 128] = 1.0
                elif d == s:
                    out[:, kind, m, s * 128:(s + 1) * 128] = tri
    return out.reshape(128, 2 * 16 * 512).astype(ml_dtypes.bfloat16)


def _smask():
    k = np.arange(32)
    out = np.zeros((128, 64), np.float32)
    out[:32, 0:32] = (k[:, None] < k[None, :])
    out[:32, 32:64] = (k[:, None] <= k[None, :])
    return out.astype(ml_dtypes.bfloat16)


_PROG_CACHE = {}


def run(cfg, inputs):
    NT, L, NTOK, NTOT = cfg.NT, cfg.DEPTH, cfg.NTOK, cfg.NTOT
    f32 = np.float32
    g = {k: np.asarray(v) for k, v in inputs.items()}
    key = (NT, L, cfg.sample)
    if key not in _PROG_CACHE:
        _PROG_CACHE[key] = build_program(cfg)
    nc, S = _PROG_CACHE[key]

    w_in = g["w_in"].astype(f32, copy=False)
    w_in_r = w_in[:, :, :4096].reshape(L, KC, 128, 32, 128).transpose(0, 3, 2, 1, 4).reshape(L * 32 * 128 * 2, 1024)
    w_f_r = w_in[:, :, 4096:4100].reshape(L, KC, 128, 4).transpose(0, 2, 1, 3).reshape(L * 128, 64)
    w_out_r = g["w_out"].reshape(L, 8, 128, 16, 128).transpose(0, 3, 2, 1, 4).reshape(L * 16 * 128, 1024)
    w_g_r = g["w_ple_gate"].reshape(L, KC, 128, 16, 128).transpose(0, 3, 2, 1, 4).reshape(L * 16 * 128 * 2, 1024)
    w_ple_r = g["w_ple"].reshape(L, 2, 128, 16, 128).transpose(0, 3, 2, 1, 4).reshape(L * 16 * 128, 256)
    w_in_r, w_f_r, w_out_r, w_g_r, w_ple_r = (np.ascontiguousarray(a, dtype=f32) for a in
                                              (w_in_r, w_f_r, w_out_r, w_g_r, w_ple_r))
    cbf, cf32 = _consts()
    smask = _smask()
    NPAR = L * 48 + 16 + 8
    in_maps = []
    for c in range(8):
        b, j = c // 4, c % 4
        tiles = [4 * i + j for i in range(NT)]
        xp = g["x_prompt"][b].reshape(4 * NT, 512, D)[tiles].reshape(NTOK, D)
        xs = g["x_sample"][c]
        xTc = np.ascontiguousarray(np.concatenate([xp, xs], axis=0).T, dtype=f32)
        pp = g["p_prompt"][:, b].reshape(L, 4 * NT, 512, PLE)[:, tiles].reshape(L, NTOK, PLE)
        pTc = np.ascontiguousarray(np.concatenate([pp, g["p_sample"][:, c]], axis=1).transpose(0, 2, 1), dtype=f32)
        par = np.zeros((128, NPAR), f32)
        for l in range(L):
            par[:, l * 48: l * 48 + 16] = g["g_attn_norm"][l].reshape(KC, 128).T
            par[:, l * 48 + 16: l * 48 + 32] = g["g_ple_norm"][l].reshape(KC, 128).T
            par[:, l * 48 + 32: l * 48 + 36] = g["g_out_sb"][l].reshape(4, 128).T
            par[:, l * 48 + 36: l * 48 + 40] = g["g_out_fox"][l].reshape(4, 128).T
            par[0:4, l * 48 + 40] = -g["b_forget"][l]
        par[:, L * 48: L * 48 + 16] = g["g_final"].reshape(KC, 128).T
        par[:, L * 48 + 16 + j] = 1.0
        m = {"xT": xTc, "pT": pTc, "w_in_r": w_in_r, "w_f_r": w_f_r, "w_out_r": w_out_r, "w_g_r": w_g_r,
             "w_ple_r": w_ple_r, "params": par, "cbf": cbf, "cf32": cf32, "masks": _masks(j), "smask": smask}
        if cfg.sample:
            ck = np.concatenate([g["cache_sb_k"][:, c], g["cache_fox_k"][:, c]], axis=2)
            m["ckT"] = np.ascontiguousarray(ck.transpose(0, 2, 3, 1), dtype=f32)
            cvv = np.concatenate([g["cache_sb_v"][:, c], g["cache_fox_v"][:, c]], axis=2)
            m["cv"] = np.ascontiguousarray(cvv.reshape(L, PAST, NH * HD), dtype=f32)
            m["clf"] = np.ascontiguousarray(g["cache_fox_logf"][:, c].reshape(L, 8, 128, 4).transpose(0, 1, 3, 2), dtype=f32)
        in_maps.append(m)

    res = run_bass_kernel_spmd(nc, in_maps, core_ids=list(range(8)), trace=getattr(cfg, "trace", False))
    R = res.results
    global LAST_R, LAST_RES
    LAST_R = R
    LAST_RES = res
    SEQ = cfg.SEQ
    y_p = np.zeros((2, SEQ, D), f32)
    y_s = np.zeros((8, DEC_T, D), f32)
    kp = np.zeros((L, 2, SEQ, NH, HD), f32)
    vp = np.zeros((L, 2, SEQ, NH, HD), f32)
    lp = np.zeros((L, 2, SEQ, 4), f32)
    ks = np.zeros((L, 8, DEC_T, NH, HD), f32)
    vsa = np.zeros((L, 8, DEC_T, NH, HD), f32)
    ls = np.zeros((L, 8, DEC_T, 4), f32)
    for c in range(8):
        b, j = c // 4, c % 4
        r = R[c]
        yt = r["yT"].T
        kt = r["kT_out"].transpose(0, 3, 1, 2)
        vt = r["v_out"].reshape(L, NTOT, NH, HD)
        lt = r["lf_out"].transpose(0, 2, 1)
        for i in range(NT):
            gt = 4 * i + j
            y_p[b, gt * 512:(gt + 1) * 512] = yt[i * 512:(i + 1) * 512]
            kp[:, b, gt * 512:(gt + 1) * 512] = kt[:, i * 512:(i + 1) * 512]
            vp[:, b, gt * 512:(gt + 1) * 512] = vt[:, i * 512:(i + 1) * 512]
            lp[:, b, gt * 512:(gt + 1) * 512] = lt[:, i * 512:(i + 1) * 512]
        y_s[c] = yt[NTOK:]
        ks[:, c] = kt[:, NTOK:]
        vsa[:, c] = vt[:, NTOK:]
        ls[:, c] = lt[:, NTOK:]
    return (y_p, y_s, kp[..., 0:4, :], kp[..., 4:8, :], vp[..., 0:4, :], vp[..., 4:8, :], lp,
            ks[..., 0:4, :], ks[..., 4:8, :], vsa[..., 0:4, :], vsa[..., 4:8, :], ls)


def kernel(**inputs):
    cfg = Cfg(NT=8, DEPTH=4, sample=True)
    outs = run(cfg, inputs)
    y_p, y_s, skp, fkp, svp, fvp, lp, sks, fks, svs, fvs, ls = outs
    c = np.ascontiguousarray
    return (c(y_p), c(y_s), c(skp), c(svp), c(fkp), c(fvp), c(lp), c(sks), c(svs), c(fks), c(fvs), c(ls))
```

```python
import contextlib
import numpy as np
import ml_dtypes
import concourse.bass as bass
import concourse.mybir as mybir
from concourse.bass_utils import run_bass_kernel_spmd

F32 = mybir.dt.float32
BF16 = mybir.dt.bfloat16
AF = mybir.ActivationFunctionType
ALU = mybir.AluOpType

D = 2048
KC = 16
HD = 128
NH = 8
PLE = 256
EPS = 1e-6
DEC_T = 32
PAST = 1024
QSCALE = HD ** -0.5


class Cfg:
    def __init__(self, NT=8, DEPTH=4, sample=True):
        self.NT = NT
        self.DEPTH = DEPTH
        self.sample = sample
        self.NTOK = NT * 512
        self.NTOT = self.NTOK + DEC_T
        self.NB = NT * 16
        self.SEQ = NT * 4 * 512


class Buf:
    __slots__ = ("name", "w", "r", "excl")

    def __init__(self, name, excl=False):
        self.name = name
        self.w = None
        self.r = {}
        self.excl = excl


ENGS = ("pe", "act", "dve", "pool", "sync")


def _flat(x):
    out = []
    for b in x:
        if isinstance(b, (list, tuple)):
            out.extend(_flat(b))
        else:
            out.append(b)
    return out


class Sched:
    def __init__(self, nc, sems, dsems):
        self.nc = nc
        self.sem = sems
        self.q = {e: [] for e in ENGS}
        self.cnt = {e: 0 for e in ENGS}
        self.waited = {e: {} for e in ENGS}
        self.dsem = dsems
        self.duse = {k: [0] * len(v) for k, v in dsems.items()}
        self.dnext = {k: 0 for k in dsems}
        self.n_ops = 0
        self.n_waits = 0

    def _semobj(self, key):
        if isinstance(key, tuple):
            return self.dsem[key[0]][key[1]]
        return self.sem[key]

    def _collect(self, reads, writes):
        need = {}

        def add(tok):
            if tok is None:
                return
            k, v = tok
            if need.get(k, 0) < v:
                need[k] = v

        for b in reads:
            add(b.w)
        for b in writes:
            add(b.w)
            for k, v in b.r.items():
                add((k, v))
        return need

    def _emit_waits(self, eng, need):
        wl = []
        wd = self.waited[eng]
        for k, v in need.items():
            if eng == "pe" and k == "pe":
                continue
            if wd.get(k, 0) >= v:
                continue
            wd[k] = v
            wl.append((self._semobj(k), v))
        return wl

    def _commit(self, tok, reads, writes):
        k, v = tok
        for b in reads:
            if b.r.get(k, 0) < v:
                b.r[k] = v
        for b in writes:
            b.w = tok
            b.r = {}

    def op(self, eng, fn, reads=(), writes=()):
        reads, writes = _flat(reads), _flat(writes)
        ex = [b for b in reads if b.excl]
        if ex:
            writes = list(writes) + ex
        need = self._collect(reads, writes)
        wl = self._emit_waits(eng, need)
        self.cnt[eng] += 1
        tok = (eng, self.cnt[eng])
        sem = self.sem[eng]
        self.n_ops += 1
        self.n_waits += len(wl)

        def run(e, wl=wl, fn=fn, sem=sem):
            for s, v in wl:
                e.wait_ge(s, v)
            fn(e).then_inc(sem, 1)

        self.q[eng].append(run)
        self._commit(tok, reads, writes)

    def dma(self, queue, out, in_, reads=(), writes=(), **kw):
        eng = "pool" if queue == "cast" else queue
        reads, writes = _flat(reads), _flat(writes)
        pool = self.dsem[queue]
        i = self.dnext[queue]
        self.dnext[queue] = (i + 1) % len(pool)
        self.duse[queue][i] += 1
        use = self.duse[queue][i]
        key = (queue, i)
        need = self._collect(reads, writes)
        if use > 1:
            if need.get(key, 0) < 16 * (use - 1):
                need[key] = 16 * (use - 1)
        wl = self._emit_waits(eng, need)
        sem = pool[i]
        self.n_ops += 1
        self.n_waits += len(wl)

        def run(e, wl=wl, sem=sem, out=out, in_=in_, kw=kw):
            for s, v in wl:
                e.wait_ge(s, v)
            e.dma_start(out=out, in_=in_, **kw).then_inc(sem, 16)

        self.q[eng].append(run)
        self._commit((key, 16 * use), reads, writes)

    def custom(self, eng, fn, semobj_key, val, reads=(), writes=()):
        reads, writes = _flat(reads), _flat(writes)
        need = self._collect(reads, writes)
        wl = self._emit_waits(eng, need)

        def run(e, wl=wl, fn=fn):
            for s, v in wl:
                e.wait_ge(s, v)
            fn(e)

        self.q[eng].append(run)
        self._commit((semobj_key, val), reads, writes)

    def final_waits(self, eng):
        wl = []
        for qn, pool in self.dsem.items():
            for i, s in enumerate(pool):
                if self.duse[qn][i] > 0:
                    wl.append((s, 16 * self.duse[qn][i]))

        def run(e, wl=wl):
            for s, v in wl:
                e.wait_ge(s, v)

        self.q[eng].append(run)


def build_program(cfg):
    NT, L, NTOK, NTOT, NB = cfg.NT, cfg.DEPTH, cfg.NTOK, cfg.NTOT, cfg.NB
    nc = bass.Bass("TRN2", target_bir_lowering=False)

    def din(name, shape, dt=F32):
        return nc.dram_tensor(name, list(shape), dt, kind="ExternalInput")

    def dout(name, shape, dt=F32):
        return nc.dram_tensor(name, list(shape), dt, kind="ExternalOutput")

    def dint(name, shape, dt):
        return nc.dram_tensor(name, list(shape), dt)

    xT = din("xT", [D, NTOT])
    pT = din("pT", [L, PLE, NTOT])
    w_in_r = din("w_in_r", [L * 32 * 128 * 2, 1024])
    w_f_r = din("w_f_r", [L * 128, 64])
    w_out_r = din("w_out_r", [L * 16 * 128, 1024])
    w_g_r = din("w_g_r", [L * 16 * 128 * 2, 1024])
    w_ple_r = din("w_ple_r", [L * 16 * 128, 256])
    NPAR = L * 48 + 16 + 8
    params = din("params", [128, NPAR])
    cbf = din("cbf", [128, 384], BF16)
    cf32 = din("cf32", [128, 6 * 128])
    masks = din("masks", [128, 2 * 16 * 512], BF16)
    smask = din("smask", [128, 2 * 32], BF16)
    if cfg.sample:
        ckT = din("ckT", [L, NH, HD, PAST])
        cv = din("cv", [L, PAST, NH * HD])
        clf = din("clf", [L, 8, 4, 128])

    yT = dout("yT", [D, NTOT])
    kT_out = dout("kT_out", [L, NH, HD, NTOT])
    v_out = dout("v_out", [L, NTOT, NH * HD])
    lf_out = dout("lf_out", [L, 4, NTOT])

    wb_in = dint("wb_in", [L * 32 * 128 * 2, 1024], BF16)
    wb_f = dint("wb_f", [L * 128, 64], BF16)
    wb_out = dint("wb_out", [L * 16 * 128, 1024], BF16)
    wb_g = dint("wb_g", [L * 16 * 128 * 2, 1024], BF16)
    wb_ple = dint("wb_ple", [L * 16 * 128, 256], BF16)
    xS = dint("xS", [D, NTOT], F32)
    qS = dint("qS", [NH, HD, NTOT], BF16)
    gS = dint("gS", [NH, HD, NTOT], BF16)
    ogS = dout("ogS", [NH, HD, NTOT], BF16) if getattr(cfg, "debug", False) else dint("ogS", [NH, HD, NTOT], BF16)
    skK = [[dint(f"skK{l}_{t}", [1024, 512], BF16) for t in range(NT)] for l in range(L)]
    skV = [[dint(f"skV{l}_{t}", [512, 1024], BF16) for t in range(NT)] for l in range(L)]
    gK = [[dint(f"gK{l}_{t}", [4 * 1024, 512], BF16) for t in range(NT)] for l in range(L)]
    gV = [[dint(f"gV{l}_{t}", [4 * 512, 1024], BF16) for t in range(NT)] for l in range(L)]
    slf = [dint(f"slf{l}", [4, NTOK], F32) for l in range(L)]
    glf = [dint(f"glf{l}", [16, NTOK], F32) for l in range(L)]
    sks = dint("sks", [NH, HD, DEC_T], BF16)
    svs = dint("svs", [DEC_T, NH * HD], BF16)
    slfs = dint("slfs", [4, DEC_T], F32)

    es = contextlib.ExitStack()
    with es:
        def sb(name, shape, dt):
            return es.enter_context(nc.sbuf_tensor(name, list(shape), dt))

        def ps(name):
            return es.enter_context(nc.psum_tensor(name, [128, 512], F32))

        sems = {e: es.enter_context(nc.semaphore("s_" + e)) for e in ("pe", "act", "dve", "pool")}
        dsems = {
            "sync": [es.enter_context(nc.semaphore(f"ds{i}")) for i in range(24)],
            "pool": [es.enter_context(nc.semaphore(f"dp{i}")) for i in range(12)],
            "cast": [es.enter_context(nc.semaphore(f"dc{i}")) for i in range(3)],
        }
        cc_sem = es.enter_context(nc.semaphore("cc"))
        sems["cc"] = cc_sem
        S = Sched(nc, sems, dsems)

        BIG = sb("BIG", [128, 32768], BF16)
        KT = BIG[:, 0:16384]
        VV = BIG[:, 16384:32768]
        xt = BIG[:, 0:16384].bitcast(F32).rearrange("p (k t) -> p k t", t=512)
        hT = BIG[:, 16384:24576].rearrange("p (k t) -> p k t", t=512)
        B_KT = [Buf(f"KT{g}") for g in range(8)]
        B_VVg = [Buf(f"VV{g}") for g in range(8)]
        B_VV0, B_VV1 = B_VVg[0:4], B_VVg[4:8]
        NWS = 6
        wring = [sb(f"w{i}", [128, 2048], BF16) for i in range(NWS)]
        b_wring = [Buf(f"w{i}") for i in range(NWS)]
        ogt = sb("ogt", [128, 8, 512], BF16)
        b_ogt = Buf("ogt")
        ptb = sb("ptb", [128, 2, 512], BF16)
        b_ptb = Buf("ptb")
        mask_t = sb("mask_t", [128, 2 * 16 * 512], BF16)
        smask_t = sb("smask_t", [128, 64], BF16)
        cb = sb("cb", [128, 384], BF16)
        cf = sb("cf", [128, 768], F32)
        par = sb("par", [128, NPAR], F32)
        b_const = Buf("const")
        ones_b, tri_b, omt_b = cb[:, 0:128], cb[:, 128:256], cb[:, 256:384]
        ident_f, ones_f, triinc_f, su_f, sel127_f, sel31_f = (cf[:, i * 128:(i + 1) * 128] for i in range(6))
        NST = 3
        stf = [sb(f"stf{i}", [128, 512], F32) for i in range(NST)]
        b_stf = [Buf(f"stf{i}") for i in range(NST)]
        stb = [sb(f"stb{i}", [128, 512], BF16) for i in range(NST)]
        b_stb = [Buf(f"stb{i}") for i in range(NST)]
        sqb = [sb(f"sqb{i}", [128, 512], BF16) for i in range(2)]
        b_sqb = [Buf(f"sqb{i}") for i in range(2)]
        rstd = sb("rstd", [128, 512], F32)
        b_rstd = Buf("rstd")
        tmpf = [sb(f"tmpf{i}", [128, 512], F32) for i in range(2)]
        b_tmpf = [Buf(f"tmpf{i}") for i in range(2)]
        qh = [sb(f"qh{i}", [128, 512], BF16) for i in range(2)]
        b_qh = [Buf(f"qh{i}") for i in range(2)]
        gh = [sb(f"gh{i}", [128, 512], BF16) for i in range(2)]
        b_gh = [Buf(f"gh{i}") for i in range(2)]
        qg_ctr = [0]

        def load_qg(hh, ti, c0, QW):
            i = qg_ctr[0] % 2
            qg_ctr[0] += 1
            S.dma("sync", qh[i][:, :QW], qS[hh, :, c0:c0 + QW], reads=[b_qS[hh][ti]], writes=[b_qh[i]])
            S.dma("sync", gh[i][:, :QW], gS[hh, :, c0:c0 + QW], reads=[b_gS[hh][ti]], writes=[b_gh[i]])
            return i
        NE = 4
        e_t = [sb(f"e{i}", [128, 512], F32) for i in range(NE)]
        b_e = [Buf(f"e{i}") for i in range(NE)]
        sp_t = [sb(f"sp{i}", [128, 512], BF16) for i in range(NE)]
        b_sp = [Buf(f"sp{i}") for i in range(NE)]
        t_t = [sb(f"t{i}", [128, 512], F32) for i in range(NE)]
        b_t = [Buf(f"t{i}") for i in range(NE)]
        NW = 3
        w_t = [sb(f"wt{i}", [128, 512], BF16) for i in range(NW)]
        b_w = [Buf(f"wt{i}") for i in range(NW)]
        ncb = sb("ncb", [128, 4, 128], F32)
        b_ncb = Buf("ncb")
        lnp = sb("lnp", [128, 4, 128], F32)
        b_lnp = Buf("lnp")
        ltb = sb("ltb", [128, 128], F32)
        b_ltb = Buf("ltb")
        tott = sb("tott", [128, 128], F32)
        b_tott = Buf("tott")
        xn = sb("xn", [128, 128], F32)
        b_xn = Buf("xn")
        ncref = sb("ncref", [128, 128], F32)
        b_ncref = Buf("ncref")
        bt = [sb(f"bt{i}", [128, 4, 128], F32) for i in range(2)]
        b_bt = [Buf(f"bt{i}") for i in range(2)]
        lfst = sb("lfst", [4, 512], F32)
        b_lfst = Buf("lfst")
        kts = sb("kts", [128, PAST + DEC_T], BF16)
        b_kts = Buf("kts")
        vs = sb("vs", [128, 9, 128], BF16)
        b_vs = Buf("vs")
        lnps = sb("lnps", [128, 4, 128], F32)
        b_lnps = Buf("lnps")
        ncbs = sb("ncbs", [128, 4, 16], F32)
        b_ncbs = Buf("ncbs")
        xs_ref = sb("xs_ref", [128, 16], F32)
        b_xsref = Buf("xs_ref")

        pbank = [ps(f"pb{i}") for i in range(8)]
        b_pb = [Buf(f"pb{i}", excl=True) for i in range(8)]

        b_xS = [Buf(f"xS{t}") for t in range(NT + 1)]
        b_qS = [[Buf(f"qS{h}_{t}") for t in range(NT + 1)] for h in range(NH)]
        b_gS = [[Buf(f"gS{h}_{t}") for t in range(NT + 1)] for h in range(NH)]
        b_ogS = [[Buf(f"ogS{h}_{t}") for t in range(NT + 1)] for h in range(NH)]
        b_skK = [[[] for _ in range(NT)] for _ in range(L)]
        b_skV = [[[] for _ in range(NT)] for _ in range(L)]
        b_gK = [[Buf(f"gK{l}_{t}") for t in range(NT)] for l in range(L)]
        b_gV = [[Buf(f"gV{l}_{t}") for t in range(NT)] for l in range(L)]
        b_slf = [[] for _ in range(L)]
        b_glf = [Buf(f"glf{l}") for l in range(L)]
        b_wb = {}
        b_sks, b_svs, b_slfs = [Buf(f"sks{h}") for h in range(NH)], [Buf(f"svs{h}") for h in range(NH)], Buf("slfs")

        def pcol(l, what, k=None):
            base = l * 48
            if what == "g_attn":
                return par[:, base + k: base + k + 1]
            if what == "g_ple":
                return par[:, base + 16 + k: base + 16 + k + 1]
            if what == "g_o":
                return par[:, base + 32 + k: base + 32 + k + 1]
            if what == "nbf":
                return par[0:4, base + 40: base + 41]
            raise KeyError(what)

        def gfin(k):
            return par[:, L * 48 + k: L * 48 + k + 1]

        onehot = par[:, L * 48 + 16: L * 48 + 20]

        S.dma("sync", cb[:], cbf[:, :], writes=[b_const])
        S.dma("sync", cf[:], cf32[:, :], writes=[b_const])
        S.dma("sync", par[:], params[:, :], writes=[b_const])
        S.dma("sync", mask_t[:], masks[:, :], writes=[b_const])
        S.dma("sync", smask_t[:], smask[:, :], writes=[b_const])

        def cast_rows(dst, src, r0, r1, key, step=2048):
            bl = []
            for a in range(r0, r1, step):
                b = min(a + step, r1)
                bb = Buf(f"wb_{key}_{a}")
                S.dma("cast", dst[a:b, :], src[a:b, :], writes=[bb])
                bl.append(bb)
            return bl

        def cast_layer(l, part):
            if part == 0:
                b_wb[("in", l)] = cast_rows(wb_in, w_in_r, l * 8192, l * 8192 + 4096, f"in{l}")
                b_wb[("f", l)] = cast_rows(wb_f, w_f_r, l * 128, (l + 1) * 128, f"f{l}")
            elif part == 1:
                b_wb[("in", l)] += cast_rows(wb_in, w_in_r, l * 8192 + 4096, (l + 1) * 8192, f"in{l}b")
            elif part == 2:
                b_wb[("out", l)] = cast_rows(wb_out, w_out_r, l * 2048, (l + 1) * 2048, f"out{l}")
                b_wb[("ple", l)] = cast_rows(wb_ple, w_ple_r, l * 2048, (l + 1) * 2048, f"ple{l}")
            elif part == 3:
                b_wb[("g", l)] = cast_rows(wb_g, w_g_r, l * 4096, (l + 1) * 4096, f"g{l}")

        for part in range(4):
            cast_layer(0, part)

        wctr = [0]

        def load_w(kind, l, c):
            i = wctr[0] % NWS
            wctr[0] += 1
            slot, bs = wring[i], b_wring[i]
            if kind == "in":
                r = (l * 32 + c) * 256
                S.dma("sync", slot[:, :], wb_in[r:r + 256, :].rearrange("(p t) x -> p (t x)", t=2),
                      reads=b_wb[("in", l)], writes=[bs])
                return slot[:, :].rearrange("p (k n) -> p k n", n=128), bs
            if kind == "g":
                r = (l * 16 + c) * 256
                S.dma("sync", slot[:, :], wb_g[r:r + 256, :].rearrange("(p t) x -> p (t x)", t=2),
                      reads=b_wb[("g", l)], writes=[bs])
                return slot[:, :].rearrange("p (k n) -> p k n", n=128), bs
            if kind == "out":
                r = (l * 16 + c) * 128
                S.dma("sync", slot[:, 0:1024], wb_out[r:r + 128, :], reads=b_wb[("out", l)], writes=[bs])
                return slot[:, 0:1024].rearrange("p (k n) -> p k n", n=128), bs
            if kind == "ple":
                r = (l * 16 + c) * 128
                S.dma("sync", slot[:, 0:256], wb_ple[r:r + 128, :], reads=b_wb[("ple", l)], writes=[bs])
                return slot[:, 0:256].rearrange("p (k n) -> p k n", n=128), bs
            if kind == "f":
                S.dma("sync", slot[:, 0:64], wb_f[l * 128:(l + 1) * 128, :], reads=b_wb[("f", l)], writes=[bs])
                return slot[:, 0:64].rearrange("p (k n) -> p k n", n=4), bs
            raise KeyError(kind)

        pa_ctr = [0]

        def pa_bank():
            i = pa_ctr[0] % 3
            pa_ctr[0] += 1
            return pbank[i], b_pb[i]

        st_ctr = [0]

        def stage():
            i = st_ctr[0] % NST
            st_ctr[0] += 1
            return stf[i], b_stf[i], stb[i], b_stb[i]

        def rms_stats(TW):
            acc, bacc = pbank[5], b_pb[5]
            for k in range(KC):
                i = k % 2
                S.op("act", lambda e, k=k, i=i: e.activation(out=sqb[i][:, :TW], in_=xt[:, k, :TW], func=AF.Square),
                     reads=[B_KT], writes=[b_sqb[i]])
                S.op("pe", lambda e, k=k, i=i: e.matmul(acc[:, :TW], lhsT=ones_b, rhs=sqb[i][:, :TW],
                                                        start=(k == 0), stop=(k == KC - 1)),
                     reads=[b_sqb[i], b_const], writes=[bacc])
            S.op("act", lambda e: e.activation(out=rstd[:, :TW], in_=acc[:, :TW], func=AF.Sqrt, bias=EPS_t[:, 0:1],
                                               scale=1.0 / D),
                 reads=[bacc, b_const], writes=[b_rstd])
            S.op("dve", lambda e: e.reciprocal(out=rstd[:, :TW], in_=rstd[:, :TW]), reads=[b_rstd], writes=[b_rstd])

        def make_h(TW, gname, l):
            for k in range(KC):
                g = pcol(l, gname, k) if gname != "fin" else gfin(k)
                S.op("dve", lambda e, k=k, g=g: e.scalar_tensor_tensor(
                    out=hT[:, k, :TW], in0=xt[:, k, :TW], scalar=g, in1=rstd[:, :TW], op0=ALU.mult, op1=ALU.mult),
                     reads=[B_KT, b_rstd, b_const], writes=[B_VV0])

        S.op("dve", lambda e: e.memset(xn[:, :], 0.0), writes=[b_xn])
        EPS_t = sb("eps_t", [128, 2], F32)
        S.op("dve", lambda e: e.memset(EPS_t[:, 0:1], EPS), writes=[b_const])
        S.op("dve", lambda e: e.memset(EPS_t[:, 1:2], 1.0), writes=[b_const])

        def phase_c(l, ti, TW, c0):
            S.dma("sync", ogt[:, :, :TW], ogS[:, :, c0:c0 + TW].rearrange("h p t -> p h t"),
                  reads=[b_ogS[h][ti] for h in range(NH)], writes=[b_ogt])
            S.dma("pool", ptb[:, :, :TW], pT[l, :, c0:c0 + TW].rearrange("(k p) t -> p k t", p=128),
                  writes=[b_ptb])
            for c in range(KC):
                wv, bw = load_w("out", l, c)
                pb, bpb = pa_bank()
                for k in range(8):
                    S.op("pe", lambda e, k=k, wv=wv, pb=pb: e.matmul(pb[:, :TW], lhsT=wv[:, k, :], rhs=ogt[:, k, :TW],
                                                                      start=(k == 0), stop=(k == 7)),
                         reads=[bw, b_ogt], writes=[bpb])
                S.op("dve", lambda e, c=c, pb=pb: e.tensor_tensor(out=xt[:, c, :TW], in0=pb[:, :TW], in1=xt[:, c, :TW],
                                                                  op=ALU.add),
                     reads=[bpb, B_KT], writes=[B_KT])
            rms_stats(TW)
            make_h(TW, "g_ple", l)
            for c in range(KC):
                wv, bw = load_w("g", l, c)
                pb, bpb = pa_bank()
                for k in range(KC):
                    S.op("pe", lambda e, k=k, wv=wv, pb=pb: e.matmul(pb[:, :TW], lhsT=wv[:, k, :], rhs=hT[:, k, :TW],
                                                                      start=(k == 0), stop=(k == KC - 1)),
                         reads=[bw, B_VV0], writes=[bpb])
                i = c % 2
                S.op("act", lambda e, pb=pb, i=i: e.activation(out=tmpf[i][:, :TW], in_=pb[:, :TW], func=AF.Sigmoid),
                     reads=[bpb], writes=[b_tmpf[i]])
                wv2, bw2 = load_w("ple", l, c)
                pb2, bpb2 = pa_bank()
                for k in range(2):
                    S.op("pe", lambda e, k=k, wv2=wv2, pb2=pb2: e.matmul(pb2[:, :TW], lhsT=wv2[:, k, :],
                                                                          rhs=ptb[:, k, :TW], start=(k == 0), stop=(k == 1)),
                         reads=[bw2, b_ptb], writes=[bpb2])
                S.op("dve", lambda e, pb2=pb2, i=i: e.tensor_tensor(out=tmpf[i][:, :TW], in0=pb2[:, :TW],
                                                                    in1=tmpf[i][:, :TW], op=ALU.mult),
                     reads=[bpb2, b_tmpf[i]], writes=[b_tmpf[i]])
                S.op("dve", lambda e, c=c, i=i: e.tensor_tensor(out=xt[:, c, :TW], in0=xt[:, c, :TW],
                                                                in1=tmpf[i][:, :TW], op=ALU.add),
                     reads=[b_tmpf[i], B_KT], writes=[B_KT])

        def phase_a(l, ti, TW, c0, after_h=None):
            is_s = ti == NT
            AP_ = getattr(cfg, "aparts", 255)
            rms_stats(TW)
            make_h(TW, "g_attn", l)
            if after_h is not None:
                after_h()
            if not AP_ & 1:
                return
            nsub = max(TW // 128, 1)
            SW = min(TW, 128)
            pdma = S.dma if AP_ & 8 else (lambda *a, **k: None)
            kinds_ok = getattr(cfg, 'kinds', 'qkvz')
            for c in range(32):
                grp, hh4 = c // 4, c % 4
                kind = ("q", "k", "v", "z")[grp % 4]
                if kind not in kinds_ok:
                    continue
                hh = hh4 + (4 if grp >= 4 else 0)
                wv, bw = load_w("in", l, c)
                pb, bpb = pa_bank()
                sf, bsf, sbb, bsb = stage()
                if kind != "v":
                    for k in range(KC):
                        S.op("pe", lambda e, k=k, wv=wv, pb=pb: e.matmul(pb[:, :TW], lhsT=wv[:, k, :], rhs=hT[:, k, :TW],
                                                                          start=(k == 0), stop=(k == KC - 1)),
                             reads=[bw, B_VV0], writes=[bpb])
                else:
                    for s in range(nsub):
                        for k in range(KC):
                            S.op("pe", lambda e, k=k, s=s, wv=wv, pb=pb: e.matmul(
                                pb[:SW, s * 128:(s + 1) * 128], lhsT=hT[:, k, s * 128:s * 128 + SW], rhs=wv[:, k, :],
                                start=(k == 0), stop=(k == KC - 1)),
                                 reads=[bw, B_VV0], writes=[bpb])
                if kind == "q":
                    S.op("act", lambda e, pb=pb, sbb=sbb: e.activation(out=sbb[:, :TW], in_=pb[:, :TW], func=AF.Identity,
                                                                         scale=QSCALE),
                         reads=[bpb], writes=[bsb])
                    pdma("pool", qS[hh, :, c0:c0 + TW], sbb[:, :TW], reads=[bsb], writes=[b_qS[hh][ti]])
                elif kind == "z":
                    S.op("act", lambda e, pb=pb, sbb=sbb: e.activation(out=sbb[:, :TW], in_=pb[:, :TW], func=AF.Silu),
                         reads=[bpb], writes=[bsb])
                    pdma("pool", gS[hh, :, c0:c0 + TW], sbb[:, :TW], reads=[bsb], writes=[b_gS[hh][ti]])
                elif kind == "k":
                    S.op("act", lambda e, pb=pb, sf=sf: e.activation(out=sf[:, :TW], in_=pb[:, :TW], func=AF.Identity),
                         reads=[bpb], writes=[bsf])
                    S.op("dve", lambda e, pb=pb, sbb=sbb: e.tensor_copy(out=sbb[:, :TW], in_=pb[:, :TW]),
                         reads=[bpb], writes=[bsb])
                    pdma("pool", kT_out[l, hh, :, c0:c0 + TW], sf[:, :TW], reads=[bsf])
                    if not is_s:
                        bb = Buf("skvp")
                        b_skK[l][ti].append(bb)
                        pdma("pool", skK[l][ti][hh * 128:(hh + 1) * 128, :], sbb[:, :TW], reads=[bsb], writes=[bb])
                    else:
                        pdma("pool", sks[hh, :, :], sbb[:, :TW], reads=[bsb], writes=[b_sks[hh]])
                else:
                    W4 = nsub * 128
                    S.op("act", lambda e, pb=pb, sf=sf: e.activation(out=sf[:SW, :W4], in_=pb[:SW, :W4], func=AF.Identity),
                         reads=[bpb], writes=[bsf])
                    S.op("dve", lambda e, pb=pb, sbb=sbb: e.tensor_copy(out=sbb[:SW, :W4], in_=pb[:SW, :W4]),
                         reads=[bpb], writes=[bsb])
                    pdma("pool", v_out[l, c0:c0 + TW, hh * 128:(hh + 1) * 128].rearrange("(s t) d -> t s d", t=SW),
                          sf[:SW, :W4].rearrange("t (s d) -> t s d", d=128), reads=[bsf])
                    if not is_s:
                        bb = Buf("skvp")
                        b_skV[l][ti].append(bb)
                        pdma("pool", skV[l][ti][:, hh * 128:(hh + 1) * 128].rearrange("(s t) d -> t s d", t=SW),
                              sbb[:SW, :W4].rearrange("t (s d) -> t s d", d=128), reads=[bsb], writes=[bb])
                    else:
                        pdma("pool", svs[:, hh * 128:(hh + 1) * 128], sbb[:SW, :128], reads=[bsb], writes=[b_svs[hh]])
            if not AP_ & 2:
                return
            wv, bw = load_w("f", l, 0)
            pb, bpb = pa_bank()
            for k in range(KC):
                S.op("pe", lambda e, k=k, wv=wv, pb=pb: e.matmul(pb[0:4, :TW], lhsT=wv[:, k, :], rhs=hT[:, k, :TW],
                                                                  start=(k == 0), stop=(k == KC - 1)),
                     reads=[bw, B_VV0], writes=[bpb])
            S.op("act", lambda e, pb=pb: e.activation(out=lfst[:, :TW], in_=pb[0:4, :TW], func=AF.Exp,
                                                      bias=pcol(l, "nbf"), scale=-1.0),
                 reads=[bpb, b_const], writes=[b_lfst])
            S.op("act", lambda e: e.activation(out=lfst[:, :TW], in_=lfst[:, :TW], func=AF.Ln, bias=EPS_t[0:4, 1:2]),
                 reads=[b_lfst, b_const], writes=[b_lfst])
            S.op("dve", lambda e: e.tensor_scalar(out=lfst[:, :TW], in0=lfst[:, :TW], scalar1=-1.0, scalar2=None,
                                                  op0=ALU.mult),
                 reads=[b_lfst], writes=[b_lfst])
            S.dma("pool", lf_out[l, :, c0:c0 + TW], lfst[:, :TW], reads=[b_lfst])
            if not is_s:
                bb = Buf("slfp")
                b_slf[l].append(bb)
                S.dma("pool", slf[l][:, c0:c0 + TW], lfst[:, :TW], reads=[b_lfst], writes=[bb])
            else:
                S.dma("pool", slfs[:, :], lfst[:, :TW], reads=[b_lfst], writes=[b_slfs])
            if not AP_ & 4:
                return
            if not is_s:
                all_gather(skK[l][ti], gK[l][ti], b_skK[l][ti], b_gK[l][ti])
                all_gather(skV[l][ti], gV[l][ti], b_skV[l][ti], b_gV[l][ti])
                if ti == NT - 1:
                    all_gather(slf[l], glf[l], b_slf[l], b_glf[l])

        def final_norm(ti, TW, c0):
            rms_stats(TW)
            for k in range(KC):
                i = k % 2
                S.op("dve", lambda e, k=k, i=i: e.scalar_tensor_tensor(
                    out=tmpf[i][:, :TW], in0=xt[:, k, :TW], scalar=gfin(k), in1=rstd[:, :TW], op0=ALU.mult, op1=ALU.mult),
                     reads=[B_KT, b_rstd, b_const], writes=[b_tmpf[i]])
                S.dma("pool", yT[k * 128:(k + 1) * 128, c0:c0 + TW], tmpf[i][:, :TW], reads=[b_tmpf[i]])

        def tile_pass(l):
            tiles = list(range(NT)) + ([NT] if cfg.sample else [])

            def load_x(ti, queue):
                TW = 512 if ti < NT else DEC_T
                c0 = ti * 512
                src = xT if l == 0 else xS
                S.dma(queue, xt[:, :, :TW], src[:, c0:c0 + TW].rearrange("(k p) t -> p k t", p=128),
                      reads=([b_xS[ti]] if l > 0 else []), writes=[B_KT])

            for idx, ti in enumerate(tiles):
                TW = 512 if ti < NT else DEC_T
                c0 = ti * 512
                if idx == 0 or l == L:
                    load_x(ti, "sync")
                if l > 0:
                    phase_c(l - 1, ti, TW, c0)
                if l < L:
                    S.dma("pool", xS[:, c0:c0 + TW].rearrange("(k p) t -> p k t", p=128), xt[:, :, :TW],
                          reads=[B_KT], writes=[b_xS[ti]])
                    nxt = tiles[idx + 1] if idx + 1 < len(tiles) else None
                    phase_a(l, ti, TW, c0,
                            after_h=(lambda nxt=nxt: load_x(nxt, "pool")) if nxt is not None else None)
                else:
                    final_norm(ti, TW, c0)

        def all_gather(src, dst, reads, bdst):
            S.cc_n += 1
            n = S.cc_n
            S.custom("pool", lambda e, src=src, dst=dst: e.collective_compute(
                "AllGather", ALU.bypass, replica_groups=[[0, 1, 2, 3], [4, 5, 6, 7]],
                ins=[src.ap().opt()], outs=[dst.ap().opt()]).then_inc(cc_sem, 1),
                     "cc", n, reads=reads, writes=[bdst])

        S.cc_n = 0

        def cumsum_tables(l):
            for ip in range(NT):
                for jp in range(4):
                    S.dma("sync", lnp[16 * ip + 4 * jp:16 * ip + 4 * jp + 4, :, :],
                          glf[l][jp * 4:(jp + 1) * 4, 512 * ip:512 * ip + 512].rearrange("h (s p) -> s h p", p=128),
                          reads=[b_glf[l]], writes=[b_lnp])
            cum_core(lnp, b_lnp, NB, ncb, b_ncb, NB)
            for h in range(4):
                for jp in range(4):
                    src = ncb[:, h, :NB].rearrange("p (i m) -> p i m", m=16)[:, :, 4 * jp:4 * jp + 4]
                    dst = xn[:, h * 32:h * 32 + NT * 4].rearrange("p (i s) -> p i s", s=4)
                    if jp == 0:
                        S.op("dve", lambda e, src=src, dst=dst: e.tensor_scalar(
                            out=dst, in0=src, scalar1=onehot[:, 0:1], scalar2=None, op0=ALU.mult),
                             reads=[b_ncb, b_const], writes=[b_xn])
                    else:
                        S.op("dve", lambda e, src=src, dst=dst, jp=jp: e.scalar_tensor_tensor(
                            out=dst, in0=src, scalar=onehot[:, jp:jp + 1], in1=dst, op0=ALU.mult, op1=ALU.add),
                             reads=[b_ncb, b_const, b_xn], writes=[b_xn])
            pb, bpb = pbank[6], b_pb[6]
            S.op("pe", lambda e: e.matmul(pb[:, 0:128], lhsT=sel127_f, rhs=xn[:, :], start=True, stop=True),
                 reads=[b_xn, b_const], writes=[bpb])
            S.op("act", lambda e: e.activation(out=ncref[:, :], in_=pb[:, 0:128], func=AF.Identity), reads=[bpb],
                 writes=[b_ncref])

        def cum_core(Lsrc, bL, nb, dst, bdst, nbp):
            pb, bpb = pbank[6], b_pb[6]
            pb2, bpb2 = pbank[7], b_pb[7]
            for h in range(4):
                S.op("pe", lambda e, h=h: e.transpose(pb[:, 0:nb], Lsrc[:nb, h, :], ident_f[:nb, :nb]),
                     reads=[bL, b_const], writes=[bpb])
                S.op("act", lambda e: e.activation(out=ltb[:, :nb], in_=pb[:, 0:nb], func=AF.Identity), reads=[bpb],
                     writes=[b_ltb])
                S.op("pe", lambda e: e.matmul(pb2[:nb, 0:128], lhsT=ltb[:, :nb], rhs=ones_f, start=True, stop=True),
                     reads=[b_ltb, b_const], writes=[bpb2])
                S.op("act", lambda e: e.activation(out=tott[:nb, :], in_=pb2[:nb, 0:128], func=AF.Identity), reads=[bpb2],
                     writes=[b_tott])
                S.op("pe", lambda e: e.matmul(pb[:, 128:128 + nb], lhsT=triinc_f, rhs=ltb[:, :nb], start=True, stop=False),
                     reads=[b_ltb, b_const], writes=[bpb])
                S.op("pe", lambda e: e.matmul(pb[:, 128:128 + nb], lhsT=tott[:nb, :], rhs=su_f[:nb, :nb], start=False,
                                              stop=True),
                     reads=[b_tott, b_const], writes=[bpb])
                S.op("act", lambda e, h=h: e.activation(out=dst[:, h, :nb], in_=pb[:, 128:128 + nb], func=AF.Identity,
                                                        scale=-1.0),
                     reads=[bpb], writes=[bdst])

        ectr = [0]
        wctr2 = [0]

        acc_t = [sb(f"acc{i}", [128, 512], BF16) for i in range(2)]
        b_acc = [Buf(f"acc{i}") for i in range(2)]
        rc = {"z": 0, "e": 0, "t": 0, "w": 0, "p": 0, "a": 0}

        def attn_tile(hh, QW, q_ap, bq, steps, bias_of, kind, ob=0, hook=None):
            Zb = (0, 1, 2)
            Pb = (6, 7)
            O, bO = pbank[3 + ob], b_pb[3 + ob]
            DEN, bDEN = pbank[6 + ob], b_pb[6 + ob]
            ns = len(steps)
            R = [dict() for _ in range(ns)]

            def qk(si):
                kt_ap, v_ap, KP, m_ap, rds = steps[si]
                zi = Zb[rc["z"] % 3]
                rc["z"] += 1
                R[si]["z"] = (pbank[zi], b_pb[zi])
                z = pbank[zi]
                S.op("pe", lambda e, z=z, kt_ap=kt_ap, KP=KP: e.matmul(z[:KP, :QW], lhsT=kt_ap, rhs=q_ap, start=True,
                                                                        stop=True),
                     reads=rds + [bq], writes=[b_pb[zi]])

            def pv(si):
                kt_ap, v_ap, KP, m_ap, rds = steps[si]
                wt, bw = R[si]["w"]
                first, last = si == 0, si == ns - 1
                S.op("pe", lambda e, wt=wt, v_ap=v_ap, KP=KP, first=first, last=last: e.matmul(
                    O[:, :QW], lhsT=v_ap, rhs=wt[:KP, :QW], start=first, stop=last),
                     reads=rds + [bw], writes=[bO])
                if kind == "fox":
                    S.op("pe", lambda e, wt=wt, KP=KP, first=first, last=last: e.matmul(
                        DEN[:, :QW], lhsT=ones_b[:KP, :], rhs=wt[:KP, :QW], start=first, stop=last),
                         reads=[bw, b_const], writes=[bDEN])

            def take_w(si):
                wi = rc["w"] % NW
                rc["w"] += 1
                R[si]["w"] = (w_t[wi], b_w[wi])
                return w_t[wi], b_w[wi]

            def fox_x(si):
                kt_ap, v_ap, KP, m_ap, rds = steps[si]
                z, bz = R[si]["z"]
                wt, bw = take_w(si)
                sw_ = min(QW, 256)
                for sq in range(max(QW // 256, 1)):
                    bcol, bbt = bias_of(si, sq)
                    S.op("act", lambda e, z=z, wt=wt, KP=KP, bcol=bcol, sq=sq, sw_=sw_: e.activation(
                        out=wt[:KP, sq * sw_:(sq + 1) * sw_], in_=z[:KP, sq * sw_:(sq + 1) * sw_], func=AF.Exp,
                        bias=bcol),
                         reads=[bz, bbt], writes=[bw])
                if m_ap is not None:
                    S.op("dve", lambda e, wt=wt, m_ap=m_ap, KP=KP: e.tensor_tensor(
                        out=wt[:KP, :QW], in0=wt[:KP, :QW], in1=m_ap, op=ALU.mult),
                         reads=[bw, b_const], writes=[bw])

            hook_at = min(3, ns - 1)
            if kind == "fox":
                qk(0)
                if ns > 1:
                    qk(1)
                for si in range(ns):
                    if si + 2 < ns:
                        qk(si + 2)
                    fox_x(si)
                    if si >= 1:
                        pv(si - 1)
                    if si == hook_at and hook is not None:
                        hook()
                pv(ns - 1)
                return

            def sb_e(si):
                kt_ap, v_ap, KP, m_ap, rds = steps[si]
                z, bz = R[si]["z"]
                ei = rc["e"] % NE
                rc["e"] += 1
                et, be, spt, bsp = e_t[ei], b_e[ei], sp_t[ei], b_sp[ei]
                R[si]["e"] = (et, be)
                R[si]["sp"] = (spt, bsp)
                S.op("act", lambda e, z=z, et=et, KP=KP: e.activation(out=et[:KP, :QW], in_=z[:KP, :QW], func=AF.Exp),
                     reads=[bz], writes=[be])
                if m_ap is not None:
                    S.op("dve", lambda e, et=et, m_ap=m_ap, KP=KP: e.tensor_tensor(
                        out=et[:KP, :QW], in0=et[:KP, :QW], in1=m_ap, op=ALU.mult),
                         reads=[be, b_const], writes=[be])

            def sb_l(si):
                kt_ap, v_ap, KP, m_ap, rds = steps[si]
                et, be = R[si]["e"]
                spt, bsp = R[si]["sp"]
                S.op("act", lambda e, et=et, spt=spt, KP=KP: e.activation(out=spt[:KP, :QW], in_=et[:KP, :QW],
                                                                          func=AF.Ln, bias=EPS_t[:KP, 1:2]),
                     reads=[be, b_const], writes=[bsp])

            def sb_p(si):
                kt_ap, v_ap, KP, m_ap, rds = steps[si]
                spt, bsp = R[si]["sp"]
                pi = Pb[rc["p"] % 2]
                rc["p"] += 1
                P, bP = pbank[pi], b_pb[pi]
                R[si]["p"] = (P, bP)
                S.op("pe", lambda e, spt=spt, KP=KP, P=P, si=si: e.matmul(
                    P[:KP, :QW], lhsT=tri_b[:KP, :KP], rhs=spt[:KP, :QW], start=True, stop=(si == 0)),
                     reads=[bsp, b_const], writes=[bP])
                if si > 0:
                    at, ba = R[si]["acc"]
                    S.op("pe", lambda e, at=at, KP=KP, P=P: e.matmul(
                        P[:KP, :QW], lhsT=ones_b[:, :KP], rhs=at[:, :QW], start=False, stop=True),
                         reads=[ba, b_const], writes=[bP])

            def sb_acc(si):
                kt_ap, v_ap, KP, m_ap, rds = steps[si]
                spt, bsp = R[si]["sp"]
                ai = rc["a"] % 2
                rc["a"] += 1
                an, ban = acc_t[ai], b_acc[ai]
                R[si + 1]["acc"] = (an, ban)
                if si == 0:
                    if KP < 128:
                        S.op("dve", lambda e, an=an: e.memset(an[:, :QW], 0.0), writes=[ban])
                    S.op("dve", lambda e, an=an, spt=spt, KP=KP: e.tensor_copy(out=an[:KP, :QW], in_=spt[:KP, :QW]),
                         reads=[bsp], writes=[ban])
                else:
                    ac, bac = R[si]["acc"]
                    if KP < 128:
                        raise NotImplementedError
                    S.op("dve", lambda e, an=an, ac=ac, spt=spt: e.tensor_tensor(
                        out=an[:, :QW], in0=ac[:, :QW], in1=spt[:, :QW], op=ALU.add),
                         reads=[bac, bsp], writes=[ban])

            def sb_tw(si):
                kt_ap, v_ap, KP, m_ap, rds = steps[si]
                et, be = R[si]["e"]
                P, bP = R[si]["p"]
                ti_ = rc["t"] % NE
                rc["t"] += 1
                tt, btt = t_t[ti_], b_t[ti_]
                wt, bw = take_w(si)
                S.op("act", lambda e, tt=tt, KP=KP, P=P: e.activation(out=tt[:KP, :QW], in_=P[:KP, :QW], func=AF.Exp,
                                                                     scale=-1.0),
                     reads=[bP], writes=[btt])
                S.op("dve", lambda e, wt=wt, et=et, tt=tt, KP=KP: e.tensor_tensor(
                    out=wt[:KP, :QW], in0=et[:KP, :QW], in1=tt[:KP, :QW], op=ALU.mult),
                     reads=[be, btt], writes=[bw])

            qk(0)
            sb_e(0)
            if ns > 1:
                qk(1)
                sb_e(1)
            sb_l(0)
            if ns > 1:
                sb_l(1)
            for si in range(ns):
                if si + 2 < ns:
                    qk(si + 2)
                    sb_e(si + 2)
                sb_p(si)
                if si + 1 < ns:
                    sb_acc(si)
                sb_tw(si)
                if si + 2 < ns:
                    sb_l(si + 2)
                if si >= 1:
                    pv(si - 1)
                if si == hook_at and hook is not None:
                    hook()
            pv(ns - 1)

        def epilogue(l, hh, QW, c0, ti, gate_ap, bgate, kind, ob=0):
            O, bO = pbank[3 + ob], b_pb[3 + ob]
            DEN, bDEN = pbank[6 + ob], b_pb[6 + ob]
            SS, bSS = pbank[5], b_pb[5]
            i = ectr[0] % 2
            if kind == "fox":
                S.op("dve", lambda e: e.reciprocal(out=tmpf[0][:, :QW], in_=DEN[:, :QW]), reads=[bDEN], writes=[b_tmpf[0]])
                S.op("dve", lambda e: e.tensor_tensor(out=tmpf[1][:, :QW], in0=O[:, :QW], in1=tmpf[0][:, :QW], op=ALU.mult),
                     reads=[bO, b_tmpf[0]], writes=[b_tmpf[1]])
                u_ap, bu = tmpf[1], b_tmpf[1]
            else:
                u_ap, bu = O, bO
            S.op("act", lambda e: e.activation(out=sqb[i][:, :QW], in_=u_ap[:, :QW], func=AF.Square), reads=[bu],
                 writes=[b_sqb[i]])
            S.op("pe", lambda e: e.matmul(SS[:, :QW], lhsT=ones_b, rhs=sqb[i][:, :QW], start=True, stop=True),
                 reads=[b_sqb[i], b_const], writes=[bSS])
            S.op("act", lambda e: e.activation(out=tmpf[0][:, :QW], in_=SS[:, :QW], func=AF.Sqrt, bias=EPS_t[:, 0:1],
                                               scale=1.0 / 128),
                 reads=[bSS, b_const], writes=[b_tmpf[0]])
            S.op("dve", lambda e: e.reciprocal(out=tmpf[0][:, :QW], in_=tmpf[0][:, :QW]), reads=[b_tmpf[0]],
                 writes=[b_tmpf[0]])
            S.op("dve", lambda e: e.scalar_tensor_tensor(out=tmpf[1][:, :QW], in0=u_ap[:, :QW], scalar=pcol(l, "g_o", hh),
                                                         in1=tmpf[0][:, :QW], op0=ALU.mult, op1=ALU.mult),
                 reads=[bu, b_tmpf[0], b_const], writes=[b_tmpf[1]])
            sf, bsf, sbb, bsb = stage()
            S.op("dve", lambda e, sbb=sbb: e.tensor_tensor(out=sbb[:, :QW], in0=tmpf[1][:, :QW], in1=gate_ap, op=ALU.mult),
                 reads=[b_tmpf[1], bgate], writes=[bsb])
            S.dma("pool", ogS[hh, :, c0:c0 + QW], sbb[:, :QW], reads=[bsb], writes=[b_ogS[hh][ti]])

        ob_ctr = [0]
        pend = [None]

        def next_ob():
            i = ob_ctr[0] % 2
            ob_ctr[0] += 1
            return i

        def phase_b(l):
            cumsum_tables(l)
            if cfg.sample:
                sample_tables(l)
            if getattr(cfg, "stop", 99) < 4:
                return
            for hh in range(NH if getattr(cfg, "stop", 99) >= 5 else 1):
                kind = "sb" if hh < 4 else "fox"
                if l + 1 < L and hh < 4:
                    cast_layer(l + 1, hh)
                qi = hh % 2
                def load_kv(hx, ip):
                    for jp in range(4):
                        n0 = 16 * ip + 4 * jp
                        S.dma("sync", KT[:, n0 * 128:(n0 + 4) * 128],
                              gK[l][ip][jp * 1024 + hx * 128: jp * 1024 + hx * 128 + 128, :],
                              reads=[b_gK[l][ip]], writes=[B_KT[ip]])
                        S.dma("sync", VV[:, n0 * 128:(n0 + 4) * 128].rearrange("p (s d) -> p s d", d=128),
                              gV[l][ip][jp * 512:(jp + 1) * 512, hx * 128:(hx + 1) * 128].rearrange("(s p) d -> p s d", p=128),
                              reads=[b_gV[l][ip]], writes=[B_VVg[ip]])

                for ip in (range(NT) if hh == 0 else range(1)):
                    load_kv(hh, ip)
                def fox_bias(ti):
                    h = hh - 4
                    bi = ti % 2
                    nkb_ = 16 * ti + 16
                    for sq in range(2):
                        cix = h * 32 + ti * 4 + 2 * sq
                        S.op("dve", lambda e, h=h, bi=bi, sq=sq, cix=cix, nkb_=nkb_: e.tensor_scalar(
                            out=bt[bi][:, sq, :nkb_], in0=ncb[:, h, :nkb_], scalar1=ncref[:, cix:cix + 1],
                            scalar2=60.0, op0=ALU.subtract, op1=ALU.min),
                             reads=[b_ncb, b_ncref], writes=[b_bt[bi]])

                if kind == "fox":
                    fox_bias(NT - 1)
                for ti in range(NT - 1, -1, -1):
                    nkb = 16 * ti + 16
                    order = list(range(nkb - 1, -1, -1)) if kind == "sb" else list(range(nkb))
                    steps = []
                    for n in order:
                        m = n - 16 * ti
                        m_ap = None
                        if m >= 0:
                            off = ((0 if kind == "sb" else 1) * 16 + m) * 512
                            m_ap = mask_t[:, off:off + 512]
                        steps.append((KT[:, n * 128:(n + 1) * 128], VV[:, n * 128:(n + 1) * 128], 128, m_ap,
                                      [B_KT[n // 16], B_VVg[n // 16]]))
                    bias_of = None
                    if kind == "fox":
                        bi = ti % 2
                        bias_of = lambda si, sq, bi=bi, order=order: (bt[bi][:, sq, order[si]:order[si] + 1], b_bt[bi])
                    qi = load_qg(hh, ti, ti * 512, 512)
                    ob = next_ob()
                    hk = pend[0]
                    pend[0] = None
                    if kind == "fox" and ti >= 1:
                        hk0 = hk
                        hk = lambda hk0=hk0, ti=ti: ((hk0() if hk0 is not None else None), fox_bias(ti - 1))
                    attn_tile(hh, 512, qh[qi][:, :], b_qh[qi], steps, bias_of, kind, ob, hook=hk)
                    pend[0] = (lambda l=l, hh=hh, ti=ti, qi=qi, kind=kind, ob=ob: epilogue(
                        l, hh, 512, ti * 512, ti, gh[qi][:, :], b_gh[qi], kind, ob))
                    if hh + 1 < NH and ti + 1 <= NT - 1:
                        load_kv(hh + 1, ti + 1)
                if cfg.sample:
                    sample_attn(l, hh, kind, qi)
            if pend[0] is not None:
                pend[0]()
                pend[0] = None

        def sample_tables(l):
            S.op("dve", lambda e: e.memset(lnps[:, :, :], 0.0), writes=[b_lnps])
            S.dma("sync", lnps[0:8, :, :], clf[l, :, :, :], reads=[], writes=[b_lnps])
            S.dma("sync", lnps[8:9, :, 0:DEC_T], slfs[:, :].rearrange("(o h) p -> o h p", o=1), reads=[b_slfs],
                  writes=[b_lnps])
            cum_core(lnps, b_lnps, 9, ncbs, b_ncbs, 9)

        def sample_attn(l, hh, kind, qi):
            c0 = NTOK
            QW = DEC_T
            S.dma("pool", kts[:, 0:PAST], ckT[l, hh, :, :], writes=[b_kts])
            S.dma("sync", kts[:, PAST:PAST + DEC_T], sks[hh, :, :], reads=[b_sks[hh]], writes=[b_kts])
            S.dma("pool", vs[:, 0:8, :], cv[l, :, hh * 128:(hh + 1) * 128].rearrange("(n p) d -> p n d", p=128),
                  writes=[b_vs])
            S.dma("sync", vs[0:DEC_T, 8, :], svs[:, hh * 128:(hh + 1) * 128], reads=[b_svs[hh]], writes=[b_vs])
            qi = load_qg(hh, NT, c0, QW)
            q_ap = qh[qi][:, :QW]
            mk = smask_t[0:DEC_T, (0 if kind == "sb" else 32):(0 if kind == "sb" else 32) + 32]
            blocks = [(kts[:, PAST:PAST + DEC_T], vs[0:DEC_T, 8, :], DEC_T, mk, [b_kts, b_vs])]
            for n in range(7, -1, -1):
                blocks.append((kts[:, n * 128:(n + 1) * 128], vs[:, n, :], 128, None, [b_kts, b_vs]))
            bias_of = None
            if kind == "fox":
                h = hh - 4
                pb, bpb = pbank[5], b_pb[5]
                S.op("pe", lambda e, h=h: e.matmul(pb[:, 0:16], lhsT=sel31_f, rhs=ncbs[:, h, :], start=True, stop=True),
                     reads=[b_ncbs, b_const], writes=[bpb])
                S.op("act", lambda e: e.activation(out=xs_ref[:, 0:16], in_=pb[:, 0:16], func=AF.Identity), reads=[bpb],
                     writes=[b_xsref])
                S.op("dve", lambda e, h=h: e.tensor_scalar(out=bt[0][:, 0, 0:9], in0=ncbs[:, h, 0:9], scalar1=xs_ref[:, 8:9],
                                                           scalar2=0.0, op0=ALU.subtract, op1=ALU.min),
                     reads=[b_ncbs, b_xsref], writes=[b_bt[0]])
                nidx = [8] + list(range(7, -1, -1))
                bias_of = lambda si, sq: (bt[0][:(DEC_T if si == 0 else 128), 0, nidx[si]:nidx[si] + 1], b_bt[0])
            ob = next_ob()
            hk = pend[0]
            pend[0] = None
            attn_tile(hh, QW, q_ap, b_qh[qi], blocks, bias_of, kind, ob, hook=hk)
            pend[0] = (lambda l=l, hh=hh, qi=qi, kind=kind, ob=ob: epilogue(
                l, hh, QW, c0, NT, gh[qi][:, :QW], b_gh[qi], kind, ob))

        stop = getattr(cfg, "stop", 99)
        if stop >= 2:
            for l in range(L):
                tile_pass(l)
                if stop >= 3:
                    phase_b(l)
            if stop >= 6:
                tile_pass(L)
        S.final_waits("sync")

        with nc.Block() as block:
            @block.tensor
            def _(e):
                for f in S.q["pe"]:
                    f(e)

            @block.scalar
            def _(e):
                for f in S.q["act"]:
                    f(e)

            @block.vector
            def _(e):
                for f in S.q["dve"]:
                    f(e)

            @block.gpsimd
            def _(e):
                for f in S.q["pool"]:
                    f(e)

            @block.sync
            def _(e):
                for f in S.q["sync"]:
                    f(e)
    return nc, S


def _consts():
    k = np.arange(128)
    ones = np.ones((128, 128), np.float32)
    tri = (k[:, None] >= k[None, :]).astype(np.float32)
    omt = 1.0 - tri
    cbf = np.concatenate([ones, tri, omt], axis=1).astype(ml_dtypes.bfloat16)
    ident = np.eye(128, dtype=np.float32)
    triinc = (k[:, None] <= k[None, :]).astype(np.float32)
    su = (k[:, None] < k[None, :]).astype(np.float32)
    sel = np.zeros((128, 128), np.float32)
    sel[127, :] = 1.0
    sel31 = np.zeros((128, 128), np.float32)
    sel31[31, :] = 1.0
    cf32 = np.concatenate([ident, ones, triinc, su, sel, sel31], axis=1).astype(np.float32)
    return cbf, cf32


def _masks(j):
    k = np.arange(128)
    out = np.zeros((128, 2, 16, 512), np.float32)
    tri_sb = (k[:, None] < k[None, :]).astype(np.float32)
    tri_fx = (k[:, None] <= k[None, :]).astype(np.float32)
    for kind, tri in enumerate((tri_sb, tri_fx)):
        for m in range(16):
            d = m - 4 * j
            for s in range(4):
                if d < s:
                    out[:, kind, m, s * 128:(s + 1) * 128] = 1.0
                elif d == s:
                    out[:, kind, m, s * 128:(s + 1) * 128] = tri
    return out.reshape(128, 2 * 16 * 512).astype(ml_dtypes.bfloat16)


def _smask():
    k = np.arange(32)
    out = np.zeros((128, 64), np.float32)
    out[:32, 0:32] = (k[:, None] < k[None, :])
    out[:32, 32:64] = (k[:, None] <= k[None, :])
    return out.astype(ml_dtypes.bfloat16)


_PROG_CACHE = {}


def run(cfg, inputs):
    NT, L, NTOK, NTOT = cfg.NT, cfg.DEPTH, cfg.NTOK, cfg.NTOT
    f32 = np.float32
    g = {k: np.asarray(v) for k, v in inputs.items()}
    key = (NT, L, cfg.sample)
    if key not in _PROG_CACHE:
        _PROG_CACHE[key] = build_program(cfg)
    nc, S = _PROG_CACHE[key]

    w_in = g["w_in"].astype(f32, copy=False)
    w_in_r = w_in[:, :, :4096].reshape(L, KC, 128, 32, 128).transpose(0, 3, 2, 1, 4).reshape(L * 32 * 128 * 2, 1024)
    w_f_r = w_in[:, :, 4096:4100].reshape(L, KC, 128, 4).transpose(0, 2, 1, 3).reshape(L * 128, 64)
    w_out_r = g["w_out"].reshape(L, 8, 128, 16, 128).transpose(0, 3, 2, 1, 4).reshape(L * 16 * 128, 1024)
    w_g_r = g["w_ple_gate"].reshape(L, KC, 128, 16, 128).transpose(0, 3, 2, 1, 4).reshape(L * 16 * 128 * 2, 1024)
    w_ple_r = g["w_ple"].reshape(L, 2, 128, 16, 128).transpose(0, 3, 2, 1, 4).reshape(L * 16 * 128, 256)
    w_in_r, w_f_r, w_out_r, w_g_r, w_ple_r = (np.ascontiguousarray(a, dtype=f32) for a in
                                              (w_in_r, w_f_r, w_out_r, w_g_r, w_ple_r))
    cbf, cf32 = _consts()
    smask = _smask()
    NPAR = L * 48 + 16 + 8
    in_maps = []
    for c in range(8):
        b, j = c // 4, c % 4
        tiles = [4 * i + j for i in range(NT)]
        xp = g["x_prompt"][b].reshape(4 * NT, 512, D)[tiles].reshape(NTOK, D)
        xs = g["x_sample"][c]
        xTc = np.ascontiguousarray(np.concatenate([xp, xs], axis=0).T, dtype=f32)
        pp = g["p_prompt"][:, b].reshape(L, 4 * NT, 512, PLE)[:, tiles].reshape(L, NTOK, PLE)
        pTc = np.ascontiguousarray(np.concatenate([pp, g["p_sample"][:, c]], axis=1).transpose(0, 2, 1), dtype=f32)
        par = np.zeros((128, NPAR), f32)
        for l in range(L):
            par[:, l * 48: l * 48 + 16] = g["g_attn_norm"][l].reshape(KC, 128).T
            par[:, l * 48 + 16: l * 48 + 32] = g["g_ple_norm"][l].reshape(KC, 128).T
            par[:, l * 48 + 32: l * 48 + 36] = g["g_out_sb"][l].reshape(4, 128).T
            par[:, l * 48 + 36: l * 48 + 40] = g["g_out_fox"][l].reshape(4, 128).T
            par[0:4, l * 48 + 40] = -g["b_forget"][l]
        par[:, L * 48: L * 48 + 16] = g["g_final"].reshape(KC, 128).T
        par[:, L * 48 + 16 + j] = 1.0
        m = {"xT": xTc, "pT": pTc, "w_in_r": w_in_r, "w_f_r": w_f_r, "w_out_r": w_out_r, "w_g_r": w_g_r,
             "w_ple_r": w_ple_r, "params": par, "cbf": cbf, "cf32": cf32, "masks": _masks(j), "smask": smask}
        if cfg.sample:
            ck = np.concatenate([g["cache_sb_k"][:, c], g["cache_fox_k"][:, c]], axis=2)
            m["ckT"] = np.ascontiguousarray(ck.transpose(0, 2, 3, 1), dtype=f32)
            cvv = np.concatenate([g["cache_sb_v"][:, c], g["cache_fox_v"][:, c]], axis=2)
            m["cv"] = np.ascontiguousarray(cvv.reshape(L, PAST, NH * HD), dtype=f32)
            m["clf"] = np.ascontiguousarray(g["cache_fox_logf"][:, c].reshape(L, 8, 128, 4).transpose(0, 1, 3, 2), dtype=f32)
        in_maps.append(m)

    res = run_bass_kernel_spmd(nc, in_maps, core_ids=list(range(8)), trace=getattr(cfg, "trace", False))
    R = res.results
    global LAST_R, LAST_RES
    LAST_R = R
    LAST_RES = res
    SEQ = cfg.SEQ
    y_p = np.zeros((2, SEQ, D), f32)
    y_s = np.zeros((8, DEC_T, D), f32)
    kp = np.zeros((L, 2, SEQ, NH, HD), f32)
    vp = np.zeros((L, 2, SEQ, NH, HD), f32)
    lp = np.zeros((L, 2, SEQ, 4), f32)
    ks = np.zeros((L, 8, DEC_T, NH, HD), f32)
    vsa = np.zeros((L, 8, DEC_T, NH, HD), f32)
    ls = np.zeros((L, 8, DEC_T, 4), f32)
    for c in range(8):
        b, j = c // 4, c % 4
        r = R[c]
        yt = r["yT"].T
        kt = r["kT_out"].transpose(0, 3, 1, 2)
        vt = r["v_out"].reshape(L, NTOT, NH, HD)
        lt = r["lf_out"].transpose(0, 2, 1)
        for i in range(NT):
            gt = 4 * i + j
            y_p[b, gt * 512:(gt + 1) * 512] = yt[i * 512:(i + 1) * 512]
            kp[:, b, gt * 512:(gt + 1) * 512] = kt[:, i * 512:(i + 1) * 512]
            vp[:, b, gt * 512:(gt + 1) * 512] = vt[:, i * 512:(i + 1) * 512]
            lp[:, b, gt * 512:(gt + 1) * 512] = lt[:, i * 512:(i + 1) * 512]
        y_s[c] = yt[NTOK:]
        ks[:, c] = kt[:, NTOK:]
        vsa[:, c] = vt[:, NTOK:]
        ls[:, c] = lt[:, NTOK:]
    return (y_p, y_s, kp[..., 0:4, :], kp[..., 4:8, :], vp[..., 0:4, :], vp[..., 4:8, :], lp,
            ks[..., 0:4, :], ks[..., 4:8, :], vsa[..., 0:4, :], vsa[..., 4:8, :], ls)


def kernel(**inputs):
    cfg = Cfg(NT=8, DEPTH=4, sample=True)
    outs = run(cfg, inputs)
    y_p, y_s, skp, fkp, svp, fvp, lp, sks, fks, svs, fvs, ls = outs
    c = np.ascontiguousarray
    return (c(y_p), c(y_s), c(skp), c(svp), c(fkp), c(fvp), c(lp), c(sks), c(svs), c(fks), c(fvs), c(ls))
```

```python
import contextlib
import numpy as np
import ml_dtypes
import concourse.bass as bass
import concourse.mybir as mybir
from concourse.bass_utils import run_bass_kernel_spmd

F32 = mybir.dt.float32
BF16 = mybir.dt.bfloat16
AF = mybir.ActivationFunctionType
ALU = mybir.AluOpType

D = 2048
KC = 16
HD = 128
NH = 8
PLE = 256
EPS = 1e-6
DEC_T = 32
PAST = 1024
QSCALE = HD ** -0.5


class Cfg:
    def __init__(self, NT=8, DEPTH=4, sample=True):
        self.NT = NT
        self.DEPTH = DEPTH
        self.sample = sample
        self.NTOK = NT * 512
        self.NTOT = self.NTOK + DEC_T
        self.NB = NT * 16
        self.SEQ = NT * 4 * 512


class Buf:
    __slots__ = ("name", "w", "r", "excl")

    def __init__(self, name, excl=False):
        self.name = name
        self.w = None
        self.r = {}
        self.excl = excl


ENGS = ("pe", "act", "dve", "pool", "sync")


def _flat(x):
    out = []
    for b in x:
        if isinstance(b, (list, tuple)):
            out.extend(_flat(b))
        else:
            out.append(b)
    return out


class Sched:
    def __init__(self, nc, sems, dsems):
        self.nc = nc
        self.sem = sems
        self.q = {e: [] for e in ENGS}
        self.cnt = {e: 0 for e in ENGS}
        self.waited = {e: {} for e in ENGS}
        self.dsem = dsems
        self.duse = {k: [0] * len(v) for k, v in dsems.items()}
        self.dnext = {k: 0 for k in dsems}
        self.n_ops = 0
        self.n_waits = 0

    def _semobj(self, key):
        if isinstance(key, tuple):
            return self.dsem[key[0]][key[1]]
        return self.sem[key]

    def _collect(self, reads, writes):
        need = {}

        def add(tok):
            if tok is None:
                return
            k, v = tok
            if need.get(k, 0) < v:
                need[k] = v

        for b in reads:
            add(b.w)
        for b in writes:
            add(b.w)
            for k, v in b.r.items():
                add((k, v))
        return need

    def _emit_waits(self, eng, need):
        wl = []
        wd = self.waited[eng]
        for k, v in need.items():
            if eng == "pe" and k == "pe":
                continue
            if wd.get(k, 0) >= v:
                continue
            wd[k] = v
            wl.append((self._semobj(k), v))
        return wl

    def _commit(self, tok, reads, writes):
        k, v = tok
        for b in reads:
            if b.r.get(k, 0) < v:
                b.r[k] = v
        for b in writes:
            b.w = tok
            b.r = {}

    def op(self, eng, fn, reads=(), writes=()):
        reads, writes = _flat(reads), _flat(writes)
        ex = [b for b in reads if b.excl]
        if ex:
            writes = list(writes) + ex
        need = self._collect(reads, writes)
        wl = self._emit_waits(eng, need)
        self.cnt[eng] += 1
        tok = (eng, self.cnt[eng])
        sem = self.sem[eng]
        self.n_ops += 1
        self.n_waits += len(wl)

        def run(e, wl=wl, fn=fn, sem=sem):
            for s, v in wl:
                e.wait_ge(s, v)
            fn(e).then_inc(sem, 1)

        self.q[eng].append(run)
        self._commit(tok, reads, writes)

    def dma(self, queue, out, in_, reads=(), writes=(), **kw):
        eng = "pool" if queue == "cast" else queue
        reads, writes = _flat(reads), _flat(writes)
        pool = self.dsem[queue]
        i = self.dnext[queue]
        self.dnext[queue] = (i + 1) % len(pool)
        self.duse[queue][i] += 1
        use = self.duse[queue][i]
        key = (queue, i)
        need = self._collect(reads, writes)
        if use > 1:
            if need.get(key, 0) < 16 * (use - 1):
                need[key] = 16 * (use - 1)
        wl = self._emit_waits(eng, need)
        sem = pool[i]
        self.n_ops += 1
        self.n_waits += len(wl)

        def run(e, wl=wl, sem=sem, out=out, in_=in_, kw=kw):
            for s, v in wl:
                e.wait_ge(s, v)
            e.dma_start(out=out, in_=in_, **kw).then_inc(sem, 16)

        self.q[eng].append(run)
        self._commit((key, 16 * use), reads, writes)

    def custom(self, eng, fn, semobj_key, val, reads=(), writes=()):
        reads, writes = _flat(reads), _flat(writes)
        need = self._collect(reads, writes)
        wl = self._emit_waits(eng, need)

        def run(e, wl=wl, fn=fn):
            for s, v in wl:
                e.wait_ge(s, v)
            fn(e)

        self.q[eng].append(run)
        self._commit((semobj_key, val), reads, writes)

    def final_waits(self, eng):
        wl = []
        for qn, pool in self.dsem.items():
            for i, s in enumerate(pool):
                if self.duse[qn][i] > 0:
                    wl.append((s, 16 * self.duse[qn][i]))

        def run(e, wl=wl):
            for s, v in wl:
                e.wait_ge(s, v)

        self.q[eng].append(run)


def build_program(cfg):
    NT, L, NTOK, NTOT, NB = cfg.NT, cfg.DEPTH, cfg.NTOK, cfg.NTOT, cfg.NB
    nc = bass.Bass("TRN2", target_bir_lowering=False)

    def din(name, shape, dt=F32):
        return nc.dram_tensor(name, list(shape), dt, kind="ExternalInput")

    def dout(name, shape, dt=F32):
        return nc.dram_tensor(name, list(shape), dt, kind="ExternalOutput")

    def dint(name, shape, dt):
        return nc.dram_tensor(name, list(shape), dt)

    xT = din("xT", [D, NTOT])
    pT = din("pT", [L, PLE, NTOT])
    w_in_r = din("w_in_r", [L * 32 * 128 * 2, 1024])
    w_f_r = din("w_f_r", [L * 128, 64])
    w_out_r = din("w_out_r", [L * 16 * 128, 1024])
    w_g_r = din("w_g_r", [L * 16 * 128 * 2, 1024])
    w_ple_r = din("w_ple_r", [L * 16 * 128, 256])
    NPAR = L * 48 + 16 + 8
    params = din("params", [128, NPAR])
    cbf = din("cbf", [128, 384], BF16)
    cf32 = din("cf32", [128, 6 * 128])
    masks = din("masks", [128, 2 * 16 * 512], BF16)
    smask = din("smask", [128, 2 * 32], BF16)
    if cfg.sample:
        ckT = din("ckT", [L, NH, HD, PAST])
        cv = din("cv", [L, PAST, NH * HD])
        clf = din("clf", [L, 8, 4, 128])

    yT = dout("yT", [D, NTOT])
    kT_out = dout("kT_out", [L, NH, HD, NTOT])
    v_out = dout("v_out", [L, NTOT, NH * HD])
    lf_out = dout("lf_out", [L, 4, NTOT])

    wb_in = dint("wb_in", [L * 32 * 128 * 2, 1024], BF16)
    wb_f = dint("wb_f", [L * 128, 64], BF16)
    wb_out = dint("wb_out", [L * 16 * 128, 1024], BF16)
    wb_g = dint("wb_g", [L * 16 * 128 * 2, 1024], BF16)
    wb_ple = dint("wb_ple", [L * 16 * 128, 256], BF16)
    xS = dint("xS", [D, NTOT], F32)
    qS = dint("qS", [NH, HD, NTOT], BF16)
    gS = dint("gS", [NH, HD, NTOT], BF16)
    ogS = dout("ogS", [NH, HD, NTOT], BF16) if getattr(cfg, "debug", False) else dint("ogS", [NH, HD, NTOT], BF16)
    skK = [[dint(f"skK{l}_{t}", [1024, 512], BF16) for t in range(NT)] for l in range(L)]
    skV = [[dint(f"skV{l}_{t}", [512, 1024], BF16) for t in range(NT)] for l in range(L)]
    gK = [[dint(f"gK{l}_{t}", [4 * 1024, 512], BF16) for t in range(NT)] for l in range(L)]
    gV = [[dint(f"gV{l}_{t}", [4 * 512, 1024], BF16) for t in range(NT)] for l in range(L)]
    slf = [dint(f"slf{l}", [4, NTOK], F32) for l in range(L)]
    glf = [dint(f"glf{l}", [16, NTOK], F32) for l in range(L)]
    sks = dint("sks", [NH, HD, DEC_T], BF16)
    svs = dint("svs", [DEC_T, NH * HD], BF16)
    slfs = dint("slfs", [4, DEC_T], F32)

    es = contextlib.ExitStack()
    with es:
        def sb(name, shape, dt):
            return es.enter_context(nc.sbuf_tensor(name, list(shape), dt))

        def ps(name):
            return es.enter_context(nc.psum_tensor(name, [128, 512], F32))

        sems = {e: es.enter_context(nc.semaphore("s_" + e)) for e in ("pe", "act", "dve", "pool")}
        dsems = {
            "sync": [es.enter_context(nc.semaphore(f"ds{i}")) for i in range(24)],
            "pool": [es.enter_context(nc.semaphore(f"dp{i}")) for i in range(12)],
            "cast": [es.enter_context(nc.semaphore(f"dc{i}")) for i in range(3)],
        }
        cc_sem = es.enter_context(nc.semaphore("cc"))
        sems["cc"] = cc_sem
        S = Sched(nc, sems, dsems)

        BIG = sb("BIG", [128, 32768], BF16)
        KT = BIG[:, 0:16384]
        VV = BIG[:, 16384:32768]
        xt = BIG[:, 0:16384].bitcast(F32).rearrange("p (k t) -> p k t", t=512)
        hT = BIG[:, 16384:24576].rearrange("p (k t) -> p k t", t=512)
        B_KT = [Buf(f"KT{g}") for g in range(8)]
        B_VVg = [Buf(f"VV{g}") for g in range(8)]
        B_VV0, B_VV1 = B_VVg[0:4], B_VVg[4:8]
        NWS = 6
        wring = [sb(f"w{i}", [128, 2048], BF16) for i in range(NWS)]
        b_wring = [Buf(f"w{i}") for i in range(NWS)]
        ogt = sb("ogt", [128, 8, 512], BF16)
        b_ogt = Buf("ogt")
        ptb = sb("ptb", [128, 2, 512], BF16)
        b_ptb = Buf("ptb")
        mask_t = sb("mask_t", [128, 2 * 16 * 512], BF16)
        smask_t = sb("smask_t", [128, 64], BF16)
        cb = sb("cb", [128, 384], BF16)
        cf = sb("cf", [128, 768], F32)
        par = sb("par", [128, NPAR], F32)
        b_const = Buf("const")
        ones_b, tri_b, omt_b = cb[:, 0:128], cb[:, 128:256], cb[:, 256:384]
        ident_f, ones_f, triinc_f, su_f, sel127_f, sel31_f = (cf[:, i * 128:(i + 1) * 128] for i in range(6))
        NST = 3
        stf = [sb(f"stf{i}", [128, 512], F32) for i in range(NST)]
        b_stf = [Buf(f"stf{i}") for i in range(NST)]
        stb = [sb(f"stb{i}", [128, 512], BF16) for i in range(NST)]
        b_stb = [Buf(f"stb{i}") for i in range(NST)]
        sqb = [sb(f"sqb{i}", [128, 512], BF16) for i in range(2)]
        b_sqb = [Buf(f"sqb{i}") for i in range(2)]
        rstd = sb("rstd", [128, 512], F32)
        b_rstd = Buf("rstd")
        tmpf = [sb(f"tmpf{i}", [128, 512], F32) for i in range(2)]
        b_tmpf = [Buf(f"tmpf{i}") for i in range(2)]
        qh = [sb(f"qh{i}", [128, 512], BF16) for i in range(2)]
        b_qh = [Buf(f"qh{i}") for i in range(2)]
        gh = [sb(f"gh{i}", [128, 512], BF16) for i in range(2)]
        b_gh = [Buf(f"gh{i}") for i in range(2)]
        qg_ctr = [0]

        def load_qg(hh, ti, c0, QW):
            i = qg_ctr[0] % 2
            qg_ctr[0] += 1
            S.dma("sync", qh[i][:, :QW], qS[hh, :, c0:c0 + QW], reads=[b_qS[hh][ti]], writes=[b_qh[i]])
            S.dma("sync", gh[i][:, :QW], gS[hh, :, c0:c0 + QW], reads=[b_gS[hh][ti]], writes=[b_gh[i]])
            return i
        NE = 4
        e_t = [sb(f"e{i}", [128, 512], F32) for i in range(NE)]
        b_e = [Buf(f"e{i}") for i in range(NE)]
        sp_t = [sb(f"sp{i}", [128, 512], BF16) for i in range(NE)]
        b_sp = [Buf(f"sp{i}") for i in range(NE)]
        t_t = [sb(f"t{i}", [128, 512], F32) for i in range(NE)]
        b_t = [Buf(f"t{i}") for i in range(NE)]
        NW = 3
        w_t = [sb(f"wt{i}", [128, 512], BF16) for i in range(NW)]
        b_w = [Buf(f"wt{i}") for i in range(NW)]
        ncb = sb("ncb", [128, 4, 128], F32)
        b_ncb = Buf("ncb")
        lnp = sb("lnp", [128, 4, 128], F32)
        b_lnp = Buf("lnp")
        ltb = sb("ltb", [128, 128], F32)
        b_ltb = Buf("ltb")
        tott = sb("tott", [128, 128], F32)
        b_tott = Buf("tott")
        xn = sb("xn", [128, 128], F32)
        b_xn = Buf("xn")
        ncref = sb("ncref", [128, 128], F32)
        b_ncref = Buf("ncref")
        bt = [sb(f"bt{i}", [128, 4, 128], F32) for i in range(2)]
        b_bt = [Buf(f"bt{i}") for i in range(2)]
        lfst = sb("lfst", [4, 512], F32)
        b_lfst = Buf("lfst")
        kts = sb("kts", [128, PAST + DEC_T], BF16)
        b_kts = Buf("kts")
        vs = sb("vs", [128, 9, 128], BF16)
        b_vs = Buf("vs")
        lnps = sb("lnps", [128, 4, 128], F32)
        b_lnps = Buf("lnps")
        ncbs = sb("ncbs", [128, 4, 16], F32)
        b_ncbs = Buf("ncbs")
        xs_ref = sb("xs_ref", [128, 16], F32)
        b_xsref = Buf("xs_ref")

        pbank = [ps(f"pb{i}") for i in range(8)]
        b_pb = [Buf(f"pb{i}", excl=True) for i in range(8)]

        b_xS = [Buf(f"xS{t}") for t in range(NT + 1)]
        b_qS = [[Buf(f"qS{h}_{t}") for t in range(NT + 1)] for h in range(NH)]
        b_gS = [[Buf(f"gS{h}_{t}") for t in range(NT + 1)] for h in range(NH)]
        b_ogS = [[Buf(f"ogS{h}_{t}") for t in range(NT + 1)] for h in range(NH)]
        b_skK = [[[] for _ in range(NT)] for _ in range(L)]
        b_skV = [[[] for _ in range(NT)] for _ in range(L)]
        b_gK = [[Buf(f"gK{l}_{t}") for t in range(NT)] for l in range(L)]
        b_gV = [[Buf(f"gV{l}_{t}") for t in range(NT)] for l in range(L)]
        b_slf = [[] for _ in range(L)]
        b_glf = [Buf(f"glf{l}") for l in range(L)]
        b_wb = {}
        b_sks, b_svs, b_slfs = [Buf(f"sks{h}") for h in range(NH)], [Buf(f"svs{h}") for h in range(NH)], Buf("slfs")

        def pcol(l, what, k=None):
            base = l * 48
            if what == "g_attn":
                return par[:, base + k: base + k + 1]
            if what == "g_ple":
                return par[:, base + 16 + k: base + 16 + k + 1]
            if what == "g_o":
                return par[:, base + 32 + k: base + 32 + k + 1]
            if what == "nbf":
                return par[0:4, base + 40: base + 41]
            raise KeyError(what)

        def gfin(k):
            return par[:, L * 48 + k: L * 48 + k + 1]

        onehot = par[:, L * 48 + 16: L * 48 + 20]

        S.dma("sync", cb[:], cbf[:, :], writes=[b_const])
        S.dma("sync", cf[:], cf32[:, :], writes=[b_const])
        S.dma("sync", par[:], params[:, :], writes=[b_const])
        S.dma("sync", mask_t[:], masks[:, :], writes=[b_const])
        S.dma("sync", smask_t[:], smask[:, :], writes=[b_const])

        def cast_rows(dst, src, r0, r1, key, step=2048):
            bl = []
            for a in range(r0, r1, step):
                b = min(a + step, r1)
                bb = Buf(f"wb_{key}_{a}")
                S.dma("cast", dst[a:b, :], src[a:b, :], writes=[bb])
                bl.append(bb)
            return bl

        def cast_layer(l, part):
            if part == 0:
                b_wb[("in", l)] = cast_rows(wb_in, w_in_r, l * 8192, l * 8192 + 4096, f"in{l}")
                b_wb[("f", l)] = cast_rows(wb_f, w_f_r, l * 128, (l + 1) * 128, f"f{l}")
            elif part == 1:
                b_wb[("in", l)] += cast_rows(wb_in, w_in_r, l * 8192 + 4096, (l + 1) * 8192, f"in{l}b")
            elif part == 2:
                b_wb[("out", l)] = cast_rows(wb_out, w_out_r, l * 2048, (l + 1) * 2048, f"out{l}")
                b_wb[("ple", l)] = cast_rows(wb_ple, w_ple_r, l * 2048, (l + 1) * 2048, f"ple{l}")
            elif part == 3:
                b_wb[("g", l)] = cast_rows(wb_g, w_g_r, l * 4096, (l + 1) * 4096, f"g{l}")

        for part in range(4):
            cast_layer(0, part)

        wctr = [0]

        def load_w(kind, l, c):
            i = wctr[0] % NWS
            wctr[0] += 1
            slot, bs = wring[i], b_wring[i]
            if kind == "in":
                r = (l * 32 + c) * 256
                S.dma("sync", slot[:, :], wb_in[r:r + 256, :].rearrange("(p t) x -> p (t x)", t=2),
                      reads=b_wb[("in", l)], writes=[bs])
                return slot[:, :].rearrange("p (k n) -> p k n", n=128), bs
            if kind == "g":
                r = (l * 16 + c) * 256
                S.dma("sync", slot[:, :], wb_g[r:r + 256, :].rearrange("(p t) x -> p (t x)", t=2),
                      reads=b_wb[("g", l)], writes=[bs])
                return slot[:, :].rearrange("p (k n) -> p k n", n=128), bs
            if kind == "out":
                r = (l * 16 + c) * 128
                S.dma("sync", slot[:, 0:1024], wb_out[r:r + 128, :], reads=b_wb[("out", l)], writes=[bs])
                return slot[:, 0:1024].rearrange("p (k n) -> p k n", n=128), bs
            if kind == "ple":
                r = (l * 16 + c) * 128
                S.dma("sync", slot[:, 0:256], wb_ple[r:r + 128, :], reads=b_wb[("ple", l)], writes=[bs])
                return slot[:, 0:256].rearrange("p (k n) -> p k n", n=128), bs
            if kind == "f":
                S.dma("sync", slot[:, 0:64], wb_f[l * 128:(l + 1) * 128, :], reads=b_wb[("f", l)], writes=[bs])
                return slot[:, 0:64].rearrange("p (k n) -> p k n", n=4), bs
            raise KeyError(kind)

        pa_ctr = [0]

        def pa_bank():
            i = pa_ctr[0] % 3
            pa_ctr[0] += 1
            return pbank[i], b_pb[i]

        st_ctr = [0]

        def stage():
            i = st_ctr[0] % NST
            st_ctr[0] += 1
            return stf[i], b_stf[i], stb[i], b_stb[i]

        def rms_stats(TW):
            acc, bacc = pbank[5], b_pb[5]
            for k in range(KC):
                i = k % 2
                S.op("act", lambda e, k=k, i=i: e.activation(out=sqb[i][:, :TW], in_=xt[:, k, :TW], func=AF.Square),
                     reads=[B_KT], writes=[b_sqb[i]])
                S.op("pe", lambda e, k=k, i=i: e.matmul(acc[:, :TW], lhsT=ones_b, rhs=sqb[i][:, :TW],
                                                        start=(k == 0), stop=(k == KC - 1)),
                     reads=[b_sqb[i], b_const], writes=[bacc])
            S.op("act", lambda e: e.activation(out=rstd[:, :TW], in_=acc[:, :TW], func=AF.Sqrt, bias=EPS_t[:, 0:1],
                                               scale=1.0 / D),
                 reads=[bacc, b_const], writes=[b_rstd])
            S.op("dve", lambda e: e.reciprocal(out=rstd[:, :TW], in_=rstd[:, :TW]), reads=[b_rstd], writes=[b_rstd])

        def make_h(TW, gname, l):
            for k in range(KC):
                g = pcol(l, gname, k) if gname != "fin" else gfin(k)
                S.op("dve", lambda e, k=k, g=g: e.scalar_tensor_tensor(
                    out=hT[:, k, :TW], in0=xt[:, k, :TW], scalar=g, in1=rstd[:, :TW], op0=ALU.mult, op1=ALU.mult),
                     reads=[B_KT, b_rstd, b_const], writes=[B_VV0])

        S.op("dve", lambda e: e.memset(xn[:, :], 0.0), writes=[b_xn])
        EPS_t = sb("eps_t", [128, 2], F32)
        S.op("dve", lambda e: e.memset(EPS_t[:, 0:1], EPS), writes=[b_const])
        S.op("dve", lambda e: e.memset(EPS_t[:, 1:2], 1.0), writes=[b_const])

        def phase_c(l, ti, TW, c0):
            S.dma("sync", ogt[:, :, :TW], ogS[:, :, c0:c0 + TW].rearrange("h p t -> p h t"),
                  reads=[b_ogS[h][ti] for h in range(NH)], writes=[b_ogt])
            S.dma("pool", ptb[:, :, :TW], pT[l, :, c0:c0 + TW].rearrange("(k p) t -> p k t", p=128),
                  writes=[b_ptb])
            for c in range(KC):
                wv, bw = load_w("out", l, c)
                pb, bpb = pa_bank()
                for k in range(8):
                    S.op("pe", lambda e, k=k, wv=wv, pb=pb: e.matmul(pb[:, :TW], lhsT=wv[:, k, :], rhs=ogt[:, k, :TW],
                                                                      start=(k == 0), stop=(k == 7)),
                         reads=[bw, b_ogt], writes=[bpb])
                S.op("dve", lambda e, c=c, pb=pb: e.tensor_tensor(out=xt[:, c, :TW], in0=pb[:, :TW], in1=xt[:, c, :TW],
                                                                  op=ALU.add),
                     reads=[bpb, B_KT], writes=[B_KT])
            rms_stats(TW)
            make_h(TW, "g_ple", l)
            for c in range(KC):
                wv, bw = load_w("g", l, c)
                pb, bpb = pa_bank()
                for k in range(KC):
                    S.op("pe", lambda e, k=k, wv=wv, pb=pb: e.matmul(pb[:, :TW], lhsT=wv[:, k, :], rhs=hT[:, k, :TW],
                                                                      start=(k == 0), stop=(k == KC - 1)),
                         reads=[bw, B_VV0], writes=[bpb])
                i = c % 2
                S.op("act", lambda e, pb=pb, i=i: e.activation(out=tmpf[i][:, :TW], in_=pb[:, :TW], func=AF.Sigmoid),
                     reads=[bpb], writes=[b_tmpf[i]])
                wv2, bw2 = load_w("ple", l, c)
                pb2, bpb2 = pa_bank()
                for k in range(2):
                    S.op("pe", lambda e, k=k, wv2=wv2, pb2=pb2: e.matmul(pb2[:, :TW], lhsT=wv2[:, k, :],
                                                                          rhs=ptb[:, k, :TW], start=(k == 0), stop=(k == 1)),
                         reads=[bw2, b_ptb], writes=[bpb2])
                S.op("dve", lambda e, pb2=pb2, i=i: e.tensor_tensor(out=tmpf[i][:, :TW], in0=pb2[:, :TW],
                                                                    in1=tmpf[i][:, :TW], op=ALU.mult),
                     reads=[bpb2, b_tmpf[i]], writes=[b_tmpf[i]])
                S.op("dve", lambda e, c=c, i=i: e.tensor_tensor(out=xt[:, c, :TW], in0=xt[:, c, :TW],
                                                                in1=tmpf[i][:, :TW], op=ALU.add),
                     reads=[b_tmpf[i], B_KT], writes=[B_KT])

        def phase_a(l, ti, TW, c0, after_h=None):
            is_s = ti == NT
            AP_ = getattr(cfg, "aparts", 255)
            rms_stats(TW)
            make_h(TW, "g_attn", l)
            if after_h is not None:
                after_h()
            if not AP_ & 1:
                return
            nsub = max(TW // 128, 1)
            SW = min(TW, 128)
            pdma = S.dma if AP_ & 8 else (lambda *a, **k: None)
            kinds_ok = getattr(cfg, 'kinds', 'qkvz')
            for c in range(32):
                grp, hh4 = c // 4, c % 4
                kind = ("q", "k", "v", "z")[grp % 4]
                if kind not in kinds_ok:
                    continue
                hh = hh4 + (4 if grp >= 4 else 0)
                wv, bw = load_w("in", l, c)
                pb, bpb = pa_bank()
                sf, bsf, sbb, bsb = stage()
                if kind != "v":
                    for k in range(KC):
                        S.op("pe", lambda e, k=k, wv=wv, pb=pb: e.matmul(pb[:, :TW], lhsT=wv[:, k, :], rhs=hT[:, k, :TW],
                                                                          start=(k == 0), stop=(k == KC - 1)),
                             reads=[bw, B_VV0], writes=[bpb])
                else:
                    for s in range(nsub):
                        for k in range(KC):
                            S.op("pe", lambda e, k=k, s=s, wv=wv, pb=pb: e.matmul(
                                pb[:SW, s * 128:(s + 1) * 128], lhsT=hT[:, k, s * 128:s * 128 + SW], rhs=wv[:, k, :],
                                start=(k == 0), stop=(k == KC - 1)),
                                 reads=[bw, B_VV0], writes=[bpb])
                if kind == "q":
                    S.op("act", lambda e, pb=pb, sbb=sbb: e.activation(out=sbb[:, :TW], in_=pb[:, :TW], func=AF.Identity,
                                                                         scale=QSCALE),
                         reads=[bpb], writes=[bsb])
                    pdma("pool", qS[hh, :, c0:c0 + TW], sbb[:, :TW], reads=[bsb], writes=[b_qS[hh][ti]])
                elif kind == "z":
                    S.op("act", lambda e, pb=pb, sbb=sbb: e.activation(out=sbb[:, :TW], in_=pb[:, :TW], func=AF.Silu),
                         reads=[bpb], writes=[bsb])
                    pdma("pool", gS[hh, :, c0:c0 + TW], sbb[:, :TW], reads=[bsb], writes=[b_gS[hh][ti]])
                elif kind == "k":
                    S.op("act", lambda e, pb=pb, sf=sf: e.activation(out=sf[:, :TW], in_=pb[:, :TW], func=AF.Identity),
                         reads=[bpb], writes=[bsf])
                    S.op("dve", lambda e, pb=pb, sbb=sbb: e.tensor_copy(out=sbb[:, :TW], in_=pb[:, :TW]),
                         reads=[bpb], writes=[bsb])
                    pdma("pool", kT_out[l, hh, :, c0:c0 + TW], sf[:, :TW], reads=[bsf])
                    if not is_s:
                        bb = Buf("skvp")
                        b_skK[l][ti].append(bb)
                        pdma("pool", skK[l][ti][hh * 128:(hh + 1) * 128, :], sbb[:, :TW], reads=[bsb], writes=[bb])
                    else:
                        pdma("pool", sks[hh, :, :], sbb[:, :TW], reads=[bsb], writes=[b_sks[hh]])
                else:
                    W4 = nsub * 128
                    S.op("act", lambda e, pb=pb, sf=sf: e.activation(out=sf[:SW, :W4], in_=pb[:SW, :W4], func=AF.Identity),
                         reads=[bpb], writes=[bsf])
                    S.op("dve", lambda e, pb=pb, sbb=sbb: e.tensor_copy(out=sbb[:SW, :W4], in_=pb[:SW, :W4]),
                         reads=[bpb], writes=[bsb])
                    pdma("pool", v_out[l, c0:c0 + TW, hh * 128:(hh + 1) * 128].rearrange("(s t) d -> t s d", t=SW),
                          sf[:SW, :W4].rearrange("t (s d) -> t s d", d=128), reads=[bsf])
                    if not is_s:
                        bb = Buf("skvp")
                        b_skV[l][ti].append(bb)
                        pdma("pool", skV[l][ti][:, hh * 128:(hh + 1) * 128].rearrange("(s t) d -> t s d", t=SW),
                              sbb[:SW, :W4].rearrange("t (s d) -> t s d", d=128), reads=[bsb], writes=[bb])
                    else:
                        pdma("pool", svs[:, hh * 128:(hh + 1) * 128], sbb[:SW, :128], reads=[bsb], writes=[b_svs[hh]])
            if not AP_ & 2:
                return
            wv, bw = load_w("f", l, 0)
            pb, bpb = pa_bank()
            for k in range(KC):
                S.op("pe", lambda e, k=k, wv=wv, pb=pb: e.matmul(pb[0:4, :TW], lhsT=wv[:, k, :], rhs=hT[:, k, :TW],
                                                                  start=(k == 0), stop=(k == KC - 1)),
                     reads=[bw, B_VV0], writes=[bpb])
            S.op("act", lambda e, pb=pb: e.activation(out=lfst[:, :TW], in_=pb[0:4, :TW], func=AF.Exp,
                                                      bias=pcol(l, "nbf"), scale=-1.0),
                 reads=[bpb, b_const], writes=[b_lfst])
            S.op("act", lambda e: e.activation(out=lfst[:, :TW], in_=lfst[:, :TW], func=AF.Ln, bias=EPS_t[0:4, 1:2]),
                 reads=[b_lfst, b_const], writes=[b_lfst])
            S.op("dve", lambda e: e.tensor_scalar(out=lfst[:, :TW], in0=lfst[:, :TW], scalar1=-1.0, scalar2=None,
                                                  op0=ALU.mult),
                 reads=[b_lfst], writes=[b_lfst])
            S.dma("pool", lf_out[l, :, c0:c0 + TW], lfst[:, :TW], reads=[b_lfst])
            if not is_s:
                bb = Buf("slfp")
                b_slf[l].append(bb)
                S.dma("pool", slf[l][:, c0:c0 + TW], lfst[:, :TW], reads=[b_lfst], writes=[bb])
            else:
                S.dma("pool", slfs[:, :], lfst[:, :TW], reads=[b_lfst], writes=[b_slfs])
            if not AP_ & 4:
                return
            if not is_s:
                all_gather(skK[l][ti], gK[l][ti], b_skK[l][ti], b_gK[l][ti])
                all_gather(skV[l][ti], gV[l][ti], b_skV[l][ti], b_gV[l][ti])
                if ti == NT - 1:
                    all_gather(slf[l], glf[l], b_slf[l], b_glf[l])

        def final_norm(ti, TW, c0):
            rms_stats(TW)
            for k in range(KC):
                i = k % 2
                S.op("dve", lambda e, k=k, i=i: e.scalar_tensor_tensor(
                    out=tmpf[i][:, :TW], in0=xt[:, k, :TW], scalar=gfin(k), in1=rstd[:, :TW], op0=ALU.mult, op1=ALU.mult),
                     reads=[B_KT, b_rstd, b_const], writes=[b_tmpf[i]])
                S.dma("pool", yT[k * 128:(k + 1) * 128, c0:c0 + TW], tmpf[i][:, :TW], reads=[b_tmpf[i]])

        def tile_pass(l):
            tiles = list(range(NT)) + ([NT] if cfg.sample else [])

            def load_x(ti, queue):
                TW = 512 if ti < NT else DEC_T
                c0 = ti * 512
                src = xT if l == 0 else xS
                S.dma(queue, xt[:, :, :TW], src[:, c0:c0 + TW].rearrange("(k p) t -> p k t", p=128),
                      reads=([b_xS[ti]] if l > 0 else []), writes=[B_KT])

            for idx, ti in enumerate(tiles):
                TW = 512 if ti < NT else DEC_T
                c0 = ti * 512
                if idx == 0 or l == L:
                    load_x(ti, "sync")
                if l > 0:
                    phase_c(l - 1, ti, TW, c0)
                if l < L:
                    S.dma("pool", xS[:, c0:c0 + TW].rearrange("(k p) t -> p k t", p=128), xt[:, :, :TW],
                          reads=[B_KT], writes=[b_xS[ti]])
                    nxt = tiles[idx + 1] if idx + 1 < len(tiles) else None
                    phase_a(l, ti, TW, c0,
                            after_h=(lambda nxt=nxt: load_x(nxt, "pool")) if nxt is not None else None)
                else:
                    final_norm(ti, TW, c0)

        def all_gather(src, dst, reads, bdst):
            S.cc_n += 1
            n = S.cc_n
            S.custom("pool", lambda e, src=src, dst=dst: e.collective_compute(
                "AllGather", ALU.bypass, replica_groups=[[0, 1, 2, 3], [4, 5, 6, 7]],
                ins=[src.ap().opt()], outs=[dst.ap().opt()]).then_inc(cc_sem, 1),
                     "cc", n, reads=reads, writes=[bdst])

        S.cc_n = 0

        def cumsum_tables(l):
            for ip in range(NT):
                for jp in range(4):
                    S.dma("sync", lnp[16 * ip + 4 * jp:16 * ip + 4 * jp + 4, :, :],
                          glf[l][jp * 4:(jp + 1) * 4, 512 * ip:512 * ip + 512].rearrange("h (s p) -> s h p", p=128),
                          reads=[b_glf[l]], writes=[b_lnp])
            cum_core(lnp, b_lnp, NB, ncb, b_ncb, NB)
            for h in range(4):
                for jp in range(4):
                    src = ncb[:, h, :NB].rearrange("p (i m) -> p i m", m=16)[:, :, 4 * jp:4 * jp + 4]
                    dst = xn[:, h * 32:h * 32 + NT * 4].rearrange("p (i s) -> p i s", s=4)
                    if jp == 0:
                        S.op("dve", lambda e, src=src, dst=dst: e.tensor_scalar(
                            out=dst, in0=src, scalar1=onehot[:, 0:1], scalar2=None, op0=ALU.mult),
                             reads=[b_ncb, b_const], writes=[b_xn])
                    else:
                        S.op("dve", lambda e, src=src, dst=dst, jp=jp: e.scalar_tensor_tensor(
                            out=dst, in0=src, scalar=onehot[:, jp:jp + 1], in1=dst, op0=ALU.mult, op1=ALU.add),
                             reads=[b_ncb, b_const, b_xn], writes=[b_xn])
            pb, bpb = pbank[6], b_pb[6]
            S.op("pe", lambda e: e.matmul(pb[:, 0:128], lhsT=sel127_f, rhs=xn[:, :], start=True, stop=True),
                 reads=[b_xn, b_const], writes=[bpb])
            S.op("act", lambda e: e.activation(out=ncref[:, :], in_=pb[:, 0:128], func=AF.Identity), reads=[bpb],
                 writes=[b_ncref])

        def cum_core(Lsrc, bL, nb, dst, bdst, nbp):
            pb, bpb = pbank[6], b_pb[6]
            pb2, bpb2 = pbank[7], b_pb[7]
            for h in range(4):
                S.op("pe", lambda e, h=h: e.transpose(pb[:, 0:nb], Lsrc[:nb, h, :], ident_f[:nb, :nb]),
                     reads=[bL, b_const], writes=[bpb])
                S.op("act", lambda e: e.activation(out=ltb[:, :nb], in_=pb[:, 0:nb], func=AF.Identity), reads=[bpb],
                     writes=[b_ltb])
                S.op("pe", lambda e: e.matmul(pb2[:nb, 0:128], lhsT=ltb[:, :nb], rhs=ones_f, start=True, stop=True),
                     reads=[b_ltb, b_const], writes=[bpb2])
                S.op("act", lambda e: e.activation(out=tott[:nb, :], in_=pb2[:nb, 0:128], func=AF.Identity), reads=[bpb2],
                     writes=[b_tott])
                S.op("pe", lambda e: e.matmul(pb[:, 128:128 + nb], lhsT=triinc_f, rhs=ltb[:, :nb], start=True, stop=False),
                     reads=[b_ltb, b_const], writes=[bpb])
                S.op("pe", lambda e: e.matmul(pb[:, 128:128 + nb], lhsT=tott[:nb, :], rhs=su_f[:nb, :nb], start=False,
                                              stop=True),
                     reads=[b_tott, b_const], writes=[bpb])
                S.op("act", lambda e, h=h: e.activation(out=dst[:, h, :nb], in_=pb[:, 128:128 + nb], func=AF.Identity,
                                                        scale=-1.0),
                     reads=[bpb], writes=[bdst])

        ectr = [0]
        wctr2 = [0]

        acc_t = [sb(f"acc{i}", [128, 512], BF16) for i in range(2)]
        b_acc = [Buf(f"acc{i}") for i in range(2)]
        rc = {"z": 0, "e": 0, "t": 0, "w": 0, "p": 0, "a": 0}

        def attn_tile(hh, QW, q_ap, bq, steps, bias_of, kind, ob=0, hook=None):
            Zb = (0, 1, 2)
            Pb = (6, 7)
            O, bO = pbank[3 + ob], b_pb[3 + ob]
            DEN, bDEN = pbank[6 + ob], b_pb[6 + ob]
            ns = len(steps)
            R = [dict() for _ in range(ns)]

            def qk(si):
                kt_ap, v_ap, KP, m_ap, rds = steps[si]
                zi = Zb[rc["z"] % 3]
                rc["z"] += 1
                R[si]["z"] = (pbank[zi], b_pb[zi])
                z = pbank[zi]
                S.op("pe", lambda e, z=z, kt_ap=kt_ap, KP=KP: e.matmul(z[:KP, :QW], lhsT=kt_ap, rhs=q_ap, start=True,
                                                                        stop=True),
                     reads=rds + [bq], writes=[b_pb[zi]])

            def pv(si):
                kt_ap, v_ap, KP, m_ap, rds = steps[si]
                wt, bw = R[si]["w"]
                first, last = si == 0, si == ns - 1
                S.op("pe", lambda e, wt=wt, v_ap=v_ap, KP=KP, first=first, last=last: e.matmul(
                    O[:, :QW], lhsT=v_ap, rhs=wt[:KP, :QW], start=first, stop=last),
                     reads=rds + [bw], writes=[bO])
                if kind == "fox":
                    S.op("pe", lambda e, wt=wt, KP=KP, first=first, last=last: e.matmul(
                        DEN[:, :QW], lhsT=ones_b[:KP, :], rhs=wt[:KP, :QW], start=first, stop=last),
                         reads=[bw, b_const], writes=[bDEN])

            def take_w(si):
                wi = rc["w"] % NW
                rc["w"] += 1
                R[si]["w"] = (w_t[wi], b_w[wi])
                return w_t[wi], b_w[wi]

            def fox_x(si):
                kt_ap, v_ap, KP, m_ap, rds = steps[si]
                z, bz = R[si]["z"]
                wt, bw = take_w(si)
                sw_ = min(QW, 256)
                for sq in range(max(QW // 256, 1)):
                    bcol, bbt = bias_of(si, sq)
                    S.op("act", lambda e, z=z, wt=wt, KP=KP, bcol=bcol, sq=sq, sw_=sw_: e.activation(
                        out=wt[:KP, sq * sw_:(sq + 1) * sw_], in_=z[:KP, sq * sw_:(sq + 1) * sw_], func=AF.Exp,
                        bias=bcol),
                         reads=[bz, bbt], writes=[bw])
                if m_ap is not None:
                    S.op("dve", lambda e, wt=wt, m_ap=m_ap, KP=KP: e.tensor_tensor(
                        out=wt[:KP, :QW], in0=wt[:KP, :QW], in1=m_ap, op=ALU.mult),
                         reads=[bw, b_const], writes=[bw])

            hook_at = min(3, ns - 1)
            if kind == "fox":
                qk(0)
                if ns > 1:
                    qk(1)
                for si in range(ns):
                    if si + 2 < ns:
                        qk(si + 2)
                    fox_x(si)
                    if si >= 1:
                        pv(si - 1)
                    if si == hook_at and hook is not None:
                        hook()
                pv(ns - 1)
                return

            def sb_e(si):
                kt_ap, v_ap, KP, m_ap, rds = steps[si]
                z, bz = R[si]["z"]
                ei = rc["e"] % NE
                rc["e"] += 1
                et, be, spt, bsp = e_t[ei], b_e[ei], sp_t[ei], b_sp[ei]
                R[si]["e"] = (et, be)
                R[si]["sp"] = (spt, bsp)
                S.op("act", lambda e, z=z, et=et, KP=KP: e.activation(out=et[:KP, :QW], in_=z[:KP, :QW], func=AF.Exp),
                     reads=[bz], writes=[be])
                if m_ap is not None:
                    S.op("dve", lambda e, et=et, m_ap=m_ap, KP=KP: e.tensor_tensor(
                        out=et[:KP, :QW], in0=et[:KP, :QW], in1=m_ap, op=ALU.mult),
                         reads=[be, b_const], writes=[be])

            def sb_l(si):
                kt_ap, v_ap, KP, m_ap, rds = steps[si]
                et, be = R[si]["e"]
                spt, bsp = R[si]["sp"]
                S.op("act", lambda e, et=et, spt=spt, KP=KP: e.activation(out=spt[:KP, :QW], in_=et[:KP, :QW],
                                                                          func=AF.Ln, bias=EPS_t[:KP, 1:2]),
                     reads=[be, b_const], writes=[bsp])

            def sb_p(si):
                kt_ap, v_ap, KP, m_ap, rds = steps[si]
                spt, bsp = R[si]["sp"]
                pi = Pb[rc["p"] % 2]
                rc["p"] += 1
                P, bP = pbank[pi], b_pb[pi]
                R[si]["p"] = (P, bP)
                S.op("pe", lambda e, spt=spt, KP=KP, P=P, si=si: e.matmul(
                    P[:KP, :QW], lhsT=tri_b[:KP, :KP], rhs=spt[:KP, :QW], start=True, stop=(si == 0)),
                     reads=[bsp, b_const], writes=[bP])
                if si > 0:
                    at, ba = R[si]["acc"]
                    S.op("pe", lambda e, at=at, KP=KP, P=P: e.matmul(
                        P[:KP, :QW], lhsT=ones_b[:, :KP], rhs=at[:, :QW], start=False, stop=True),
                         reads=[ba, b_const], writes=[bP])

            def sb_acc(si):
                kt_ap, v_ap, KP, m_ap, rds = steps[si]
                spt, bsp = R[si]["sp"]
                ai = rc["a"] % 2
                rc["a"] += 1
                an, ban = acc_t[ai], b_acc[ai]
                R[si + 1]["acc"] = (an, ban)
                if si == 0:
                    if KP < 128:
                        S.op("dve", lambda e, an=an: e.memset(an[:, :QW], 0.0), writes=[ban])
                    S.op("dve", lambda e, an=an, spt=spt, KP=KP: e.tensor_copy(out=an[:KP, :QW], in_=spt[:KP, :QW]),
                         reads=[bsp], writes=[ban])
                else:
                    ac, bac = R[si]["acc"]
                    if KP < 128:
                        raise NotImplementedError
                    S.op("dve", lambda e, an=an, ac=ac, spt=spt: e.tensor_tensor(
                        out=an[:, :QW], in0=ac[:, :QW], in1=spt[:, :QW], op=ALU.add),
                         reads=[bac, bsp], writes=[ban])

            def sb_tw(si):
                kt_ap, v_ap, KP, m_ap, rds = steps[si]
                et, be = R[si]["e"]
                P, bP = R[si]["p"]
                ti_ = rc["t"] % NE
                rc["t"] += 1
                tt, btt = t_t[ti_], b_t[ti_]
                wt, bw = take_w(si)
                S.op("act", lambda e, tt=tt, KP=KP, P=P: e.activation(out=tt[:KP, :QW], in_=P[:KP, :QW], func=AF.Exp,
                                                                     scale=-1.0),
                     reads=[bP], writes=[btt])
                S.op("dve", lambda e, wt=wt, et=et, tt=tt, KP=KP: e.tensor_tensor(
                    out=wt[:KP, :QW], in0=et[:KP, :QW], in1=tt[:KP, :QW], op=ALU.mult),
                     reads=[be, btt], writes=[bw])

            qk(0)
            sb_e(0)
            if ns > 1:
                qk(1)
                sb_e(1)
            sb_l(0)
            if ns > 1:
                sb_l(1)
            for si in range(ns):
                if si + 2 < ns:
                    qk(si + 2)
                    sb_e(si + 2)
                sb_p(si)
                if si + 1 < ns:
                    sb_acc(si)
                sb_tw(si)
                if si + 2 < ns:
                    sb_l(si + 2)
                if si >= 1:
                    pv(si - 1)
                if si == hook_at and hook is not None:
                    hook()
            pv(ns - 1)

        def epilogue(l, hh, QW, c0, ti, gate_ap, bgate, kind, ob=0):
            O, bO = pbank[3 + ob], b_pb[3 + ob]
            DEN, bDEN = pbank[6 + ob], b_pb[6 + ob]
            SS, bSS = pbank[5], b_pb[5]
            i = ectr[0] % 2
            if kind == "fox":
                S.op("dve", lambda e: e.reciprocal(out=tmpf[0][:, :QW], in_=DEN[:, :QW]), reads=[bDEN], writes=[b_tmpf[0]])
                S.op("dve", lambda e: e.tensor_tensor(out=tmpf[1][:, :QW], in0=O[:, :QW], in1=tmpf[0][:, :QW], op=ALU.mult),
                     reads=[bO, b_tmpf[0]], writes=[b_tmpf[1]])
                u_ap, bu = tmpf[1], b_tmpf[1]
            else:
                u_ap, bu = O, bO
            S.op("act", lambda e: e.activation(out=sqb[i][:, :QW], in_=u_ap[:, :QW], func=AF.Square), reads=[bu],
                 writes=[b_sqb[i]])
            S.op("pe", lambda e: e.matmul(SS[:, :QW], lhsT=ones_b, rhs=sqb[i][:, :QW], start=True, stop=True),
                 reads=[b_sqb[i], b_const], writes=[bSS])
            S.op("act", lambda e: e.activation(out=tmpf[0][:, :QW], in_=SS[:, :QW], func=AF.Ln, bias=EPS_t[:, 0:1],
                                               scale=1.0 / 128),
                 reads=[bSS, b_const], writes=[b_tmpf[0]])
            S.op("act", lambda e: e.activation(out=tmpf[0][:, :QW], in_=tmpf[0][:, :QW], func=AF.Exp, scale=-0.5),
                 reads=[b_tmpf[0]], writes=[b_tmpf[0]])
            S.op("dve", lambda e: e.scalar_tensor_tensor(out=tmpf[1][:, :QW], in0=u_ap[:, :QW], scalar=pcol(l, "g_o", hh),
                                                         in1=tmpf[0][:, :QW], op0=ALU.mult, op1=ALU.mult),
                 reads=[bu, b_tmpf[0], b_const], writes=[b_tmpf[1]])
            sf, bsf, sbb, bsb = stage()
            S.op("dve", lambda e, sbb=sbb: e.tensor_tensor(out=sbb[:, :QW], in0=tmpf[1][:, :QW], in1=gate_ap, op=ALU.mult),
                 reads=[b_tmpf[1], bgate], writes=[bsb])
            S.dma("pool", ogS[hh, :, c0:c0 + QW], sbb[:, :QW], reads=[bsb], writes=[b_ogS[hh][ti]])

        ob_ctr = [0]
        pend = [None]

        def next_ob():
            i = ob_ctr[0] % 2
            ob_ctr[0] += 1
            return i

        def phase_b(l):
            cumsum_tables(l)
            if cfg.sample:
                sample_tables(l)
            if getattr(cfg, "stop", 99) < 4:
                return
            for hh in range(NH if getattr(cfg, "stop", 99) >= 5 else 1):
                kind = "sb" if hh < 4 else "fox"
                if l + 1 < L and hh < 4:
                    cast_layer(l + 1, hh)
                qi = hh % 2
                def load_kv(hx, ip):
                    for jp in range(4):
                        n0 = 16 * ip + 4 * jp
                        S.dma("sync", KT[:, n0 * 128:(n0 + 4) * 128],
                              gK[l][ip][jp * 1024 + hx * 128: jp * 1024 + hx * 128 + 128, :],
                              reads=[b_gK[l][ip]], writes=[B_KT[ip]])
                        S.dma("sync", VV[:, n0 * 128:(n0 + 4) * 128].rearrange("p (s d) -> p s d", d=128),
                              gV[l][ip][jp * 512:(jp + 1) * 512, hx * 128:(hx + 1) * 128].rearrange("(s p) d -> p s d", p=128),
                              reads=[b_gV[l][ip]], writes=[B_VVg[ip]])

                for ip in (range(NT) if hh == 0 else range(1)):
                    load_kv(hh, ip)
                def fox_bias(ti):
                    h = hh - 4
                    bi = ti % 2
                    nkb_ = 16 * ti + 16
                    for sq in range(2):
                        cix = h * 32 + ti * 4 + 2 * sq
                        S.op("dve", lambda e, h=h, bi=bi, sq=sq, cix=cix, nkb_=nkb_: e.tensor_scalar(
                            out=bt[bi][:, sq, :nkb_], in0=ncb[:, h, :nkb_], scalar1=ncref[:, cix:cix + 1],
                            scalar2=60.0, op0=ALU.subtract, op1=ALU.min),
                             reads=[b_ncb, b_ncref], writes=[b_bt[bi]])

                if kind == "fox":
                    fox_bias(NT - 1)
                for ti in range(NT - 1, -1, -1):
                    nkb = 16 * ti + 16
                    order = list(range(nkb - 1, -1, -1)) if kind == "sb" else list(range(nkb))
                    steps = []
                    for n in order:
                        m = n - 16 * ti
                        m_ap = None
                        if m >= 0:
                            off = ((0 if kind == "sb" else 1) * 16 + m) * 512
                            m_ap = mask_t[:, off:off + 512]
                        steps.append((KT[:, n * 128:(n + 1) * 128], VV[:, n * 128:(n + 1) * 128], 128, m_ap,
                                      [B_KT[n // 16], B_VVg[n // 16]]))
                    bias_of = None
                    if kind == "fox":
                        bi = ti % 2
                        bias_of = lambda si, sq, bi=bi, order=order: (bt[bi][:, sq, order[si]:order[si] + 1], b_bt[bi])
                    qi = load_qg(hh, ti, ti * 512, 512)
                    ob = next_ob()
                    hk = pend[0]
                    pend[0] = None
                    if kind == "fox" and ti >= 1:
                        hk0 = hk
                        hk = lambda hk0=hk0, ti=ti: ((hk0() if hk0 is not None else None), fox_bias(ti - 1))
                    attn_tile(hh, 512, qh[qi][:, :], b_qh[qi], steps, bias_of, kind, ob, hook=hk)
                    pend[0] = (lambda l=l, hh=hh, ti=ti, qi=qi, kind=kind, ob=ob: epilogue(
                        l, hh, 512, ti * 512, ti, gh[qi][:, :], b_gh[qi], kind, ob))
                    if hh + 1 < NH and ti + 1 <= NT - 1:
                        load_kv(hh + 1, ti + 1)
                if cfg.sample:
                    sample_attn(l, hh, kind, qi)
            if pend[0] is not None:
                pend[0]()
                pend[0] = None

        def sample_tables(l):
            S.op("dve", lambda e: e.memset(lnps[:, :, :], 0.0), writes=[b_lnps])
            S.dma("sync", lnps[0:8, :, :], clf[l, :, :, :], reads=[], writes=[b_lnps])
            S.dma("sync", lnps[8:9, :, 0:DEC_T], slfs[:, :].rearrange("(o h) p -> o h p", o=1), reads=[b_slfs],
                  writes=[b_lnps])
            cum_core(lnps, b_lnps, 9, ncbs, b_ncbs, 9)

        def sample_attn(l, hh, kind, qi):
            c0 = NTOK
            QW = DEC_T
            S.dma("pool", kts[:, 0:PAST], ckT[l, hh, :, :], writes=[b_kts])
            S.dma("sync", kts[:, PAST:PAST + DEC_T], sks[hh, :, :], reads=[b_sks[hh]], writes=[b_kts])
            S.dma("pool", vs[:, 0:8, :], cv[l, :, hh * 128:(hh + 1) * 128].rearrange("(n p) d -> p n d", p=128),
                  writes=[b_vs])
            S.dma("sync", vs[0:DEC_T, 8, :], svs[:, hh * 128:(hh + 1) * 128], reads=[b_svs[hh]], writes=[b_vs])
            qi = load_qg(hh, NT, c0, QW)
            q_ap = qh[qi][:, :QW]
            mk = smask_t[0:DEC_T, (0 if kind == "sb" else 32):(0 if kind == "sb" else 32) + 32]
            blocks = [(kts[:, PAST:PAST + DEC_T], vs[0:DEC_T, 8, :], DEC_T, mk, [b_kts, b_vs])]
            for n in range(7, -1, -1):
                blocks.append((kts[:, n * 128:(n + 1) * 128], vs[:, n, :], 128, None, [b_kts, b_vs]))
            bias_of = None
            if kind == "fox":
                h = hh - 4
                pb, bpb = pbank[5], b_pb[5]
                S.op("pe", lambda e, h=h: e.matmul(pb[:, 0:16], lhsT=sel31_f, rhs=ncbs[:, h, :], start=True, stop=True),
                     reads=[b_ncbs, b_const], writes=[bpb])
                S.op("act", lambda e: e.activation(out=xs_ref[:, 0:16], in_=pb[:, 0:16], func=AF.Identity), reads=[bpb],
                     writes=[b_xsref])
                S.op("dve", lambda e, h=h: e.tensor_scalar(out=bt[0][:, 0, 0:9], in0=ncbs[:, h, 0:9], scalar1=xs_ref[:, 8:9],
                                                           scalar2=0.0, op0=ALU.subtract, op1=ALU.min),
                     reads=[b_ncbs, b_xsref], writes=[b_bt[0]])
                nidx = [8] + list(range(7, -1, -1))
                bias_of = lambda si, sq: (bt[0][:(DEC_T if si == 0 else 128), 0, nidx[si]:nidx[si] + 1], b_bt[0])
            ob = next_ob()
            hk = pend[0]
            pend[0] = None
            attn_tile(hh, QW, q_ap, b_qh[qi], blocks, bias_of, kind, ob, hook=hk)
            pend[0] = (lambda l=l, hh=hh, qi=qi, kind=kind, ob=ob: epilogue(
                l, hh, QW, c0, NT, gh[qi][:, :QW], b_gh[qi], kind, ob))

        stop = getattr(cfg, "stop", 99)
        if stop >= 2:
            for l in range(L):
                tile_pass(l)
                if stop >= 3:
                    phase_b(l)
            if stop >= 6:
                tile_pass(L)
        S.final_waits("sync")

        with nc.Block() as block:
            @block.tensor
            def _(e):
                for f in S.q["pe"]:
                    f(e)

            @block.scalar
            def _(e):
                for f in S.q["act"]:
                    f(e)

            @block.vector
            def _(e):
                for f in S.q["dve"]:
                    f(e)

            @block.gpsimd
            def _(e):
                for f in S.q["pool"]:
                    f(e)

            @block.sync
            def _(e):
                for f in S.q["sync"]:
                    f(e)
    return nc, S


def _consts():
    k = np.arange(128)
    ones = np.ones((128, 128), np.float32)
    tri = (k[:, None] >= k[None, :]).astype(np.float32)
    omt = 1.0 - tri
    cbf = np.concatenate([ones, tri, omt], axis=1).astype(ml_dtypes.bfloat16)
    ident = np.eye(128, dtype=np.float32)
    triinc = (k[:, None] <= k[None, :]).astype(np.float32)
    su = (k[:, None] < k[None, :]).astype(np.float32)
    sel = np.zeros((128, 128), np.float32)
    sel[127, :] = 1.0
    sel31 = np.zeros((128, 128), np.float32)
    sel31[31, :] = 1.0
    cf32 = np.concatenate([ident, ones, triinc, su, sel, sel31], axis=1).astype(np.float32)
    return cbf, cf32


def _masks(j):
    k = np.arange(128)
    out = np.zeros((128, 2, 16, 512), np.float32)
    tri_sb = (k[:, None] < k[None, :]).astype(np.float32)
    tri_fx = (k[:, None] <= k[None, :]).astype(np.float32)
    for kind, tri in enumerate((tri_sb, tri_fx)):
        for m in range(16):
            d = m - 4 * j
            for s in range(4):
                if d < s:
                    out[:, kind, m, s * 128:(s + 1) * 128] = 1.0
                elif d == s:
                    out[:, kind, m, s * 128:(s + 1) * 128] = tri
    return out.reshape(128, 2 * 16 * 512).astype(ml_dtypes.bfloat16)


def _smask():
    k = np.arange(32)
    out = np.zeros((128, 64), np.float32)
    out[:32, 0:32] = (k[:, None] < k[None, :])
    out[:32, 32:64] = (k[:, None] <= k[None, :])
    return out.astype(ml_dtypes.bfloat16)


_PROG_CACHE = {}


def run(cfg, inputs):
    NT, L, NTOK, NTOT = cfg.NT, cfg.DEPTH, cfg.NTOK, cfg.NTOT
    f32 = np.float32
    g = {k: np.asarray(v) for k, v in inputs.items()}
    key = (NT, L, cfg.sample)
    if key not in _PROG_CACHE:
        _PROG_CACHE[key] = build_program(cfg)
    nc, S = _PROG_CACHE[key]

    w_in = g["w_in"].astype(f32, copy=False)
    w_in_r = w_in[:, :, :4096].reshape(L, KC, 128, 32, 128).transpose(0, 3, 2, 1, 4).reshape(L * 32 * 128 * 2, 1024)
    w_f_r = w_in[:, :, 4096:4100].reshape(L, KC, 128, 4).transpose(0, 2, 1, 3).reshape(L * 128, 64)
    w_out_r = g["w_out"].reshape(L, 8, 128, 16, 128).transpose(0, 3, 2, 1, 4).reshape(L * 16 * 128, 1024)
    w_g_r = g["w_ple_gate"].reshape(L, KC, 128, 16, 128).transpose(0, 3, 2, 1, 4).reshape(L * 16 * 128 * 2, 1024)
    w_ple_r = g["w_ple"].reshape(L, 2, 128, 16, 128).transpose(0, 3, 2, 1, 4).reshape(L * 16 * 128, 256)
    w_in_r, w_f_r, w_out_r, w_g_r, w_ple_r = (np.ascontiguousarray(a, dtype=f32) for a in
                                              (w_in_r, w_f_r, w_out_r, w_g_r, w_ple_r))
    cbf, cf32 = _consts()
    smask = _smask()
    NPAR = L * 48 + 16 + 8
    in_maps = []
    for c in range(8):
        b, j = c // 4, c % 4
        tiles = [4 * i + j for i in range(NT)]
        xp = g["x_prompt"][b].reshape(4 * NT, 512, D)[tiles].reshape(NTOK, D)
        xs = g["x_sample"][c]
        xTc = np.ascontiguousarray(np.concatenate([xp, xs], axis=0).T, dtype=f32)
        pp = g["p_prompt"][:, b].reshape(L, 4 * NT, 512, PLE)[:, tiles].reshape(L, NTOK, PLE)
        pTc = np.ascontiguousarray(np.concatenate([pp, g["p_sample"][:, c]], axis=1).transpose(0, 2, 1), dtype=f32)
        par = np.zeros((128, NPAR), f32)
        for l in range(L):
            par[:, l * 48: l * 48 + 16] = g["g_attn_norm"][l].reshape(KC, 128).T
            par[:, l * 48 + 16: l * 48 + 32] = g["g_ple_norm"][l].reshape(KC, 128).T
            par[:, l * 48 + 32: l * 48 + 36] = g["g_out_sb"][l].reshape(4, 128).T
            par[:, l * 48 + 36: l * 48 + 40] = g["g_out_fox"][l].reshape(4, 128).T
            par[0:4, l * 48 + 40] = -g["b_forget"][l]
        par[:, L * 48: L * 48 + 16] = g["g_final"].reshape(KC, 128).T
        par[:, L * 48 + 16 + j] = 1.0
        m = {"xT": xTc, "pT": pTc, "w_in_r": w_in_r, "w_f_r": w_f_r, "w_out_r": w_out_r, "w_g_r": w_g_r,
             "w_ple_r": w_ple_r, "params": par, "cbf": cbf, "cf32": cf32, "masks": _masks(j), "smask": smask}
        if cfg.sample:
            ck = np.concatenate([g["cache_sb_k"][:, c], g["cache_fox_k"][:, c]], axis=2)
            m["ckT"] = np.ascontiguousarray(ck.transpose(0, 2, 3, 1), dtype=f32)
            cvv = np.concatenate([g["cache_sb_v"][:, c], g["cache_fox_v"][:, c]], axis=2)
            m["cv"] = np.ascontiguousarray(cvv.reshape(L, PAST, NH * HD), dtype=f32)
            m["clf"] = np.ascontiguousarray(g["cache_fox_logf"][:, c].reshape(L, 8, 128, 4).transpose(0, 1, 3, 2), dtype=f32)
        in_maps.append(m)

    res = run_bass_kernel_spmd(nc, in_maps, core_ids=list(range(8)), trace=getattr(cfg, "trace", False))
    R = res.results
    global LAST_R, LAST_RES
    LAST_R = R
    LAST_RES = res
    SEQ = cfg.SEQ
    y_p = np.zeros((2, SEQ, D), f32)
    y_s = np.zeros((8, DEC_T, D), f32)
    kp = np.zeros((L, 2, SEQ, NH, HD), f32)
    vp = np.zeros((L, 2, SEQ, NH, HD), f32)
    lp = np.zeros((L, 2, SEQ, 4), f32)
    ks = np.zeros((L, 8, DEC_T, NH, HD), f32)
    vsa = np.zeros((L, 8, DEC_T, NH, HD), f32)
    ls = np.zeros((L, 8, DEC_T, 4), f32)
    for c in range(8):
        b, j = c // 4, c % 4
        r = R[c]
        yt = r["yT"].T
        kt = r["kT_out"].transpose(0, 3, 1, 2)
        vt = r["v_out"].reshape(L, NTOT, NH, HD)
        lt = r["lf_out"].transpose(0, 2, 1)
        for i in range(NT):
            gt = 4 * i + j
            y_p[b, gt * 512:(gt + 1) * 512] = yt[i * 512:(i + 1) * 512]
            kp[:, b, gt * 512:(gt + 1) * 512] = kt[:, i * 512:(i + 1) * 512]
            vp[:, b, gt * 512:(gt + 1) * 512] = vt[:, i * 512:(i + 1) * 512]
            lp[:, b, gt * 512:(gt + 1) * 512] = lt[:, i * 512:(i + 1) * 512]
        y_s[c] = yt[NTOK:]
        ks[:, c] = kt[:, NTOK:]
        vsa[:, c] = vt[:, NTOK:]
        ls[:, c] = lt[:, NTOK:]
    return (y_p, y_s, kp[..., 0:4, :], kp[..., 4:8, :], vp[..., 0:4, :], vp[..., 4:8, :], lp,
            ks[..., 0:4, :], ks[..., 4:8, :], vsa[..., 0:4, :], vsa[..., 4:8, :], ls)


def kernel(**inputs):
    cfg = Cfg(NT=8, DEPTH=4, sample=True)
    outs = run(cfg, inputs)
    y_p, y_s, skp, fkp, svp, fvp, lp, sks, fks, svs, fvs, ls = outs
    c = np.ascontiguousarray
    return (c(y_p), c(y_s), c(skp), c(svp), c(fkp), c(fvp), c(lp), c(sks), c(svs), c(fks), c(fvs), c(ls))
```
